# Optimizing a Trainium2 kernel written in Bass

```python
import jax, jax.numpy as jnp
from jax import lax
import numpy as np

D_MODEL = 1024
BATCH = 1
SEQ = 16384
DEPTH = 1
DEC_BATCH = 32
DEC_SEQ = 2048
PAST_LEN = 128

HEAD_DIM = 64
ATTN_WIDTH = D_MODEL // 2
ATTN_HEADS = ATTN_WIDTH // HEAD_DIM
RWKV_WIDTH = D_MODEL - ATTN_WIDTH
RWKV_HEADS = RWKV_WIDTH // HEAD_DIM
DECAY_RANK = 64
ICLR_RANK = 64
GATE_RANK = 128
D_FF = 4 * D_MODEL
DILATION_PATTERNS = ((128, 1), (512, 4), (2048, 16))
ATTN_IN = 3 * ATTN_WIDTH
RWKV_IN = 3 * RWKV_WIDTH + 2 * DECAY_RANK + 2 * ICLR_RANK + GATE_RANK
IN_WIDTH = ATTN_IN + RWKV_IN
NORM_EPS = 1e-6
LN_X_EPS = 64e-5
NEG_INF = -1e30

kernel_name = 'hymba_longnet_rwkv7_adaln_encoder'


def rmsnorm(x, g):
    x32 = x.astype(jnp.float32)
    y = x32 * lax.rsqrt(jnp.mean(x32 * x32, axis=-1, keepdims=True) + NORM_EPS) * g
    return y.astype(x.dtype)


def alibi_slopes(n_heads):
    return 2.0 ** (-8.0 * (jnp.arange(n_heads, dtype=jnp.float32) + 1.0) / n_heads)


def dilated_band_attention(q, k, v, dil, half, slopes):
    B, S, H, Dh = q.shape
    L = S // dil
    nb = -(-L // half)
    Lp = nb * half
    N = B * dil

    def to_sub(z):
        return z.reshape(B, L, dil, H, Dh).transpose(0, 2, 1, 3, 4).reshape(N, L, H, Dh)

    def band(z):
        zp = jnp.pad(z, ((0, 0), (half, Lp - L + half), (0, 0), (0, 0))).reshape(N, nb + 2, half, H, Dh)
        return jnp.concatenate([zp[:, :-2], zp[:, 1:-1], zp[:, 2:]], axis=2)

    qb = jnp.pad(to_sub(q), ((0, 0), (0, Lp - L), (0, 0), (0, 0))).reshape(N, nb, half, H, Dh)
    kb = band(to_sub(k))
    vb = band(to_sub(v))

    qi = jnp.arange(half)[:, None]
    kj = jnp.arange(3 * half)[None, :]
    rel = kj - half - qi
    kpos = jnp.arange(nb)[:, None, None] * half + kj[None] - half
    valid = (jnp.abs(rel)[None] <= half) & (kpos >= 0) & (kpos < L)
    bias = -(slopes[:, None, None] * (dil * jnp.abs(rel)).astype(jnp.float32))

    s = jnp.einsum('nbqhd,nbkhd->nbhqk', qb, kb).astype(jnp.float32) + bias[None, None]
    s = jnp.where(valid[None, :, None], s, NEG_INF)
    m = jnp.max(s, axis=-1, keepdims=True)
    p = jnp.exp(s - m)
    l = jnp.sum(p, axis=-1)
    o = jnp.einsum('nbhqk,nbkhd->nbqhd', p.astype(vb.dtype), vb).astype(jnp.float32)
    o = o / jnp.swapaxes(l, 2, 3)[..., None]
    lse = jnp.swapaxes(m[..., 0] + jnp.log(l), 2, 3)

    def from_sub(z):
        tail = z.shape[3:]
        z = z.reshape(B, dil, Lp, *tail)[:, :, :L]
        return jnp.swapaxes(z, 1, 2).reshape(B, S, *tail)

    return from_sub(o), from_sub(lse)


def head_rmsnorm(z, g):
    z32 = z.astype(jnp.float32)
    return (z32 * lax.rsqrt(jnp.mean(z32 * z32, axis=-1, keepdims=True) + NORM_EPS) * g).astype(z.dtype)


def attention_mixer(p, q_norm_g, k_norm_g, attn_beta):
    B, S, _ = p.shape
    q, k, v = (z.reshape(B, S, ATTN_HEADS, HEAD_DIM) for z in jnp.split(p, 3, axis=-1))
    q = head_rmsnorm(q, q_norm_g) * (HEAD_DIM ** -0.5)
    k = head_rmsnorm(k, k_norm_g)
    slopes = alibi_slopes(ATTN_HEADS)
    outs, lses = [], []
    for window, dil in DILATION_PATTERNS:
        o, lse = dilated_band_attention(q, k, v, dil, window // (2 * dil), slopes)
        outs.append(o)
        lses.append(lse)
    wts = jax.nn.softmax(jnp.stack(lses), axis=0)
    o = jnp.sum(wts[..., None] * jnp.stack(outs), axis=0)
    return (o.reshape(B, S, ATTN_WIDTH) * attn_beta).astype(p.dtype)


def _rwkv_step(state, inp):
    w, kk, b, k, v, r = inp
    sa = jnp.einsum('dbhvk,dbhk->dbhv', state, kk)
    state = state * w[..., None, :] - sa[..., :, None] * b[..., None, :] + v[..., :, None] * k[..., None, :]
    y = jnp.einsum('dbhvk,dbhk->dbhv', state, r)
    return state, y


def rwkv7_mixer(p, mu_prev, mu_next, w0, w_up, a0, a_up, g_up, k_k, k_a, r_k, ln_x_w, ln_x_b):
    B, S, _ = p.shape
    f32 = jnp.float32
    prev = jnp.pad(p[:, :-1], ((0, 0), (1, 0), (0, 0)))
    nxt = jnp.pad(p[:, 1:], ((0, 0), (0, 1), (0, 0)))
    p = p + mu_prev * (prev - p) + mu_next * (nxt - p)
    cuts = tuple(int(c) for c in np.cumsum([RWKV_WIDTH] * 3 + [DECAY_RANK] * 2 + [ICLR_RANK] * 2))
    r, k, v, wd_f, wd_b, ad_f, ad_b, gd = jnp.split(p, cuts, axis=-1)
    wd = jnp.stack([wd_f, wd_b])
    ad = jnp.stack([ad_f, ad_b])
    w_raw = (w0[:, None, None] + jnp.einsum('dbsr,drc->dbsc', jnp.tanh(wd), w_up)).astype(f32)
    decay = jnp.exp(-jnp.exp(-jax.nn.softplus(-w_raw) - 0.5))
    a = jax.nn.sigmoid((a0[:, None, None] + jnp.einsum('dbsr,drc->dbsc', ad, a_up)).astype(f32))
    g = jax.nn.sigmoid(gd) @ g_up

    def heads(z):
        return z.reshape(*z.shape[:-1], RWKV_HEADS, HEAD_DIM)

    r32, k32, v32 = r.astype(f32), k.astype(f32), v.astype(f32)
    kk = heads(k32 * k_k)
    kk = kk * lax.rsqrt(jnp.maximum(jnp.sum(kk * kk, axis=-1, keepdims=True), 1e-24))
    kd = heads(k32[None] * (1.0 + (a - 1.0) * k_a))
    bd = kk[None] * heads(a)
    rh, vh = heads(r32), heads(v32)

    def both(z):
        return jnp.stack([z, jnp.flip(z, 1)])

    def dirs(z):
        return jnp.stack([z[0], jnp.flip(z[1], 1)])

    seqs = (dirs(heads(decay)), both(kk), dirs(bd), dirs(kd), both(vh), both(rh))
    seqs = tuple(jnp.moveaxis(z, 2, 0) for z in seqs)
    state0 = jnp.zeros((2, B, RWKV_HEADS, HEAD_DIM, HEAD_DIM), f32)
    _, ys = lax.scan(_rwkv_step, state0, seqs)
    ys = jnp.moveaxis(ys, 0, 2)
    y = ys[0] + jnp.flip(ys[1], 1)
    mu = jnp.mean(y, axis=-1, keepdims=True)
    var = jnp.mean(jnp.square(y - mu), axis=-1, keepdims=True)
    yn = ((y - mu) * lax.rsqrt(var + LN_X_EPS)).reshape(B, S, RWKV_WIDTH) * ln_x_w + ln_x_b
    bonus = jnp.sum(rh * (kd[0] + kd[1]) * r_k, axis=-1, keepdims=True) * vh
    out = (yn + bonus.reshape(B, S, RWKV_WIDTH)) * g
    return out.astype(p.dtype)


def encoder_layer(x, c, w_ada, b_ada, g_norm1, g_norm2, w_in, q_norm_g, k_norm_g, attn_beta,
                  mu_prev, mu_next, w0, w_up, a0, a_up, g_up, k_k, k_a, r_k, ln_x_w, ln_x_b,
                  w_out, w_ff1, w_ff2):
    mod = jax.nn.silu(c) @ w_ada + b_ada
    sh1, sc1, gt1, sh2, sc2, gt2 = (m[:, None, :] for m in jnp.split(mod, 6, axis=-1))
    h = rmsnorm(x, g_norm1) * (1.0 + sc1) + sh1
    p = h @ w_in
    attn = attention_mixer(p[..., :ATTN_IN], q_norm_g, k_norm_g, attn_beta)
    rw = rwkv7_mixer(p[..., ATTN_IN:], mu_prev, mu_next, w0, w_up, a0, a_up, g_up,
                     k_k, k_a, r_k, ln_x_w, ln_x_b)
    x = x + gt1 * (jnp.concatenate([attn, rw], axis=-1) @ w_out)
    h = rmsnorm(x, g_norm2) * (1.0 + sc2) + sh2
    x = x + gt2 * (jnp.square(jax.nn.relu(h @ w_ff1)) @ w_ff2)
    return x


def setup_inputs(seed: int = 0) -> dict:
    key = jax.random.key(seed)
    ks = jax.random.split(key, 32)
    f32 = jnp.float32
    L = DEPTH

    def nrm(k, shape, scale):
        return jax.random.normal(k, shape, f32) * scale

    return {
        'x_prompt': nrm(ks[0], (BATCH, SEQ, D_MODEL), 1.0),
        'x_sample': nrm(ks[1], (DEC_BATCH, DEC_SEQ, D_MODEL), 1.0),
        'c_prompt': nrm(ks[2], (BATCH, D_MODEL), 1.0),
        'c_sample': nrm(ks[3], (DEC_BATCH, D_MODEL), 1.0),
        'w_ada': nrm(ks[4], (L, D_MODEL, 6 * D_MODEL), 0.5 * D_MODEL ** -0.5),
        'b_ada': nrm(ks[5], (L, 6 * D_MODEL), 0.02),
        'g_norm1': 1.0 + nrm(ks[6], (L, D_MODEL), 0.02),
        'g_norm2': 1.0 + nrm(ks[7], (L, D_MODEL), 0.02),
        'w_in': nrm(ks[8], (L, D_MODEL, IN_WIDTH), D_MODEL ** -0.5),
        'q_norm_g': 1.0 + nrm(ks[9], (L, HEAD_DIM), 0.02),
        'k_norm_g': 1.0 + nrm(ks[10], (L, HEAD_DIM), 0.02),
        'attn_beta': 1.0 + nrm(ks[11], (L, ATTN_WIDTH), 0.02),
        'mu_prev': jax.random.uniform(ks[12], (L, RWKV_IN), f32, 0.0, 0.5),
        'mu_next': jax.random.uniform(ks[13], (L, RWKV_IN), f32, 0.0, 0.5),
        'w0': jax.random.uniform(ks[14], (L, 2, RWKV_WIDTH), f32, -5.0, 0.0),
        'w_up': nrm(ks[15], (L, 2, DECAY_RANK, RWKV_WIDTH), 0.1),
        'a0': nrm(ks[16], (L, 2, RWKV_WIDTH), 0.5),
        'a_up': nrm(ks[17], (L, 2, ICLR_RANK, RWKV_WIDTH), 0.1),
        'g_up': nrm(ks[18], (L, GATE_RANK, RWKV_WIDTH), GATE_RANK ** -0.5),
        'k_k': 0.85 + nrm(ks[19], (L, RWKV_WIDTH), 0.05),
        'k_a': 1.0 + nrm(ks[20], (L, RWKV_WIDTH), 0.05),
        'r_k': nrm(ks[21], (L, RWKV_HEADS, HEAD_DIM), 0.1),
        'ln_x_w': 1.0 + nrm(ks[22], (L, RWKV_WIDTH), 0.02),
        'ln_x_b': nrm(ks[23], (L, RWKV_WIDTH), 0.02),
        'w_out': nrm(ks[24], (L, D_MODEL, D_MODEL), D_MODEL ** -0.5),
        'w_ff1': nrm(ks[25], (L, D_MODEL, D_FF), D_MODEL ** -0.5),
        'w_ff2': nrm(ks[26], (L, D_FF, D_MODEL), 0.5 * D_FF ** -0.5),
    }


def reference(x_prompt, x_sample, c_prompt, c_sample, w_ada, b_ada, g_norm1, g_norm2, w_in,
              q_norm_g, k_norm_g, attn_beta, mu_prev, mu_next, w0, w_up, a0, a_up, g_up,
              k_k, k_a, r_k, ln_x_w, ln_x_b, w_out, w_ff1, w_ff2):
    y_prompt = x_prompt
    y_sample = x_sample
    for i in range(DEPTH):
        layer_params = (w_ada[i], b_ada[i], g_norm1[i], g_norm2[i], w_in[i], q_norm_g[i], k_norm_g[i],
                        attn_beta[i], mu_prev[i], mu_next[i], w0[i], w_up[i], a0[i], a_up[i], g_up[i],
                        k_k[i], k_a[i], r_k[i], ln_x_w[i], ln_x_b[i], w_out[i], w_ff1[i], w_ff2[i])
        y_prompt = encoder_layer(y_prompt, c_prompt, *layer_params)
        y_sample = encoder_layer(y_sample, c_sample, *layer_params)
    return (y_prompt, y_sample)
```

```python
import numpy as np
from contextlib import ExitStack
import concourse.bass as bass
import concourse.mybir as mybir
from concourse.bass_utils import run_bass_kernel_spmd

F32 = mybir.dt.float32
BF16 = mybir.dt.bfloat16
AF = mybir.ActivationFunctionType
ALU = mybir.AluOpType
AX = mybir.AxisListType

NCORES = 8
D = 1024
SEG = 2048
NSEG = 5
NTOK = NSEG * SEG
KS_TOT = 4096 + 4 * SEG
RS_TOT = 3072 + 4 * SEG
NP = 88


class T:
    __slots__ = ("name", "lw", "rd")

    def __init__(self, name):
        self.name = name
        self.lw = None
        self.rd = {}


class KB:
    ENGS = ("pe", "act", "dve", "pool", "sp")

    def __init__(self, nc):
        self.nc = nc
        self.q = {e: [] for e in self.ENGS}
        self.sems = {}
        self.cnt = {}
        self.seen = {e: {} for e in self.ENGS}
        self.pending = {e: False for e in self.ENGS}
        self._stack = []
        for e in self.ENGS:
            self.newsem("E_" + e)
        self.nchan = 0
        self.nops = 0

    def newsem(self, key):
        cm = self.nc.semaphore(key)
        s = cm.__enter__()
        self._stack.append(cm)
        self.sems[key] = s
        self.cnt[key] = 0
        return key

    def chan(self):
        self.nchan += 1
        return self.newsem("C%d" % self.nchan)

    def _waits(self, eng, reads, writes):
        need = {}
        seen = self.seen[eng]
        own = "E_" + eng

        def add(k, v):
            if k == own and eng == "pe":
                return
            if seen.get(k, 0) >= v:
                return
            if need.get(k, 0) < v:
                need[k] = v
        for t in reads:
            if t.lw is not None:
                add(*t.lw)
        for t in writes:
            if t.lw is not None:
                add(*t.lw)
            for k, v in t.rd.items():
                add(k, v)
        return need

    def _emit_waits(self, eng, need):
        for k, v in need.items():
            self.seen[eng][k] = v
            s = self.sems[k]
            self.q[eng].append(lambda e, s=s, v=v: e.wait_ge(s, v))

    def _mark(self, tok, reads, writes):
        for t in reads:
            if t.rd.get(tok[0], 0) < tok[1]:
                t.rd[tok[0]] = tok[1]
        for t in writes:
            t.lw = tok
            t.rd = {}

    def op(self, eng, fn, reads=(), writes=(), inc=True):
        self._emit_waits(eng, self._waits(eng, reads, writes))
        key = "E_" + eng
        self.nops += 1
        if inc:
            self.cnt[key] += 1
            tok = (key, self.cnt[key])
            s = self.sems[key]
            self.q[eng].append(lambda e, fn=fn, s=s: fn(e).then_inc(s, 1))
            self.pending[eng] = False
        else:
            tok = (key, self.cnt[key] + 1)
            self.q[eng].append(lambda e, fn=fn: fn(e))
            self.pending[eng] = True
        self._mark(tok, reads, writes)
        return tok

    def dma(self, eng, ch, out, in_, reads=(), writes=(), **kw):
        self._emit_waits(eng, self._waits(eng, reads, writes))
        self.cnt[ch] += 16
        tok = (ch, self.cnt[ch])
        s = self.sems[ch]
        self.nops += 1
        self.q[eng].append(lambda e, s=s, out=out, in_=in_, kw=kw:
                           e.dma_start(out=out, in_=in_, **kw).then_inc(s, 16))
        self._mark(tok, reads, writes)
        return tok

    def barrier(self):
        for e in self.ENGS:
            need = {}
            for k, v in self.cnt.items():
                if v > 0 and self.seen[e].get(k, 0) < v and not (k == "E_" + e):
                    need[k] = v
            self._emit_waits(e, need)

    def finish(self):
        for e in self.ENGS:
            if self.pending[e]:
                raise RuntimeError("engine %s has un-signalled trailing op" % e)
        need = {k: v for k, v in self.cnt.items() if v > 0 and k != "E_sp"}
        for k, v in need.items():
            s = self.sems[k]
            self.q["sp"].append(lambda e, s=s, v=v: e.wait_ge(s, v))
        with self.nc.Block() as block:
            def run(name):
                def f(e):
                    for c in self.q[name]:
                        c(e)
                return f
            block.tensor(run("pe"))
            block.scalar(run("act"))
            block.vector(run("dve"))
            block.gpsimd(run("pool"))
            block.sync(run("sp"))
        for cm in reversed(self._stack):
            cm.__exit__(None, None, None)
        self._stack = []


class Buf:
    kb = None

    def __init__(self, t, name):
        self.t = t
        self.d = T(name)
        self._ch = None
        self._sch = None

    @property
    def ch(self):
        if self._ch is None:
            self._ch = Buf.kb.chan()
        return self._ch

    @property
    def sch(self):
        if self._sch is None:
            self._sch = Buf.kb.chan()
        return self._sch

    def __getitem__(self, k):
        return self.t[k]


class Ring:
    def __init__(self, bufs):
        self.bufs = bufs
        self.i = 0

    def next(self):
        b = self.bufs[self.i % len(self.bufs)]
        self.i += 1
        return b


class Ctx:
    pass


def build_program(phases=("p0", "p1"), debug=False, p2a_args={}, p2b_args={}):
    nc = bass.Bass("TRN2", target_bir_lowering=False)
    g = Ctx()
    g.nc = nc
    g.debug = debug
    g.p2a_args = p2a_args
    g.p2b_args = p2b_args
    kb = g.kb = KB(nc)
    Buf.kb = kb

    def din(name, shape, dt=F32):
        return nc.dram_tensor(name, list(shape), dt, kind="ExternalInput").ap()

    def dscratch(name, shape, dt):
        return nc.dram_tensor(name, list(shape), dt, kind=("ExternalOutput" if debug else "Internal")).ap()

    g.xa = din("xa", [NTOK, D])
    g.xh = din("xh", [2048, D])
    g.c5T = din("c5T", [128, 8, NSEG])
    g.w_ada = din("w_ada", [D, 6 * D])
    g.b_ada = din("b_ada", [1, 6 * D])
    g.pp_d = din("pp", [128, NP])
    g.w_in = din("w_in", [D, 3456])
    g.w_out = din("w_out", [D, D])
    g.w_ff1 = din("w_ff1", [D, 4 * D])
    g.w_ff2 = din("w_ff2", [4 * D, D])
    g.w_up = din("w_up", [128, 512])
    g.a_up = din("a_up", [128, 512])
    g.g_up = din("g_up", [128, 512])
    g.vt_d = din("vt", [128, NSEG * 48 * 2])
    g.nbv_d = din("nbv", [128, 2])
    g.xslot = din("xslot", [7, 2176, D])
    g.sf_d = din("sf", [128, 7 * 6])
    g.y = nc.dram_tensor("y", [NTOK, D], F32, kind="ExternalOutput").ap()

    g.qT_s = dscratch("qT_s", [4, 128, NTOK], BF16)
    g.kT_s = dscratch("kT_s", [4, 128, KS_TOT], BF16)
    g.V_s = dscratch("V_s", [KS_TOT, 512], BF16)
    g.pr_s = dscratch("pr_s", [15, 128, RS_TOT], F32)
    g.mix_s = dscratch("mix_s", [8, 128, NTOK], BF16)
    g.prs_s = dscratch("prs_s", [10, 128, 7 * 2176], F32)
    if debug:
        g.dbg_mod = nc.dram_tensor("dbg_mod", [NSEG, 6 * D], F32, kind="ExternalOutput").ap()

    with ExitStack() as es:
        g.es = es

        def sb(name, shape, dt, scope=es):
            return Buf(scope.enter_context(nc.sbuf_tensor(name, list(shape), dt)), name)
        g.sb = sb
        g.banks = [Buf(es.enter_context(nc.psum_tensor("bank%d" % i, [128, 512], F32)), "bank%d" % i)
                   for i in range(8)]
        phase0(g)
        if "p1" in phases:
            kb.barrier()
            phase1(g)
        if "p2a" in phases:
            phase2a(g, **g.p2a_args)
        if "p2b" in phases:
            phase2b(g, **g.p2b_args)
        if "p3" in phases:
            phase3(g)
        if "p4" in phases:
            phase4(g)
        kb.barrier()
        kb.finish()
    return nc


def phase0(g):
    nc, kb, sb = g.nc, g.kb, g.sb
    c = g.c = Ctx()
    c.identf = sb("identf", [128, 128], F32)
    c.identb = sb("identb", [128, 128], BF16)
    c.pp = sb("pp_sb", [128, NP], F32)
    c.sel = sb("sel", [NSEG, NSEG, 128], F32)
    c.ones1 = sb("ones1", [1, 128], F32)
    c.gates = sb("gates", [NSEG, 2 * D], F32)
    c.modT = sb("modT", [128, 48, NSEG], F32)
    c.scale1 = sb("scale1", [128, 8, NSEG], F32)
    c.scale2 = sb("scale2", [128, 8, NSEG], F32)
    c.bdones = sb("bdones", [128, 128], BF16)
    c.bdonesf = sb("bdonesf", [128, 128], F32)
    c.gq8 = sb("gq8", [128, 1], F32)

    kb.dma("sp", c.pp.ch, c.pp[:], g.pp_d, writes=[c.pp.d])
    kb.op("pool", lambda e: e.memset(c.identf[:], 1.0), writes=[c.identf.d])
    kb.op("pool", lambda e: e.affine_select(out=c.identf[:], in_=c.identf[:], pattern=[[-1, 128]],
                                            compare_op=ALU.is_equal, fill=0.0, base=0, channel_multiplier=1),
          reads=[c.identf.d], writes=[c.identf.d])
    kb.op("dve", lambda e: e.tensor_copy(out=c.identb[:], in_=c.identf[:]), reads=[c.identf.d], writes=[c.identb.d])
    kb.op("pool", lambda e: e.memset(c.bdonesf[:], 1.0 / 64), writes=[c.bdonesf.d])
    kb.op("pool", lambda e: e.memset(c.bdonesf[0:64, 64:128], 0.0), reads=[c.bdonesf.d], writes=[c.bdonesf.d])
    kb.op("pool", lambda e: e.memset(c.bdonesf[64:128, 0:64], 0.0), reads=[c.bdonesf.d], writes=[c.bdonesf.d])
    kb.op("dve", lambda e: e.tensor_copy(out=c.bdones[:], in_=c.bdonesf[:]), reads=[c.bdonesf.d], writes=[c.bdones.d])
    kb.op("pool", lambda e: e.memset(c.sel[:], 1.0), writes=[c.sel.d])
    kb.op("pool", lambda e: e.affine_select(out=c.sel[:], in_=c.sel[:], pattern=[[-1, NSEG], [0, 128]],
                                            compare_op=ALU.is_equal, fill=0.0, base=0, channel_multiplier=1),
          reads=[c.sel.d], writes=[c.sel.d])
    kb.op("pool", lambda e: e.memset(c.ones1[:], 1.0), writes=[c.ones1.d])
    kb.op("act", lambda e: e.mul(out=c.gq8[:], in_=c.pp[:, 16:17], mul=0.125), reads=[c.pp.d], writes=[c.gq8.d])

    with ExitStack() as ls:
        c.mod5 = sb("mod5", [NSEG, 6 * D], F32, ls)
        silu = sb("siluT", [128, 8, NSEG], F32, ls)
        brow = sb("brow", [1, 6 * D], F32, ls)
        wring = Ring([sb("wada%d" % i, [128, 8, 512], F32, ls) for i in range(2)])
        kb.dma("sp", silu.ch, silu[:], g.c5T, writes=[silu.d])
        kb.dma("sp", brow.ch, brow[:], g.b_ada, writes=[brow.d])
        kb.op("act", lambda e: e.activation(out=silu[:], in_=silu[:], func=AF.Silu), reads=[silu.d], writes=[silu.d])
        wv = g.w_ada.rearrange("(kc p) n -> p kc n", p=128)
        for cb in range(12):
            w = wring.next()
            kb.dma("sp", w.ch, w[:], wv[:, :, cb * 512:(cb + 1) * 512], writes=[w.d])
            bank = g.banks[cb % 2]
            for kc in range(8):
                kb.op("pe", lambda e, w=w, kc=kc, bank=bank: e.matmul(
                    bank[0:NSEG, :], lhsT=silu[:, kc, :], rhs=w[:, kc, :], start=(kc == 0), stop=False),
                    reads=[silu.d, w.d], writes=[bank.d], inc=False)
            kb.op("pe", lambda e, cb=cb, bank=bank: e.matmul(
                bank[0:NSEG, :], lhsT=c.ones1[0:1, 0:NSEG], rhs=brow[0:1, cb * 512:(cb + 1) * 512],
                start=False, stop=True), reads=[c.ones1.d, brow.d], writes=[bank.d])
            kb.op("act", lambda e, cb=cb, bank=bank: e.copy(out=c.mod5[:, cb * 512:(cb + 1) * 512], in_=bank[0:NSEG, :]),
                  reads=[bank.d], writes=[c.mod5.d])
        if g.debug:
            kb.dma("sp", c.mod5.sch, g.dbg_mod, c.mod5[:], reads=[c.mod5.d])
        kb.op("act", lambda e: e.copy(out=c.gates[:, 0:D], in_=c.mod5[:, 2 * D:3 * D]), reads=[c.mod5.d], writes=[c.gates.d])
        kb.op("act", lambda e: e.copy(out=c.gates[:, D:2 * D], in_=c.mod5[:, 5 * D:6 * D]), reads=[c.mod5.d], writes=[c.gates.d])
        bank = g.banks[2]
        for cc in range(48):
            kb.op("pe", lambda e, cc=cc: e.transpose(out=bank[:, cc * NSEG:(cc + 1) * NSEG],
                                                     in_=c.mod5[:, cc * 128:(cc + 1) * 128],
                                                     identity=c.identf[0:NSEG, 0:NSEG]),
                  reads=[c.mod5.d, c.identf.d], writes=[bank.d], inc=(cc == 47))
        kb.op("dve", lambda e: e.tensor_copy(out=c.modT[:].rearrange("p a s -> p (a s)"), in_=bank[:, 0:48 * NSEG]),
              reads=[bank.d], writes=[c.modT.d])
        for (dst, goff, coff) in ((c.scale1, 0, 8), (c.scale2, 8, 32)):
            for kc in range(8):
                kb.op("dve", lambda e, dst=dst, goff=goff, coff=coff, kc=kc: e.tensor_scalar(
                    out=dst[:, kc, :], in0=c.modT[:, coff + kc, :], scalar1=1.0, scalar2=c.pp[:, goff + kc:goff + kc + 1],
                    op0=ALU.add, op1=ALU.mult), reads=[c.modT.d, c.pp.d], writes=[dst.d])
        kb.barrier()


def p1_blocks():
    bl = []
    bl.append(dict(src="xh", row=0, seg=0, ks=0, rs=None, qs=None))
    bl.append(dict(src="xh", row=512, seg=0, ks=512, rs=0, qs=None))
    for b in range(4):
        bl.append(dict(src="xa", row=512 * b, seg=0, ks=1024 + 512 * b, rs=512 + 512 * b, qs=512 * b))
    bl.append(dict(src="xh", row=1024, seg=0, ks=3072, rs=2560, qs=None))
    bl.append(dict(src="xh", row=1536, seg=0, ks=3584, rs=None, qs=None))
    for s in range(1, NSEG):
        for b in range(4):
            bl.append(dict(src="xa", row=SEG * s + 512 * b, seg=s, ks=4096 + (s - 1) * SEG + 512 * b,
                           rs=3072 + (s - 1) * SEG + 512 * b, qs=SEG * s + 512 * b))
    for j in range(7):
        for b in range(4):
            bl.append(dict(src="slot", slot=j, row=512 * b, seg=0, ks=None, rs=None, qs=None))
        bl.append(dict(src="slot", slot=j, row=2048, seg=0, ks=None, rs=None, qs=None, nt=1))
    return bl


def rms_to_hT(g, ls_bufs, src_ap, row0, scale, bias_col, seg, hT, nt=4):
    kb, c = g.kb, g.c
    xring, xsb, stat, tbanks, xch = ls_bufs
    xts = []
    for t in range(nt):
        xt = xring.next()
        xts.append(xt)
        kb.dma("sp", xt.ch, xt[:], src_ap[row0 + t * 128: row0 + (t + 1) * 128, :], writes=[xt.d])
        st = stat.next()
        junk = xsb[t]
        kb.op("act", lambda e, xt=xt, st=st, junk=junk: e.activation(out=junk[:], in_=xt[:], func=AF.Square,
                                                                      accum_out=st[:, 0:1]),
              reads=[xt.d], writes=[junk.d, st.d])
        kb.op("dve", lambda e, st=st: e.tensor_scalar(out=st[:, 1:2], in0=st[:, 0:1], scalar1=1.0 / D, scalar2=1e-6,
                                                      op0=ALU.mult, op1=ALU.add), reads=[st.d], writes=[st.d])
        kb.op("act", lambda e, st=st: e.activation(out=st[:, 1:2], in_=st[:, 1:2], func=AF.Sqrt), reads=[st.d], writes=[st.d])
        kb.op("dve", lambda e, st=st: e.reciprocal(out=st[:, 2:3], in_=st[:, 1:2]), reads=[st.d], writes=[st.d])
        kb.op("act", lambda e, xt=xt, st=st, junk=junk: e.activation(out=junk[:], in_=xt[:], func=AF.Copy, scale=st[:, 2:3]),
              reads=[xt.d, st.d], writes=[junk.d])
    for kp in range(4):
        bank = tbanks.next()
        bv = bank.t[:].bitcast(BF16)
        for j in range(2):
            kc = kp * 2 + j
            for t in range(nt):
                kb.op("pe", lambda e, bv=bv, j=j, t=t, kc=kc: e.transpose(
                    out=bv[:, j * 512 + t * 128: j * 512 + (t + 1) * 128], in_=xsb[t][:, kc * 128:(kc + 1) * 128],
                    identity=c.identb[:]), reads=[xsb[t].d, c.identb.d], writes=[bank.d],
                    inc=(j == 1 and t == nt - 1))
        for j in range(2):
            kc = kp * 2 + j
            eng = "act" if j == 0 else "dve"
            if eng == "act":
                kb.op("act", lambda e, bv=bv, j=j, kc=kc: e.activation(
                    out=hT[:, kc, 0:nt * 128], in_=bv[:, j * 512: j * 512 + nt * 128], func=AF.Identity,
                    scale=scale[:, kc, seg:seg + 1], bias=c.modT[:, bias_col + kc, seg:seg + 1]),
                    reads=[bank.d, scale.d, c.modT.d], writes=[hT.d])
            else:
                kb.op("dve", lambda e, bv=bv, j=j, kc=kc: e.tensor_scalar(
                    out=hT[:, kc, 0:nt * 128], in0=bv[:, j * 512: j * 512 + nt * 128],
                    scalar1=scale[:, kc, seg:seg + 1], scalar2=c.modT[:, bias_col + kc, seg:seg + 1],
                    op0=ALU.mult, op1=ALU.add), reads=[bank.d, scale.d, c.modT.d], writes=[hT.d])
    return xts


def phase1(g):
    nc, kb, sb, c = g.nc, g.kb, g.sb, g.c
    with ExitStack() as ls:
        win = sb("win", [128, 8, 3456], BF16, ls)
        wv = g.w_in.rearrange("(kc p) n -> p kc n", p=128)
        for kc in range(8):
            kb.dma("pool", win.ch, win[:, kc, :], wv[:, kc, :], writes=[win.d])
        xring = Ring([sb("xt%d" % i, [128, D], F32, ls) for i in range(3)])
        xch = [kb.chan() for _ in range(3)]
        xsbs = [[sb("xs%d_%d" % (r, t), [128, D], BF16, ls) for t in range(4)] for r in range(2)]
        stat = Ring([sb("st%d" % i, [128, 4], F32, ls) for i in range(4)])
        hTs = Ring([sb("hT%d" % i, [128, 8, 512], BF16, ls) for i in range(2)])
        tbanks = Ring([g.banks[0], g.banks[1]])
        pbanks = Ring([g.banks[2], g.banks[3], g.banks[4]])
        mbanks = Ring([g.banks[5], g.banks[6]])
        sq = Ring([sb("sq%d" % i, [128, 512], BF16, ls) for i in range(2)])
        rsd = Ring([sb("rsd%d" % i, [128, 512], F32, ls) for i in range(2)])
        obf = Ring([sb("obf%d" % i, [128, 512], BF16, ls) for i in range(3)])
        of32 = Ring([sb("of%d" % i, [128, 512], F32, ls) for i in range(3)])

        def store(dst, src_buf, src_ap):
            kb.dma("sp", src_buf.sch, dst, src_ap, reads=[src_buf.d])

        blocks = p1_blocks()
        hts = {}

        def rms_blk(bi):
            blk = blocks[bi]
            hT = hTs.next()
            nt = blk.get("nt", 4)
            if blk["src"] == "slot":
                src = g.xslot[blk["slot"]]
            else:
                src = g.xa if blk["src"] == "xa" else g.xh
            rms_to_hT(g, (xring, xsbs[bi % 2], stat, tbanks, xch), src, blk["row"], c.scale1, 0, blk["seg"], hT, nt=nt)
            hts[bi] = hT
        rms_blk(0)
        for bi, blk in enumerate(blocks):
            if bi + 1 < len(blocks):
                rms_blk(bi + 1)
            seg = blk["seg"]
            hT = hts.pop(bi)
            nt = blk.get("nt", 4)
            ncol = nt * 128
            chunks = []
            if blk["src"] == "slot":
                chunks += [("s", j, 12 + 4 + j) for j in range(8)] + [("s", 8, 24), ("s", 9, 25)]
            else:
                if blk["qs"] is not None:
                    chunks += [("q", hp, hp) for hp in range(4)]
                chunks += [("k", hp, 4 + hp) for hp in range(4)]
                if blk["rs"] is not None:
                    chunks += [("r", j, 12 + j) for j in range(15)]
            for (kind, idx, cc) in chunks:
                bank = pbanks.next()
                for kc in range(8):
                    kb.op("pe", lambda e, bank=bank, kc=kc, cc=cc, hT=hT, ncol=ncol: e.matmul(
                        bank[:, 0:ncol], lhsT=win[:, kc, cc * 128:(cc + 1) * 128], rhs=hT[:, kc, 0:ncol],
                        start=(kc == 0), stop=(kc == 7)), reads=[win.d, hT.d], writes=[bank.d], inc=(kc == 7))
                if kind in ("q", "k"):
                    s2, ms, rs_, ob = sq.next(), mbanks.next(), rsd.next(), obf.next()
                    kb.op("act", lambda e, s2=s2, bank=bank: e.activation(out=s2[:], in_=bank[:, :], func=AF.Square),
                          reads=[bank.d], writes=[s2.d])
                    kb.op("pe", lambda e, ms=ms, s2=s2: e.matmul(ms[:, :], lhsT=c.bdones[:], rhs=s2[:], start=True, stop=True),
                          reads=[c.bdones.d, s2.d], writes=[ms.d])
                    kb.op("dve", lambda e, rs_=rs_, ms=ms: e.tensor_scalar_add(out=rs_[:], in0=ms[:, :], scalar1=1e-6),
                          reads=[ms.d], writes=[rs_.d])
                    kb.op("act", lambda e, rs_=rs_: e.activation(out=rs_[:], in_=rs_[:], func=AF.Sqrt),
                          reads=[rs_.d], writes=[rs_.d])
                    kb.op("dve", lambda e, rs_=rs_: e.reciprocal(out=rs_[:], in_=rs_[:]), reads=[rs_.d], writes=[rs_.d])
                    gcol = c.gq8[:, 0:1] if kind == "q" else c.pp[:, 17:18]
                    gd = c.gq8.d if kind == "q" else c.pp.d
                    kb.op("dve", lambda e, ob=ob, bank=bank, rs_=rs_, gcol=gcol: e.scalar_tensor_tensor(
                        out=ob[:], in0=bank[:, :], scalar=gcol, in1=rs_[:], op0=ALU.mult, op1=ALU.mult),
                        reads=[bank.d, rs_.d, gd], writes=[ob.d])
                    if kind == "q":
                        store(g.qT_s[idx, :, blk["qs"]:blk["qs"] + 512], ob, ob[:])
                    else:
                        store(g.kT_s[idx, :, blk["ks"]:blk["ks"] + 512], ob, ob[:])
                elif kind == "s":
                    o = of32.next()
                    kb.op("act", lambda e, o=o, bank=bank, ncol=ncol: e.copy(out=o[:, 0:ncol], in_=bank[:, 0:ncol]), reads=[bank.d], writes=[o.d])
                    p0 = blk["slot"] * 2176 + blk["row"]
                    store(g.prs_s[idx, :, p0:p0 + ncol], o, o[:, 0:ncol])
                else:
                    o = of32.next()
                    kb.op("act", lambda e, o=o, bank=bank: e.copy(out=o[:], in_=bank[:, :]), reads=[bank.d], writes=[o.d])
                    store(g.pr_s[idx, :, blk["rs"]:blk["rs"] + 512], o, o[:])
            for t in range(4 if blk["src"] != "slot" else 0):
                bank = pbanks.next()
                ob = obf.next()
                for kc in range(8):
                    kb.op("pe", lambda e, bank=bank, kc=kc, hT=hT, t=t: e.matmul(
                        bank[:, :], lhsT=hT[:, kc, t * 128:(t + 1) * 128], rhs=win[:, kc, 1024:1536],
                        start=(kc == 0), stop=(kc == 7)), reads=[win.d, hT.d], writes=[bank.d], inc=(kc == 7))
                kb.op("dve", lambda e, ob=ob, bank=bank: e.tensor_copy(out=ob[:], in_=bank[:, :]), reads=[bank.d], writes=[ob.d])
                store(g.V_s[blk["ks"] + t * 128: blk["ks"] + (t + 1) * 128, :], ob, ob[:])
        kb.barrier()


def attn_geom(seg):
    if seg == 0:
        return 4096, 1024, 0
    return 2048, 0, 4096 + (seg - 1) * SEG


def phase2a(g, segs=range(NSEG), hps=range(4), level=9):
    nc, kb, sb, c = g.nc, g.kb, g.sb, g.c
    with ExitStack() as ls:
        onesb = sb("onesb", [128, 64], BF16, ls)
        vt = sb("vt_sb", [128, NSEG * 48 * 2], F32, ls)
        E = sb("Emask", [128, 12, 512], BF16, ls)
        ones128 = sb("ones128", [128, 128], BF16, ls)
        kb.op("pool", lambda e: e.memset(ones128[:], 1.0), writes=[ones128.d])
        R = sb("Rrel", [128, 2, 128], F32, ls)
        M = sb("Mband", [128, 2, 128], F32, ls)
        tmpE = Ring([sb("tmpE%d" % i, [128, 2, 128], F32, ls) for i in range(2)])
        kb.op("pool", lambda e: e.memset(onesb[:], 1.0), writes=[onesb.d])
        kb.dma("sp", vt.ch, vt[:], g.vt_d, writes=[vt.d])
        kb.op("pool", lambda e: e.iota(R[:, 0, :], pattern=[[-1, 128]], base=-64, channel_multiplier=1,
                                       allow_small_or_imprecise_dtypes=True), writes=[R.d])
        kb.op("pool", lambda e: e.iota(R[:, 1, :], pattern=[[-1, 128]], base=64, channel_multiplier=1,
                                       allow_small_or_imprecise_dtypes=True), reads=[R.d], writes=[R.d])
        kb.op("act", lambda e: e.activation(out=R[:], in_=R[:], func=AF.Abs),
              reads=[R.d], writes=[R.d])
        kb.op("dve", lambda e: e.tensor_single_scalar(out=M[:], in_=R[:], scalar=64.0, op=ALU.is_le),
              reads=[R.d], writes=[M.d])
        for h in range(8):
            slope = 2.0 ** (-(h + 1.0))
            for di, dil in enumerate((1, 4, 16)):
                tm = tmpE.next()
                kb.op("act", lambda e, tm=tm, sc=-slope * dil: e.activation(out=tm[:], in_=R[:], func=AF.Exp, scale=sc),
                      reads=[R.d], writes=[tm.d])
                ev = E[:, (h // 2) * 3 + di, :].rearrange("p (x s n) -> p x s n", x=2, s=2)[:, h % 2, :, :]
                kb.op("dve", lambda e, tm=tm, ev=ev: e.tensor_mul(out=ev, in0=tm[:], in1=M[:]),
                      reads=[tm.d, M.d], writes=[E.d])

        qAs = Ring([sb("qA%d" % i, [128, SEG], BF16, ls) for i in range(2)])
        qBs = Ring([sb("qB%d" % i, [128, SEG], BF16, ls) for i in range(2)])
        for qb_ in qAs.bufs:
            kb.op("pool", lambda e, qb_=qb_: e.memset(qb_[64:128, :], 0.0), writes=[qb_.d])
        for qb_ in qBs.bufs:
            kb.op("pool", lambda e, qb_=qb_: e.memset(qb_[0:64, :], 0.0), writes=[qb_.d])
        kTs = Ring([sb("kT%d" % i, [128, 6144], BF16, ls) for i in range(2)])
        for k_ in kTs.bufs:
            kb.op("pool", lambda e, k_=k_: e.memset(k_[:], 0.0), writes=[k_.d])
        Vds = [Ring([sb("Vd%d_%d" % (di, i), [128, 48, 128], BF16, ls) for i in range(2)]) for di in range(3)]
        for di in range(3):
            for v_ in Vds[di].bufs:
                kb.op("pool", lambda e, v_=v_: e.memset(v_[:], 0.0), writes=[v_.d])
        acc = sb("acc", [128, 2, SEG], F32, ls)
        exs = Ring([sb("ex%d" % i, [128, 512], F32, ls) for i in range(4)])
        pTs = Ring([sb("pT%d" % i, [128, 512], BF16, ls) for i in range(4)])
        osbs = Ring([sb("osb%d" % i, [128, SEG], BF16, ls) for i in range(2)])
        sbanks = Ring([g.banks[0], g.banks[1], g.banks[2]])
        obanks = Ring([g.banks[3], g.banks[4], g.banks[5]])

        def vgeom(seg, dil):
            NK, HK, ksb = attn_geom(seg)
            first = 0 if (HK // dil) % 128 == 64 else -64
            Lk = NK // dil
            nsets = -(-(Lk - first) // 128)
            return first, Lk, nsets

        def loads(seg, hp):
            NK, HK, ksb = attn_geom(seg)
            qA, qB, kT = qAs.next(), qBs.next(), kTs.next()
            kb.dma("sp", qA.ch, qA[0:64, :], g.qT_s[hp, 0:64, seg * SEG:(seg + 1) * SEG], writes=[qA.d])
            kb.dma("sp", qB.ch, qB[64:128, :], g.qT_s[hp, 64:128, seg * SEG:(seg + 1) * SEG], writes=[qB.d])
            kb.dma("sp", kT.ch, kT[:, 1024:1024 + NK], g.kT_s[hp, :, ksb:ksb + NK], writes=[kT.d])
            Vd = []
            for di, dil in enumerate((1, 4, 16)):
                V = Vds[di].next()
                Vd.append(V)
                first, Lk, nsets = vgeom(seg, dil)
                vsrc = g.V_s[ksb:ksb + NK, hp * 128:(hp + 1) * 128].rearrange("(s d) c -> d s c", d=dil)
                for r in range(dil):
                    m = 0
                    while m < nsets:
                        sub0 = first + 128 * m
                        if sub0 < 0:
                            kb.dma("sp", V.ch, V[64:128, r * nsets + m, :], vsrc[r, 0:64, :], writes=[V.d])
                            m += 1
                            continue
                        if sub0 + 128 > Lk:
                            kb.dma("sp", V.ch, V[0:64, r * nsets + m, :], vsrc[r, sub0:sub0 + 64, :], writes=[V.d])
                            m += 1
                            continue
                        cnt = min(8, (Lk - sub0) // 128)
                        kb.dma("sp", V.ch, V[:, r * nsets + m: r * nsets + m + cnt, :],
                               vsrc[r, sub0:sub0 + 128 * cnt, :].rearrange("(m p) c -> p m c", p=128), writes=[V.d])
                        m += cnt
            return qA, qB, kT, Vd

        def tile_gen(seg, hp, di, dil, r, j, ti, first, nsets, qvs, qds, kv, accv, V, Ev, kT, HK):
            sub0 = HK // dil + 128 * j - 64
            jb = (sub0 - first) // 128
            sbank = sbanks.next()
            Sv = sbank[:, :].rearrange("p (x s n) -> p x s n", x=2, s=2)
            for X in range(2):
                qx = qvs[X][:, 128 * j:128 * j + 128, r]
                for h2 in range(2):
                    kh = kv[:, sub0 + 64 + 128 * h2: sub0 + 64 + 128 * h2 + 128, r]
                    kb.op("pe", lambda e, Sv=Sv, X=X, kh=kh, qx=qx, h2=h2: e.matmul(
                        Sv[:, X, h2, :], lhsT=kh, rhs=qx, start=True, stop=True),
                        reads=[kT.d, qds[X]], writes=[sbank.d], inc=(X == 1 and h2 == 1))
            ex = exs.next()
            kb.op("act", lambda e, ex=ex, sbank=sbank: e.activation(out=ex[:], in_=sbank[:, :], func=AF.Exp),
                  reads=[sbank.d], writes=[ex.d])
            yield
            pT = pTs.next()
            exv = ex[:].rearrange("p (x s n) -> p x s n", x=2, s=2)
            pTv = pT[:].rearrange("p (x s n) -> p x s n", x=2, s=2)
            for slot in range(2):
                col = (seg * 48 + ti) * 2 + slot
                kb.op("dve", lambda e, pTv=pTv, exv=exv, slot=slot, col=col, Ev=Ev: e.scalar_tensor_tensor(
                    out=pTv[:, :, slot, :], in0=exv[:, :, slot, :], scalar=vt[:, col:col + 1],
                    in1=Ev[:, :, slot, :], op0=ALU.mult, op1=ALU.mult),
                    reads=[ex.d, vt.d, E.d], writes=[pT.d])
            obank = obanks.next()
            Ov = obank[:, :].rearrange("p (x a n) -> p x a n", x=2, a=2)
            for X in range(2):
                for half in range(2):
                    for h2 in range(2):
                        if half == 0:
                            lhsT = V[:, r * nsets + jb + h2, :]
                            rd = [V.d, pT.d]
                        else:
                            lhsT = ones128[:]
                            rd = [ones128.d, pT.d]
                        kb.op("pe", lambda e, Ov=Ov, X=X, half=half, lhsT=lhsT, pTv=pTv, h2=h2: e.matmul(
                            Ov[:, X, half, :], lhsT=lhsT, rhs=pTv[:, X, h2, :], start=(h2 == 0), stop=(h2 == 1)),
                            reads=rd, writes=[obank.d], inc=(X == 1 and half == 1 and h2 == 1))
            yield
            for X in range(2):
                dst = accv[64 * X:64 * X + 64, :, 128 * j:128 * j + 128, r]
                src = Ov[64 * X:64 * X + 64, X, :, :]
                if di == 0:
                    kb.op("act", lambda e, dst=dst, src=src: e.copy(out=dst, in_=src), reads=[obank.d], writes=[acc.d])
                else:
                    kb.op("dve", lambda e, dst=dst, src=src: e.tensor_tensor(out=dst, in0=dst, in1=src, op=ALU.add),
                          reads=[obank.d, acc.d], writes=[acc.d])

        pend = []
        work = [(s, h) for s in segs for h in hps]
        if level < 1:
            work = []
        else:
            nxt = loads(*work[0])
        for wi, (seg, hp) in enumerate(work):
            NK, HK, ksb = attn_geom(seg)
            qA, qB, kT, Vd = nxt
            if wi + 1 < len(work):
                nxt = loads(*work[wi + 1])
            ti = 0
            for di, dil in enumerate((1, 4, 16) if level >= 2 else ()):
                Lq = SEG // dil
                first, Lk, nsets = vgeom(seg, dil)
                qvs = [q_[:].rearrange("p (m d) -> p m d", d=dil) for q_ in (qA, qB)]
                qds = [qA.d, qB.d]
                kv = kT[:, 1024 - 64 * dil:1024 + NK + 64 * dil].rearrange("p (m d) -> p m d", d=dil)
                accv = acc[:].rearrange("p a (m d) -> p a m d", d=dil)
                V = Vd[di]
                Ev = E[:, hp * 3 + di, :].rearrange("p (x s n) -> p x s n", x=2, s=2)
                for r in range(dil):
                    for j in range(Lq // 128):
                        g_ = tile_gen(seg, hp, di, dil, r, j, ti, first, nsets, qvs, qds, kv, accv, V, Ev, kT, HK)
                        next(g_)
                        pend.append(g_)
                        if len(pend) >= 2:
                            next(pend[-2])
                        if len(pend) >= 3:
                            for _ in pend.pop(0):
                                pass
                        ti += 1
            while pend:
                for _ in pend.pop(0):
                    pass
            if level < 6:
                continue
            osb = osbs.next()
            kb.op("dve", lambda e: e.reciprocal(out=acc[:, 1, :], in_=acc[:, 1, :]), reads=[acc.d], writes=[acc.d])
            kb.op("dve", lambda e, osb=osb, hp=hp: e.scalar_tensor_tensor(
                out=osb[:], in0=acc[:, 0, :], scalar=c.pp[:, 18 + hp:19 + hp], in1=acc[:, 1, :], op0=ALU.mult, op1=ALU.mult),
                reads=[acc.d, c.pp.d], writes=[osb.d])
            kb.dma("sp", osb.sch, g.mix_s[hp, :, seg * SEG:(seg + 1) * SEG], osb[:], reads=[osb.d])
        kb.barrier()


KAPPA = 0.6065306597126334


def rs_base(seg):
    return 512 if seg == 0 else 3072 + (seg - 1) * SEG


def bcast_last(ap2, n):
    return bass.AP(ap2.tensor, ap2.offset, [list(ap2.ap[0]), list(ap2.ap[1]), [0, n]])


def phase2b(g, segs=range(NSEG), hps=range(4), dirs=(0, 1), do_slots=True):
    nc, kb, sb, c = g.nc, g.kb, g.sb, g.c
    NCH = SEG // 64
    with ExitStack() as ls:
        ptab = sb("ptab", [128, 20], F32, ls)
        kb.op("dve", lambda e: e.tensor_tensor(out=ptab[:, 0:15], in0=c.pp[:, 22:37], in1=c.pp[:, 37:52], op=ALU.add),
              reads=[c.pp.d], writes=[ptab.d])
        kb.op("dve", lambda e: e.tensor_scalar(out=ptab[:, 0:15], in0=ptab[:, 0:15], scalar1=-1.0, scalar2=1.0,
                                               op0=ALU.mult, op1=ALU.add), reads=[ptab.d], writes=[ptab.d])
        kb.op("dve", lambda e: e.tensor_scalar(out=ptab[:, 15:19], in0=c.pp[:, 72:76], scalar1=-1.0, scalar2=1.0,
                                               op0=ALU.mult, op1=ALU.add), reads=[c.pp.d, ptab.d], writes=[ptab.d])
        nbv = sb("nbv_sb", [128, 2], F32, ls)
        kb.dma("sp", nbv.ch, nbv[:], g.nbv_d, writes=[nbv.d])
        lw = {}
        for nm, src in (("wup", g.w_up), ("aup", g.a_up)):
            for d_, (r0, r1) in enumerate(((0, 64), (64, 128))):
                t_ = sb("%s%d" % (nm, d_), [128, 512], BF16, ls)
                kb.op("pool", lambda e, t_=t_: e.memset(t_[:], 0.0), writes=[t_.d])
                kb.dma("pool", t_.ch, t_[r0:r1, :], src[r0:r1, :], writes=[t_.d])
                lw[(nm, d_)] = t_
        gup = sb("gup", [128, 512], BF16, ls)
        kb.dma("pool", gup.ch, gup[:], g.g_up, writes=[gup.d])
        MA = [sb("MA%d" % d_, [128, 320], BF16, ls) for d_ in range(2)]
        MB = [sb("MB%d" % d_, [128, 192], BF16, ls) for d_ in range(2)]
        mscope = ExitStack()
        mf = sb("maskf", [128, 4, 128], F32, mscope)
        kb.op("pool", lambda e: e.memset(mf[:], 1.0), writes=[mf.d])
        for i, (st, cm, op) in enumerate(((1, -1, ALU.is_gt), (1, -1, ALU.is_ge), (-1, 1, ALU.is_gt), (-1, 1, ALU.is_ge))):
            kb.op("pool", lambda e, i=i, st=st, cm=cm, op=op: e.affine_select(
                out=mf[:, i, :], in_=mf[:, i, :], pattern=[[st, 128]], compare_op=op, fill=0.0, base=0,
                channel_multiplier=cm), reads=[mf.d], writes=[mf.d])
        for d_ in range(2):
            s_i, i_i, t_i = (0, 1, 2) if d_ == 0 else (2, 3, 0)
            kb.op("dve", lambda e, d_=d_, s_i=s_i: e.tensor_copy(out=MA[d_][:, 0:128], in_=mf[:, s_i, :]), reads=[mf.d], writes=[MA[d_].d])
            for X in range(2):
                kb.op("act", lambda e, d_=d_, i_i=i_i, X=X: e.mul(out=MA[d_][64 * X:64 * X + 64, 128:192], in_=mf[64 * X:64 * X + 64, i_i, 64 * X:64 * X + 64], mul=-1.0),
                      reads=[mf.d, MA[d_].d], writes=[MA[d_].d])
                kb.op("act", lambda e, d_=d_, i_i=i_i, X=X: e.copy(out=MB[d_][64 * X:64 * X + 64, 128:192], in_=mf[64 * X:64 * X + 64, i_i, 64 * X:64 * X + 64]),
                      reads=[mf.d, MB[d_].d], writes=[MB[d_].d])
            kb.op("dve", lambda e, d_=d_, t_i=t_i: e.tensor_copy(out=MA[d_][:, 192:320], in_=mf[:, t_i, :]), reads=[mf.d, MA[d_].d], writes=[MA[d_].d])
            kb.op("dve", lambda e, d_=d_, s_i=s_i: e.tensor_copy(out=MB[d_][:, 0:128], in_=mf[:, s_i, :]), reads=[mf.d, MB[d_].d], writes=[MB[d_].d])
        kb.barrier()
        mscope.close()
        B = Ctx()
        B.n = 0

        def open_prep():
            B.n += 1
            B.prep = ExitStack()
            B.P = sb("Pld_%d" % B.n, [128, SEG + 2], F32, B.prep)
            if not hasattr(B, "Pch"):
                B.Pch = kb.chan()
            B.P._ch = B.Pch
            B.Ss = Ring([sb("Sft%d_%d" % (i, B.n), [128, SEG], F32, B.prep) for i in range(2)])
            B.sgt = sb("sgt_%d" % B.n, [128, SEG], F32, B.prep)
            B.a_d = [sb("a_d%d_%d" % (i, B.n), [128, SEG], BF16, B.prep) for i in range(2)]

        def close_prep():
            kb.barrier()
            B.prep.close()
        twd = sb("twd", [128, SEG], BF16, ls)
        adb = sb("adb", [128, SEG], BF16, ls)
        sgd = sb("sgd", [128, SEG], BF16, ls)
        rb = sb("r_b", [128, SEG], BF16, ls)
        kkb = sb("kk_b", [128, SEG], BF16, ls)
        vb = sb("v_b", [128, SEG], BF16, ls)
        kd = [sb("kd%d" % i, [128, SEG], BF16, ls) for i in range(2)]
        bd = [sb("bd%d" % i, [128, SEG], BF16, ls) for i in range(2)]
        cumz = [sb("cumz%d" % i, [128, SEG + 1], F32, ls) for i in range(2)]
        gb = sb("g_b", [128, SEG], BF16, ls)
        bon = sb("bonus", [128, SEG], BF16, ls)
        yacc = sb("yacc", [128, SEG], F32, ls)
        vTbd = sb("vTbd", [128, NCH, 2, 64], BF16, ls)
        Vbd = sb("Vbd", [128, NCH, 128], BF16, ls)
        gC = [sb("gC%d" % i, [128, NCH], F32, ls) for i in range(2)]
        kb.op("pool", lambda e: e.memset(vTbd[:], 0.0), writes=[vTbd.d])
        for d_ in range(2):
            kb.op("pool", lambda e, d_=d_: e.memset(cumz[d_][:, 0:1], 0.0), writes=[cumz[d_].d])
        Dbuf = [Ring([sb("D%d_%d" % (j, i), [128, 512], F32, ls) for i in range(1)]) for j in range(3)]
        Ebuf = [Ring([sb("E%d_%d" % (j, i), [128, 512], BF16, ls) for i in range(1)]) for j in range(5)]
        WA, DEPTH = 5, 7

        class Chain:
            pass

        def open_scan(specs):
            B.n += 1
            B.scan = ExitStack()
            chains = []
            for spec in specs:
                d_, cdir, state_only = spec[0], spec[1], spec[2]
                ch = Chain()
                ch.d_, ch.cdir, ch.state_only = d_, cdir, state_only
                bufs = spec[3] if len(spec) > 3 else {}
                ch.kk = bufs.get("kk", kkb)
                ch.kd = bufs.get("kd", kd[d_])
                ch.bd = bufs.get("bd", bd[d_])
                ch.cumz = bufs.get("cumz", cumz[d_])
                ch.gC = bufs.get("gC", gC[d_])
                ch.Vbd = bufs.get("Vbd", Vbd)
                ch.r = rb
                tg = "%d_%d" % (d_, B.n)
                ch.RKs = Ring([sb("RK%d_%s" % (i, tg), [128, 8, 192], BF16, B.scan) for i in range(2)])
                ch.KH = sb("KH_%s" % tg, [128, 8, 128], BF16, B.scan)
                ch.BH = sb("BH_%s" % tg, [128, 8, 128], BF16, B.scan)
                ch.KG = sb("KG_%s" % tg, [128, 8, 128], BF16, B.scan)
                ch.BG = sb("BG_%s" % tg, [128, 8, 128], BF16, B.scan)
                for b_ in ch.RKs.bufs + [ch.KH, ch.BH, ch.KG, ch.BG]:
                    kb.op("pool", lambda e, b_=b_: e.memset(b_[:], 0.0), writes=[b_.d])
                ch.stash = []
                for i in range(DEPTH):
                    st = Ctx()
                    st.NA = sb("NA%d_%s" % (i, tg), [128, 320], BF16, B.scan)
                    st.BB = sb("BB%d_%s" % (i, tg), [128, 192], BF16, B.scan)
                    st.W = sb("W%d_%s" % (i, tg), [128, 128], BF16, B.scan)
                    st.KBt = sb("KBt%d_%s" % (i, tg), [128, 256], BF16, B.scan)
                    ch.stash.append(st)
                ch.tmps = []
                for i in range(WA):
                    tp_ = Ctx()
                    tp_.XWT = [sb("XWT%d%d_%s" % (i, k_, tg), [128, 384], BF16, B.scan) for k_ in range(2)]
                    ch.tmps.append(tp_)
                ch.RHSs = Ring([sb("RHS%d_%s" % (i, tg), [128, 128], BF16, B.scan) for i in range(2)])
                ch.Us = Ring([sb("U%d_%s" % (i, tg), [128, 128], BF16, B.scan) for i in range(2)])
                ch.M0f = sb("M0f_%s" % tg, [128, 128], F32, B.scan)
                ch.M0b = sb("M0b_%s" % tg, [128, 128], BF16, B.scan)
                chains.append(ch)
            return chains

        def close_scan():
            kb.barrier()
            B.scan.close()

        pst = Ring([sb("pst%d" % i, [128, 512], F32, ls) for i in range(2)])
        osbs = Ring([sb("orw%d" % i, [128, 512], BF16, ls) for i in range(1)])
        banks = Ring([g.banks[i] for i in range(8)])

        def load_shift(seg, chunk, out_S):
            base = rs_base(seg)
            kb.dma("sp", B.P.ch, B.P[:, 1:SEG + 1], g.pr_s[chunk, :, base:base + SEG], writes=[B.P.d])
            if seg == 0:
                kb.dma("sp", B.P.ch, B.P[:, 0:1], g.pr_s[chunk, :, base - 1:base], writes=[B.P.d], allow_slow_non_contiguous=True)
                kb.dma("sp", B.P.ch, B.P[:, SEG + 1:SEG + 2], g.pr_s[chunk, :, base + SEG:base + SEG + 1], writes=[B.P.d], allow_slow_non_contiguous=True)
                kb.op("dve", lambda e: e.tensor_scalar_mul(out=B.P[:, 0:1], in0=B.P[:, 0:1], scalar1=nbv[:, 0:1]),
                      reads=[B.P.d, nbv.d], writes=[B.P.d])
                kb.op("dve", lambda e: e.tensor_scalar_mul(out=B.P[:, SEG + 1:SEG + 2], in0=B.P[:, SEG + 1:SEG + 2], scalar1=nbv[:, 1:2]),
                      reads=[B.P.d, nbv.d], writes=[B.P.d])
            else:
                kb.op("pool", lambda e: e.memset(B.P[:, 0:1], 0.0), reads=[B.P.d], writes=[B.P.d])
                kb.op("pool", lambda e: e.memset(B.P[:, SEG + 1:SEG + 2], 0.0), reads=[B.P.d], writes=[B.P.d])
            kb.op("act", lambda e: e.activation(out=out_S[:], in_=B.P[:, 1:SEG + 1], func=AF.Copy, scale=ptab[:, chunk:chunk + 1]),
                  reads=[B.P.d, ptab.d], writes=[out_S.d])
            kb.op("dve", lambda e: e.scalar_tensor_tensor(out=out_S[:], in0=B.P[:, 0:SEG], scalar=c.pp[:, 22 + chunk:23 + chunk],
                                                          in1=out_S[:], op0=ALU.mult, op1=ALU.add),
                  reads=[B.P.d, c.pp.d, out_S.d], writes=[out_S.d])
            kb.op("dve", lambda e: e.scalar_tensor_tensor(out=out_S[:], in0=B.P[:, 2:SEG + 2], scalar=c.pp[:, 37 + chunk:38 + chunk],
                                                          in1=out_S[:], op0=ALU.mult, op1=ALU.add),
                  reads=[B.P.d, c.pp.d, out_S.d], writes=[out_S.d])

        def bdsum(dst_fn, src_buf, src_ap_fn, f32=False):
            for b4 in range(4):
                bank = banks.next()
                lhs = c.bdonesf if f32 else c.bdones
                kb.op("pe", lambda e, bank=bank, b4=b4, lhs=lhs: e.matmul(bank[:, :], lhsT=lhs[:], rhs=src_ap_fn(b4), start=True, stop=True),
                      reads=[lhs.d, src_buf.d], writes=[bank.d])
                dst_fn(b4, bank)

        def decay_and_a(d_, wup_t, aup_t, w0col, a0col, hp):
            for b4 in range(4):
                sl = slice(b4 * 512, (b4 + 1) * 512)
                bank = banks.next()
                kb.op("pe", lambda e, bank=bank, sl=sl: e.matmul(
                    bank[:, :], lhsT=wup_t[:, hp * 128:(hp + 1) * 128], rhs=twd[:, sl], start=True, stop=True),
                    reads=[wup_t.d, twd.d], writes=[bank.d])
                kb.op("act", lambda e, bank=bank, sl=sl: e.activation(out=B.sgt[:, sl], in_=bank[:, :], func=AF.Sigmoid, bias=w0col[0]),
                      reads=[bank.d, w0col[1]], writes=[B.sgt.d])
                bank = banks.next()
                kb.op("pe", lambda e, bank=bank, sl=sl: e.matmul(
                    bank[:, :], lhsT=aup_t[:, hp * 128:(hp + 1) * 128], rhs=adb[:, sl], start=True, stop=True),
                    reads=[aup_t.d, adb.d], writes=[bank.d])
                kb.op("act", lambda e, bank=bank, sl=sl: e.activation(out=B.a_d[d_][:, sl], in_=bank[:, :], func=AF.Sigmoid, bias=a0col[0]),
                      reads=[bank.d, a0col[1]], writes=[B.a_d[d_].d])
            kb.op("dve", lambda e: e.tensor_tensor_scan(out=cumz[d_][:, 1:SEG + 1], data0=B.sgt[:], data1=B.sgt[:],
                                                        initial=0.0, op0=ALU.add, op1=ALU.bypass),
                  reads=[B.sgt.d], writes=[cumz[d_].d])
            kb.op("dve", lambda e: e.tensor_tensor(out=gC[d_][:], in0=cumz[d_][:, 64:SEG + 1:64], in1=cumz[d_][:, 0:SEG:64],
                                                   op=ALU.subtract), reads=[cumz[d_].d], writes=[gC[d_].d])
            kb.op("act", lambda e: e.activation(out=gC[d_][:], in_=gC[d_][:], func=AF.Exp, scale=-KAPPA),
                  reads=[gC[d_].d], writes=[gC[d_].d])

        def prep_v(loader, vb=vb, Vbd=Vbd):
            S = B.Ss.next()
            loader(S)
            kb.op("act", lambda e, S=S: e.copy(out=vb[:], in_=S[:]), reads=[S.d], writes=[vb.d])
            for X in range(2):
                kb.op("pool", lambda e, X=X: e.tensor_copy(
                    out=vTbd[64 * X:64 * X + 64, :, X, :], in_=vb[64 * X:64 * X + 64, :].rearrange("p (c t) -> p c t", t=64)),
                    reads=[vb.d], writes=[vTbd.d])
            for c8 in range(NCH // 8):
                bank = banks.next()
                bv = bank.t[:].bitcast(BF16)
                for j in range(8):
                    ci = c8 * 8 + j
                    kb.op("pe", lambda e, bv=bv, j=j, ci=ci: e.transpose(
                        out=bv[:, j * 128:(j + 1) * 128], in_=vTbd[:, ci, :, :].rearrange("p x t -> p (x t)"), identity=c.identb[:]),
                        reads=[vTbd.d, c.identb.d], writes=[bank.d], inc=(j == 7))
                kb.op("act", lambda e, bv=bv, c8=c8: e.copy(out=Vbd[:, c8 * 8:(c8 + 1) * 8, :].rearrange("p c n -> p (c n)"), in_=bv[:, :]),
                      reads=[bank.d], writes=[Vbd.d])

        def prep_k(loader, hp, dl, kkb=kkb):
            Sk = B.Ss.next()
            loader(Sk)
            for d_ in dl:
                kb.op("dve", lambda e, d_=d_: e.tensor_scalar(
                    out=kd[d_][:], in0=B.a_d[d_][:], scalar1=c.pp[:, 72 + hp:73 + hp], scalar2=ptab[:, 15 + hp:16 + hp],
                    op0=ALU.mult, op1=ALU.add), reads=[B.a_d[d_].d, c.pp.d, ptab.d], writes=[kd[d_].d])
                kb.op("pool", lambda e, d_=d_: e.tensor_tensor(out=kd[d_][:], in0=kd[d_][:], in1=Sk[:], op=ALU.mult),
                      reads=[kd[d_].d, Sk.d], writes=[kd[d_].d])
            kb.op("act", lambda e: e.activation(out=Sk[:], in_=Sk[:], func=AF.Copy, scale=c.pp[:, 68 + hp:69 + hp]),
                  reads=[Sk.d, c.pp.d], writes=[Sk.d])
            S2 = B.Ss.next()
            kb.op("act", lambda e: e.activation(out=S2[:], in_=Sk[:], func=AF.Square), reads=[Sk.d], writes=[S2.d])
            kb.op("dve", lambda e: e.tensor_copy(out=kkb[:], in_=S2[:]), reads=[S2.d], writes=[kkb.d])

            def kk_dst(b4, bank):
                sl = slice(b4 * 512, (b4 + 1) * 512)
                kb.op("act", lambda e: e.activation(out=S2[:, sl], in_=bank[:, :], func=AF.Sqrt, scale=64.0),
                      reads=[bank.d], writes=[S2.d])
                kb.op("dve", lambda e: e.tensor_scalar_max(out=S2[:, sl], in0=S2[:, sl], scalar1=1e-12), reads=[S2.d], writes=[S2.d])
                kb.op("dve", lambda e: e.reciprocal(out=S2[:, sl], in_=S2[:, sl]), reads=[S2.d], writes=[S2.d])
            bdsum(kk_dst, kkb, lambda b4: kkb[:, b4 * 512:(b4 + 1) * 512])
            kb.op("dve", lambda e: e.tensor_tensor(out=kkb[:], in0=Sk[:], in1=S2[:], op=ALU.mult),
                  reads=[Sk.d, S2.d], writes=[kkb.d])
            for d_ in dl:
                kb.op("pool", lambda e, d_=d_: e.tensor_tensor(out=bd[d_][:], in0=kkb[:], in1=B.a_d[d_][:], op=ALU.mult),
                      reads=[kkb.d, B.a_d[d_].d], writes=[bd[d_].d])
            return S2

        yts = [T("yacc%d" % i) for i in range(NCH)]

        class View:
            def __init__(self, ap, name):
                self.t = ap
                self.d = T(name)

            def __getitem__(self, k):
                return self.t[k]
        Vbd2 = View(yacc.t[:].bitcast(BF16).rearrange("p (c n) -> p c n", n=128), "Vbd2")
        ywritten = [False] * NCH

        def block_prep(ch, b4):
            d_, cdir, state_only = ch.d_, ch.cdir, ch.state_only
            b0 = b4 * 512
            czv = ch.cumz
            cur = czv[:, 1 + b0:1 + b0 + 512].rearrange("p (c t) -> p c t", t=64)
            prv = czv[:, b0:b0 + 512].rearrange("p (c t) -> p c t", t=64)
            bs = bcast_last(czv[:, b0:b0 + 512:64], 64)
            be = bcast_last(czv[:, b0 + 64:b0 + 512 + 1:64], 64)
            Dl = [r_.next() for r_ in Dbuf]
            El = [r_.next() for r_ in Ebuf]
            if cdir == 0:
                dspec = ((cur, bs), (prv, bs), (cur, be))
                espec = ((0, -KAPPA), (1, -KAPPA), (0, KAPPA), (2, KAPPA))
            else:
                dspec = ((prv, be), (cur, be), (prv, bs))
                espec = ((0, KAPPA), (1, KAPPA), (0, -KAPPA), (2, -KAPPA))
            for j, (a_, b_) in enumerate(dspec):
                kb.op("dve", lambda e, j=j, a_=a_, b_=b_, Dl=Dl: e.tensor_tensor(
                    out=Dl[j][:].rearrange("p (c t) -> p c t", t=64), in0=a_, in1=b_, op=ALU.subtract),
                    reads=[czv.d], writes=[Dl[j].d])
            for j, (di_, sc) in enumerate(espec):
                if state_only and j == 0:
                    continue
                kb.op("act", lambda e, j=j, di_=di_, sc=sc, Dl=Dl, El=El: e.activation(out=El[j][:], in_=Dl[di_][:], func=AF.Exp, scale=sc),
                      reads=[Dl[di_].d], writes=[El[j].d])
            RK = ch.RKs.next()
            if not state_only:
                kb.op("pool", lambda e, RK=RK, El=El: e.tensor_tensor(
                    out=RK[:, :, 128:192], in0=ch.r[:, b0:b0 + 512].rearrange("p (c t) -> p c t", t=64),
                    in1=El[0][:].rearrange("p (c t) -> p c t", t=64), op=ALU.mult), reads=[ch.r.d, El[0].d], writes=[RK.d])
            kb.op("act", lambda e, El=El: e.mul(out=El[4][:], in_=El[3][:], mul=-1.0), reads=[El[3].d], writes=[El[4].d])
            prods = ((RK, ch.kk, 1), (ch.KH, ch.kd, 2), (ch.BH, ch.bd, 2), (ch.KG, ch.kd, 3), (ch.BG, ch.bd, 4))
            pi = 0
            for (dstb, srcb, ei) in prods:
                for X in range(2):
                    o_ap = dstb[64 * X:64 * X + 64, :, 64 * X:64 * X + 64]
                    i0 = srcb[64 * X:64 * X + 64, b0:b0 + 512].rearrange("p (c t) -> p c t", t=64)
                    i1 = El[ei][64 * X:64 * X + 64, :].rearrange("p (c t) -> p c t", t=64)
                    eng = "pool" if pi % 2 == 0 else "dve"
                    pi += 1
                    kb.op(eng, lambda e, o_ap=o_ap, i0=i0, i1=i1: e.tensor_tensor(out=o_ap, in0=i0, in1=i1, op=ALU.mult),
                          reads=[srcb.d, El[ei].d], writes=[dstb.d])
            return RK

        def blk2(ap_fn, w):
            base = ap_fn[:, 0:128]
            return bass.AP(base.tensor, base.offset, [list(base.ap[0]), [256, 2], [1, 128]])

        def gen_A(ch, ci):
            cg = ch.order[ci]
            b4, cl = divmod(cg, 8)
            RK = ch.blkRK[b4]
            KH, BH, KG, BG = ch.KH, ch.BH, ch.KG, ch.BG
            st, tp_ = ch.stash[ci % DEPTH], ch.tmps[ci % WA]
            MAm, MBm = MA[ch.cdir], MB[ch.cdir]
            rk = RK[:, cl, :]
            kkbd = RK[:, cl, 0:128]
            NA, BB = st.NA, st.BB
            b1 = banks.next()
            kb.op("pe", lambda e: e.matmul(b1[:, 0:192], lhsT=BH[:, cl, :], rhs=rk, start=True, stop=True), reads=[BH.d, RK.d], writes=[b1.d], inc=False)
            kb.op("pe", lambda e: e.matmul(b1[:, 192:320], lhsT=kkbd, rhs=BH[:, cl, :], start=True, stop=True), reads=[BH.d, RK.d], writes=[b1.d])
            kb.op("dve", lambda e: e.tensor_tensor(out=NA[:], in0=b1[:, 0:320], in1=MAm[:], op=ALU.mult), reads=[b1.d, MAm.d], writes=[NA.d])
            b2 = banks.next()
            kb.op("pe", lambda e: e.matmul(b2[:, 0:192], lhsT=KH[:, cl, :], rhs=rk, start=True, stop=True), reads=[KH.d, RK.d], writes=[b2.d])
            kb.op("dve", lambda e: e.tensor_tensor(out=BB[:], in0=b2[:, 0:192], in1=MBm[:], op=ALU.mult), reads=[b2.d, MBm.d], writes=[BB.d])
            yield
            bt = banks.next()
            btv = bt.t[:].bitcast(BF16)
            kb.op("pe", lambda e: e.transpose(out=btv[:, 0:128], in_=KG[:, cl, :], identity=c.identb[:]), reads=[KG.d, c.identb.d], writes=[bt.d], inc=False)
            kb.op("pe", lambda e: e.transpose(out=btv[:, 128:256], in_=BG[:, cl, :], identity=c.identb[:]), reads=[BG.d, c.identb.d], writes=[bt.d])
            kb.op("act", lambda e: e.copy(out=st.KBt[:], in_=btv[:, 0:256]), reads=[bt.d], writes=[st.KBt.d])
            Nap, NTT = NA[:, 0:128], NA[:, 192:320]
            XWT = tp_.XWT[0]
            kb.op("pool", lambda e: e.tensor_tensor(out=XWT[:, 128:256], in0=c.identb[:], in1=Nap, op=ALU.subtract),
                  reads=[c.identb.d, NA.d], writes=[XWT.d])
            bx = banks.next()
            kb.op("pe", lambda e: e.matmul(bx[:, 0:128], lhsT=NTT, rhs=Nap, start=True, stop=True), reads=[NA.d], writes=[bx.d], inc=False)
            kb.op("pe", lambda e: e.matmul(bx[:, 256:384], lhsT=Nap, rhs=NTT, start=True, stop=True), reads=[NA.d], writes=[bx.d])
            kb.op("act", lambda e: e.copy(out=blk2(XWT, 0), in_=blk2(bx, 0)), reads=[bx.d], writes=[XWT.d])
            yield
            for k in range(1, 6):
                XWc, XWn = tp_.XWT[(k - 1) % 2], tp_.XWT[k % 2]
                bz = banks.next()
                if k < 5:
                    kb.op("pe", lambda e, bz=bz, XWc=XWc: e.matmul(bz[:, 0:256], lhsT=XWc[:, 256:384], rhs=XWc[:, 0:256], start=True, stop=True),
                          reads=[XWc.d], writes=[bz.d], inc=False)
                    kb.op("pe", lambda e, bz=bz, XWc=XWc: e.matmul(bz[:, 256:384], lhsT=XWc[:, 0:128], rhs=XWc[:, 256:384], start=True, stop=True),
                          reads=[XWc.d], writes=[bz.d])
                    kb.op("act", lambda e, bz=bz, XWn=XWn: e.copy(out=blk2(XWn, 0), in_=blk2(bz, 0)), reads=[bz.d], writes=[XWn.d])
                    kb.op("dve", lambda e, bz=bz, XWn=XWn, XWc=XWc: e.tensor_tensor(out=XWn[:, 128:256], in0=XWc[:, 128:256], in1=bz[:, 128:256], op=ALU.add),
                          reads=[bz.d, XWc.d, XWn.d], writes=[XWn.d])
                else:
                    kb.op("pe", lambda e, bz=bz, XWc=XWc: e.matmul(bz[:, 0:128], lhsT=XWc[:, 256:384], rhs=XWc[:, 128:256], start=True, stop=True),
                          reads=[XWc.d], writes=[bz.d])
                    kb.op("dve", lambda e, bz=bz, XWc=XWc: e.tensor_tensor(out=st.W[:], in0=XWc[:, 128:256], in1=bz[:, 0:128], op=ALU.add),
                          reads=[bz.d, XWc.d], writes=[st.W.d])
                yield

        def gen_B(ch):
            M0f, M0b = ch.M0f, ch.M0b
            for ci in range(len(ch.order)):
                while ch.a_done <= ci:
                    yield
                cg = ch.order[ci]
                b4, cl = divmod(cg, 8)
                RK = ch.blkRK[b4]
                kkbd, rpl = RK[:, cl, 0:128], RK[:, cl, 128:192]
                st = ch.stash[ci % DEPTH]
                RHS, U = ch.RHSs.next(), ch.Us.next()
                br = banks.next()
                kb.op("pe", lambda e, br=br, kkbd=kkbd: e.matmul(br[:, 0:128], lhsT=kkbd, rhs=M0b[:], start=True, stop=False), reads=[RK.d, M0b.d], writes=[br.d], inc=False)
                kb.op("pe", lambda e, br=br, st=st, cg=cg: e.matmul(br[:, 0:128], lhsT=st.BB[:, 0:128], rhs=ch.Vbd[:, cg, :], start=False, stop=True), reads=[st.BB.d, ch.Vbd.d], writes=[br.d])
                kb.op("act", lambda e, br=br, RHS=RHS: e.copy(out=RHS[:], in_=br[:, 0:128]), reads=[br.d], writes=[RHS.d])
                yield
                bu = banks.next()
                kb.op("pe", lambda e, bu=bu, st=st, RHS=RHS: e.matmul(bu[:, 0:128], lhsT=st.W[:], rhs=RHS[:], start=True, stop=True), reads=[st.W.d, RHS.d], writes=[bu.d])
                kb.op("act", lambda e, bu=bu, U=U: e.copy(out=U[:], in_=bu[:, 0:128]), reads=[bu.d], writes=[U.d])
                yield
                bS = banks.next()
                kb.op("pe", lambda e, bS=bS, st=st, cg=cg: e.matmul(bS[:, 0:128], lhsT=st.KBt[:, 0:128], rhs=ch.Vbd[:, cg, :], start=True, stop=False), reads=[st.KBt.d, ch.Vbd.d], writes=[bS.d], inc=False)
                kb.op("pe", lambda e, bS=bS, st=st, U=U: e.matmul(bS[:, 0:128], lhsT=st.KBt[:, 128:256], rhs=U[:], start=False, stop=True), reads=[st.KBt.d, U.d], writes=[bS.d])
                if not ch.state_only:
                    bY = banks.next()
                    kb.op("pe", lambda e, bY=bY, rpl=rpl: e.matmul(bY[:, 0:64], lhsT=M0b[:], rhs=rpl, start=True, stop=False), reads=[M0b.d, RK.d], writes=[bY.d], inc=False)
                    kb.op("pe", lambda e, bY=bY, st=st, cg=cg: e.matmul(bY[:, 0:64], lhsT=ch.Vbd[:, cg, :], rhs=st.BB[:, 128:192], start=False, stop=False), reads=[ch.Vbd.d, st.BB.d], writes=[bY.d], inc=False)
                    kb.op("pe", lambda e, bY=bY, st=st, U=U: e.matmul(bY[:, 0:64], lhsT=U[:], rhs=st.NA[:, 128:192], start=False, stop=True), reads=[U.d, st.NA.d], writes=[bY.d])
                kb.op("dve", lambda e, bS=bS, cg=cg: e.scalar_tensor_tensor(out=M0f[:], in0=M0f[:], scalar=ch.gC[:, cg:cg + 1], in1=bS[:, 0:128], op0=ALU.mult, op1=ALU.add),
                      reads=[M0f.d, ch.gC.d, bS.d], writes=[M0f.d])
                kb.op("act", lambda e: e.copy(out=M0b[:], in_=M0f[:]), reads=[M0f.d], writes=[M0b.d])
                if not ch.state_only:
                    dst = yacc[:, cg * 64:(cg + 1) * 64]
                    src = bY[:, 0:64]
                    if not ywritten[cg]:
                        kb.op("act", lambda e, dst=dst, src=src: e.copy(out=dst, in_=src), reads=[bY.d], writes=[yts[cg]])
                    else:
                        kb.op("dve", lambda e, dst=dst, src=src: e.tensor_tensor(out=dst, in0=dst, in1=src, op=ALU.add), reads=[bY.d, yts[cg]], writes=[yts[cg]])
                    ywritten[cg] = True
                ch.b_done = ci + 1
                yield

        def run_chains(chains):
            for ch in chains:
                ch.order = list(range(NCH)) if ch.cdir == 0 else list(range(NCH - 1, -1, -1))
                ch.blkRK = {}
                ch.active = []
                ch.next_a = 0
                ch.a_done = 0
                ch.b_done = 0
                ch.gb = gen_B(ch)
                ch.b_fin = False
            def adv_b():
                for ch in chains:
                    if not ch.b_fin:
                        try:
                            next(ch.gb)
                        except StopIteration:
                            ch.b_fin = True

            while not all(ch.b_fin for ch in chains):
                for ch in chains:
                    while ch.next_a < NCH and len(ch.active) < WA and ch.next_a - ch.b_done < DEPTH:
                        b4 = ch.order[ch.next_a] // 8
                        if b4 not in ch.blkRK:
                            if ch.active:
                                break
                            ch.blkRK[b4] = block_prep(ch, b4)
                        ch.active.append(gen_A(ch, ch.next_a))
                        ch.next_a += 1
                snaps = [list(ch.active) for ch in chains]
                nA = max([len(sn) for sn in snaps] + [1])
                for i in range(nA):
                    for ch, sn in zip(chains, snaps):
                        if i < len(sn):
                            try:
                                next(sn[i])
                            except StopIteration:
                                ch.active.remove(sn[i])
                                ch.a_done += 1
                    adv_b()

        Sf = [sb("Sf%d" % h_, [128, 128], F32, ls) for h_ in range(4)]
        Sbk = [sb("Sbk%d" % h_, [128, 128], F32, ls) for h_ in range(4)]
        Mst = [sb("Mst%d" % h_, [128, 128], F32, ls) for h_ in range(4)]
        for t_ in Sf + Sbk + Mst:
            kb.op("pool", lambda e, t_=t_: e.memset(t_[:], 0.0), writes=[t_.d])
        if do_slots:
            sft = sb("sft", [128, 42], F32, ls)
            kb.dma("sp", sft.ch, sft[:], g.sf_d, writes=[sft.d])
            mu_a = sb("mu_a", [128, 15], F32, ls)
            mu_b = sb("mu_b", [128, 15], F32, ls)
            w0e = sb("w0e", [128, 4], F32, ls)
            a0e = sb("a0e", [128, 4], F32, ls)
            wupe = sb("wupe", [128, 512], BF16, ls)
            aupe = sb("aupe", [128, 512], BF16, ls)

            def blend(dst, srcf, srcb_, rf, rb_, ff, fb):
                kb.op("dve", lambda e: e.tensor_scalar_mul(out=dst, in0=srcf, scalar1=ff), reads=rf + [sft.d], writes=[rb_.d])
                kb.op("dve", lambda e: e.scalar_tensor_tensor(out=dst, in0=srcb_, scalar=fb, in1=dst, op0=ALU.mult, op1=ALU.add),
                      reads=rf + [sft.d, rb_.d], writes=[rb_.d])

            for j in range(7):
                fc = [sft[:, j * 6 + k_:j * 6 + k_ + 1] for k_ in range(6)]
                blend(mu_a[:], c.pp[:, 22:37], c.pp[:, 37:52], [c.pp.d], mu_a, fc[0], fc[1])
                blend(mu_b[:], c.pp[:, 37:52], c.pp[:, 22:37], [c.pp.d], mu_b, fc[0], fc[1])
                blend(w0e[:], c.pp[:, 52:56], c.pp[:, 56:60], [c.pp.d], w0e, fc[0], fc[1])
                blend(a0e[:], c.pp[:, 60:64], c.pp[:, 64:68], [c.pp.d], a0e, fc[0], fc[1])
                blend(wupe[:], lw[("wup", 0)][:], lw[("wup", 1)][:], [lw[("wup", 0)].d, lw[("wup", 1)].d], wupe, fc[0], fc[1])
                blend(aupe[:], lw[("aup", 0)][:], lw[("aup", 1)][:], [lw[("aup", 0)].d, lw[("aup", 1)].d], aupe, fc[0], fc[1])

                def slot_loader(idx, chunk, j=j, fc=fc):
                    def f(out_S):
                        p0 = j * 2176
                        kb.dma("sp", B.P.ch, B.P[:, 1:SEG + 1], g.prs_s[idx, :, p0:p0 + SEG], writes=[B.P.d])
                        kb.dma("sp", B.P.ch, B.P[:, 0:1], g.prs_s[idx, :, p0 + SEG:p0 + SEG + 1], writes=[B.P.d], allow_slow_non_contiguous=True)
                        kb.dma("sp", B.P.ch, B.P[:, SEG + 1:SEG + 2], g.prs_s[idx, :, p0 + SEG + 1:p0 + SEG + 2], writes=[B.P.d],
                               allow_slow_non_contiguous=True)
                        kb.op("dve", lambda e: e.tensor_scalar_mul(out=B.P[:, 0:1], in0=B.P[:, 0:1], scalar1=fc[2]),
                              reads=[B.P.d, sft.d], writes=[B.P.d])
                        kb.op("act", lambda e: e.activation(out=out_S[:], in_=B.P[:, 1:SEG + 1], func=AF.Copy, scale=ptab[:, chunk:chunk + 1]),
                              reads=[B.P.d, ptab.d], writes=[out_S.d])
                        kb.op("dve", lambda e: e.scalar_tensor_tensor(out=out_S[:], in0=B.P[:, 0:SEG], scalar=mu_a[:, chunk:chunk + 1],
                                                                      in1=out_S[:], op0=ALU.mult, op1=ALU.add),
                              reads=[B.P.d, mu_a.d, out_S.d], writes=[out_S.d])
                        kb.op("dve", lambda e: e.scalar_tensor_tensor(out=out_S[:], in0=B.P[:, 2:SEG + 2], scalar=mu_b[:, chunk:chunk + 1],
                                                                      in1=out_S[:], op0=ALU.mult, op1=ALU.add),
                              reads=[B.P.d, mu_b.d, out_S.d], writes=[out_S.d])
                    return f
                open_prep()
                for idx, chunk, dstb, fn in ((8, 12, twd, AF.Tanh), (9, 13, adb, AF.Copy)):
                    S = B.Ss.next()
                    slot_loader(idx, chunk)(S)
                    kb.op("act", lambda e, dstb=dstb, fn=fn, S=S: e.activation(out=dstb[:], in_=S[:], func=fn), reads=[S.d], writes=[dstb.d])
                for pi_, (hpa, hpb) in enumerate(((0, 1), (2, 3))):
                    if pi_ > 0:
                        open_prep()
                    decay_and_a(0, wupe, aupe, (w0e[:, hpa:hpa + 1], w0e.d), (a0e[:, hpa:hpa + 1], a0e.d), hpa)
                    prep_v(slot_loader(4 + hpa, 8 + hpa))
                    prep_k(slot_loader(hpa, 4 + hpa), hpa, (0,))
                    decay_and_a(1, wupe, aupe, (w0e[:, hpb:hpb + 1], w0e.d), (a0e[:, hpb:hpb + 1], a0e.d), hpb)
                    prep_v(slot_loader(4 + hpb, 8 + hpb), vb=gb, Vbd=Vbd2)
                    prep_k(slot_loader(hpb, 4 + hpb), hpb, (1,), kkb=rb)
                    close_prep()
                    chs = open_scan([(0, 0, True), (1, 0, True, dict(kk=rb, Vbd=Vbd2))])
                    for ch, hp in zip(chs, (hpa, hpb)):
                        M0f, M0b = ch.M0f, ch.M0b
                        kb.op("dve", lambda e, hp=hp, fc=fc, M0f=M0f: e.tensor_scalar_mul(out=M0f[:], in0=Mst[hp][:], scalar1=fc[3]),
                              reads=[Mst[hp].d, sft.d], writes=[M0f.d])
                        kb.op("act", lambda e, M0f=M0f, M0b=M0b: e.copy(out=M0b[:], in_=M0f[:]), reads=[M0f.d], writes=[M0b.d])
                    run_chains(chs)
                    for ch, hp in zip(chs, (hpa, hpb)):
                        M0f = ch.M0f
                        kb.op("act", lambda e, hp=hp, M0f=M0f: e.copy(out=Mst[hp][:], in_=M0f[:]), reads=[M0f.d], writes=[Mst[hp].d])
                        kb.op("dve", lambda e, hp=hp, fc=fc, M0f=M0f: e.scalar_tensor_tensor(out=Sf[hp][:], in0=M0f[:], scalar=fc[4], in1=Sf[hp][:],
                                                                                  op0=ALU.mult, op1=ALU.add),
                              reads=[M0f.d, sft.d, Sf[hp].d], writes=[Sf[hp].d])
                        kb.op("dve", lambda e, hp=hp, fc=fc, M0f=M0f: e.scalar_tensor_tensor(out=Sbk[hp][:], in0=M0f[:], scalar=fc[5], in1=Sbk[hp][:],
                                                                                  op0=ALU.mult, op1=ALU.add),
                              reads=[M0f.d, sft.d, Sbk[hp].d], writes=[Sbk[hp].d])
                    close_scan()

        for seg in segs:
            open_prep()
            for chunk, dstb, fn in ((12, twd, AF.Tanh), (13, adb, AF.Copy), (14, sgd, AF.Sigmoid)):
                S = B.Ss.next()
                load_shift(seg, chunk, S)
                kb.op("act", lambda e, dstb=dstb, fn=fn, S=S: e.activation(out=dstb[:], in_=S[:], func=fn), reads=[S.d], writes=[dstb.d])
            for hi_, hp in enumerate(hps):
                if hi_ > 0:
                    open_prep()
                for d_ in range(2):
                    decay_and_a(d_, lw[("wup", d_)], lw[("aup", d_)],
                                (c.pp[:, 52 + d_ * 4 + hp:53 + d_ * 4 + hp], c.pp.d), (c.pp[:, 60 + d_ * 4 + hp:61 + d_ * 4 + hp], c.pp.d), hp)
                for b4 in range(4):
                    sl = slice(b4 * 512, (b4 + 1) * 512)
                    bank = banks.next()
                    kb.op("pe", lambda e, bank=bank, sl=sl, hp=hp: e.matmul(
                        bank[:, :], lhsT=gup[:, hp * 128:(hp + 1) * 128], rhs=sgd[:, sl], start=True, stop=True),
                        reads=[gup.d, sgd.d], writes=[bank.d])
                    kb.op("act", lambda e, bank=bank, sl=sl: e.copy(out=gb[:, sl], in_=bank[:, :]), reads=[bank.d], writes=[gb.d])
                S = B.Ss.next()
                load_shift(seg, hp, S)
                kb.op("act", lambda e, S=S: e.copy(out=rb[:], in_=S[:]), reads=[S.d], writes=[rb.d])
                prep_v(lambda S_, seg=seg, hp=hp: load_shift(seg, 8 + hp, S_))
                S2 = prep_k(lambda S_, seg=seg, hp=hp: load_shift(seg, 4 + hp, S_), hp, (0, 1))
                kb.op("pool", lambda e, S2=S2: e.tensor_tensor(out=S2[:], in0=kd[0][:], in1=kd[1][:], op=ALU.add),
                      reads=[kd[0].d, kd[1].d], writes=[S2.d])
                kb.op("dve", lambda e, S2=S2, hp=hp: e.scalar_tensor_tensor(out=bon[:], in0=rb[:], scalar=c.pp[:, 76 + hp:77 + hp], in1=S2[:],
                                                                            op0=ALU.mult, op1=ALU.mult),
                      reads=[rb.d, c.pp.d, S2.d], writes=[bon.d])

                def bon_dst(b4, bank):
                    sl = slice(b4 * 512, (b4 + 1) * 512)
                    kb.op("dve", lambda e: e.scalar_tensor_tensor(out=bon[:, sl], in0=bank[:, :], scalar=64.0, in1=vb[:, sl],
                                                                  op0=ALU.mult, op1=ALU.mult), reads=[bank.d, vb.d], writes=[bon.d])
                bdsum(bon_dst, bon, lambda b4: bon[:, b4 * 512:(b4 + 1) * 512])
                close_prep()
                chains = open_scan([(d_, d_, False) for d_ in dirs])
                for i_ in range(NCH):
                    ywritten[i_] = False
                for ch in chains:
                    M0f, M0b = ch.M0f, ch.M0b
                    if seg == 0:
                        src_state = Sf[hp] if ch.d_ == 0 else Sbk[hp]
                        kb.op("act", lambda e, src_state=src_state, M0f=M0f: e.copy(out=M0f[:], in_=src_state[:]), reads=[src_state.d], writes=[M0f.d])
                        kb.op("act", lambda e, M0f=M0f, M0b=M0b: e.copy(out=M0b[:], in_=M0f[:]), reads=[M0f.d], writes=[M0b.d])
                    else:
                        kb.op("pool", lambda e, M0f=M0f: e.memset(M0f[:], 0.0), writes=[M0f.d])
                        kb.op("pool", lambda e, M0b=M0b: e.memset(M0b[:], 0.0), writes=[M0b.d])
                run_chains(chains)
                for b4 in range(4):
                    sl = slice(b4 * 512, (b4 + 1) * 512)
                    bank = banks.next()
                    kb.op("pe", lambda e, bank=bank, sl=sl: e.matmul(bank[:, :], lhsT=c.bdonesf[:], rhs=yacc[:, sl], start=True, stop=True),
                          reads=[c.bdonesf.d] + yts[b4 * 8:(b4 + 1) * 8], writes=[bank.d])
                    t1 = pst.next()
                    kb.op("dve", lambda e, bank=bank, sl=sl, t1=t1: e.tensor_tensor(out=t1[:], in0=yacc[:, sl], in1=bank[:, :], op=ALU.subtract),
                          reads=[bank.d] + yts[b4 * 8:(b4 + 1) * 8], writes=[t1.d])
                    t2 = pst.next()
                    kb.op("act", lambda e, t1=t1, t2=t2: e.activation(out=t2[:], in_=t1[:], func=AF.Square), reads=[t1.d], writes=[t2.d])
                    bank2 = banks.next()
                    kb.op("pe", lambda e, bank2=bank2, t2=t2: e.matmul(bank2[:, :], lhsT=c.bdonesf[:], rhs=t2[:], start=True, stop=True),
                          reads=[c.bdonesf.d, t2.d], writes=[bank2.d])
                    kb.op("dve", lambda e, bank2=bank2, t2=t2: e.tensor_scalar_add(out=t2[:], in0=bank2[:, :], scalar1=64e-5),
                          reads=[bank2.d], writes=[t2.d])
                    kb.op("act", lambda e, t2=t2: e.activation(out=t2[:], in_=t2[:], func=AF.Sqrt), reads=[t2.d], writes=[t2.d])
                    kb.op("dve", lambda e, t2=t2: e.reciprocal(out=t2[:], in_=t2[:]), reads=[t2.d], writes=[t2.d])
                    kb.op("dve", lambda e, t1=t1, t2=t2: e.tensor_tensor(out=t1[:], in0=t1[:], in1=t2[:], op=ALU.mult),
                          reads=[t1.d, t2.d], writes=[t1.d])
                    kb.op("act", lambda e, t1=t1, hp=hp: e.activation(out=t1[:], in_=t1[:], func=AF.Identity,
                                                                       scale=c.pp[:, 80 + hp:81 + hp], bias=c.pp[:, 84 + hp:85 + hp]),
                          reads=[t1.d, c.pp.d], writes=[t1.d])
                    kb.op("pool", lambda e, t1=t1, sl=sl: e.tensor_tensor(out=t1[:], in0=t1[:], in1=bon[:, sl], op=ALU.add),
                          reads=[t1.d, bon.d], writes=[t1.d])
                    o = osbs.next()
                    kb.op("pool", lambda e, t1=t1, sl=sl, o=o: e.tensor_tensor(out=o[:], in0=t1[:], in1=gb[:, sl], op=ALU.mult),
                          reads=[t1.d, gb.d], writes=[o.d])
                    kb.dma("sp", o.sch, g.mix_s[4 + hp, :, seg * SEG + b4 * 512: seg * SEG + (b4 + 1) * 512], o[:], reads=[o.d])
                close_scan()
        kb.barrier()


def scan_step(g, d_, cl, cg, RK, KH, BH, KG, BG, Vbd, M0f, M0b, gCt, yacc, masks, banks, rings, first_dir, state_only=False):
    kb, c = g.kb, g.c
    MA, MB, MT = masks
    NAs, BBs, NTTs, XWs, XTs, RHSs, Us, KGts, BGts = rings
    rk = RK[:, cl, :, :].rearrange("p a n -> p (a n)")
    kkbd = RK[:, cl, 0, :]
    rbd = RK[:, cl, 1, :]
    NA, BB, NTT = NAs.next(), BBs.next(), NTTs.next()
    nsc = 128 if state_only else 256
    b1 = banks.next()
    kb.op("pe", lambda e: e.matmul(b1[:, 0:nsc], lhsT=BH[:, cl, :], rhs=rk[:, 0:nsc], start=True, stop=True), reads=[BH.d, RK.d], writes=[b1.d])
    kb.op("dve", lambda e: e.tensor_tensor(out=NA[:, 0:nsc], in0=b1[:, 0:nsc], in1=MA[:, 0:nsc], op=ALU.mult), reads=[b1.d, MA.d], writes=[NA.d])
    b2 = banks.next()
    kb.op("pe", lambda e: e.matmul(b2[:, 0:nsc], lhsT=KH[:, cl, :], rhs=rk[:, 0:nsc], start=True, stop=True), reads=[KH.d, RK.d], writes=[b2.d])
    kb.op("dve", lambda e: e.tensor_tensor(out=BB[:, 0:nsc], in0=b2[:, 0:nsc], in1=MB[:, 0:nsc], op=ALU.mult), reads=[b2.d, MB.d], writes=[BB.d])
    b3 = banks.next()
    kb.op("pe", lambda e: e.matmul(b3[:, 0:128], lhsT=kkbd, rhs=BH[:, cl, :], start=True, stop=True), reads=[BH.d, RK.d], writes=[b3.d])
    kb.op("dve", lambda e: e.tensor_tensor(out=NTT[:], in0=b3[:, 0:128], in1=MT[:], op=ALU.mult), reads=[b3.d, MT.d], writes=[NTT.d])
    KGt, BGt = KGts.next(), BGts.next()
    bt = banks.next()
    btv = bt.t[:].bitcast(BF16)
    kb.op("pe", lambda e: e.transpose(out=btv[:, 0:128], in_=KG[:, cl, :], identity=c.identb[:]), reads=[KG.d, c.identb.d], writes=[bt.d], inc=False)
    kb.op("pe", lambda e: e.transpose(out=btv[:, 128:256], in_=BG[:, cl, :], identity=c.identb[:]), reads=[BG.d, c.identb.d], writes=[bt.d])
    kb.op("act", lambda e: e.copy(out=KGt[:], in_=btv[:, 0:128]), reads=[bt.d], writes=[KGt.d])
    kb.op("act", lambda e: e.mul(out=BGt[:], in_=btv[:, 128:256], mul=-1.0), reads=[bt.d], writes=[BGt.d])
    Nap = NA[:, 0:128]
    XW = XWs.next()
    kb.op("pool", lambda e, XW=XW: e.tensor_tensor(out=XW[:, 128:256], in0=c.identb[:], in1=Nap, op=ALU.subtract),
          reads=[c.identb.d, NA.d], writes=[XW.d])
    bx = banks.next()
    kb.op("pe", lambda e: e.matmul(bx[:, 0:128], lhsT=NTT[:], rhs=Nap, start=True, stop=True), reads=[NTT.d, NA.d], writes=[bx.d])
    kb.op("act", lambda e, XW=XW: e.copy(out=XW[:, 0:128], in_=bx[:, 0:128]), reads=[bx.d], writes=[XW.d])
    XT = XTs.next()
    by = banks.next()
    kb.op("pe", lambda e: e.matmul(by[:, 0:128], lhsT=Nap, rhs=NTT[:], start=True, stop=True), reads=[NTT.d, NA.d], writes=[by.d])
    kb.op("act", lambda e, XT=XT: e.copy(out=XT[:], in_=by[:, 0:128]), reads=[by.d], writes=[XT.d])
    for k in range(1, 6):
        last = (k == 5)
        XWn = XWs.next()
        bz = banks.next()
        if not last:
            kb.op("pe", lambda e, bz=bz, XT=XT, XW=XW: e.matmul(bz[:, 0:256], lhsT=XT[:], rhs=XW[:, :], start=True, stop=True),
                  reads=[XT.d, XW.d], writes=[bz.d])
            kb.op("act", lambda e, bz=bz, XWn=XWn: e.copy(out=XWn[:, 0:128], in_=bz[:, 0:128]), reads=[bz.d], writes=[XWn.d])
            kb.op("dve", lambda e, bz=bz, XWn=XWn, XW=XW: e.tensor_tensor(out=XWn[:, 128:256], in0=XW[:, 128:256], in1=bz[:, 128:256], op=ALU.add),
                  reads=[bz.d, XW.d], writes=[XWn.d])
            XTn = XTs.next()
            bw = banks.next()
            kb.op("pe", lambda e, bw=bw, XT=XT, XW=XW: e.matmul(bw[:, 0:128], lhsT=XW[:, 0:128], rhs=XT[:], start=True, stop=True),
                  reads=[XT.d, XW.d], writes=[bw.d])
            kb.op("act", lambda e, bw=bw, XTn=XTn: e.copy(out=XTn[:], in_=bw[:, 0:128]), reads=[bw.d], writes=[XTn.d])
            XT = XTn
        else:
            kb.op("pe", lambda e, bz=bz, XT=XT, XW=XW: e.matmul(bz[:, 0:128], lhsT=XT[:], rhs=XW[:, 128:256], start=True, stop=True),
                  reads=[XT.d, XW.d], writes=[bz.d])
            kb.op("dve", lambda e, bz=bz, XWn=XWn, XW=XW: e.tensor_tensor(out=XWn[:, 128:256], in0=XW[:, 128:256], in1=bz[:, 0:128], op=ALU.add),
                  reads=[bz.d, XW.d], writes=[XWn.d])
        XW = XWn
    W = XW[:, 128:256]
    RHS, U = RHSs.next(), Us.next()
    br = banks.next()
    kb.op("pe", lambda e: e.matmul(br[:, 0:128], lhsT=kkbd, rhs=M0b[:], start=True, stop=False), reads=[RK.d, M0b.d], writes=[br.d], inc=False)
    kb.op("pe", lambda e: e.matmul(br[:, 0:128], lhsT=BB[:, 0:128], rhs=Vbd[:, cg, :], start=False, stop=True), reads=[BB.d, Vbd.d], writes=[br.d])
    kb.op("act", lambda e: e.copy(out=RHS[:], in_=br[:, 0:128]), reads=[br.d], writes=[RHS.d])
    bu = banks.next()
    kb.op("pe", lambda e: e.matmul(bu[:, 0:128], lhsT=W, rhs=RHS[:], start=True, stop=True), reads=[XW.d, RHS.d], writes=[bu.d])
    kb.op("act", lambda e: e.copy(out=U[:], in_=bu[:, 0:128]), reads=[bu.d], writes=[U.d])
    bY = banks.next() if not state_only else None
    if not state_only:
        kb.op("pe", lambda e: e.matmul(bY[:, 0:128], lhsT=M0b[:], rhs=rbd, start=True, stop=False), reads=[M0b.d, RK.d], writes=[bY.d], inc=False)
        kb.op("pe", lambda e: e.matmul(bY[:, 0:128], lhsT=Vbd[:, cg, :], rhs=BB[:, 128:256], start=False, stop=False), reads=[Vbd.d, BB.d], writes=[bY.d], inc=False)
        kb.op("pe", lambda e: e.matmul(bY[:, 0:128], lhsT=U[:], rhs=NA[:, 128:256], start=False, stop=True), reads=[U.d, NA.d], writes=[bY.d])
    for X in range(2 if not state_only else 0):
        dst = yacc[64 * X:64 * X + 64, cg * 64:(cg + 1) * 64]
        src = bY[64 * X:64 * X + 64, 64 * X:64 * X + 64]
        if first_dir:
            kb.op("act", lambda e, dst=dst, src=src: e.copy(out=dst, in_=src), reads=[bY.d], writes=[yacc.d])
        else:
            kb.op("dve", lambda e, dst=dst, src=src: e.tensor_tensor(out=dst, in0=dst, in1=src, op=ALU.add), reads=[bY.d, yacc.d], writes=[yacc.d])
    bS = banks.next()
    kb.op("pe", lambda e: e.matmul(bS[:, 0:128], lhsT=KGt[:], rhs=Vbd[:, cg, :], start=True, stop=False), reads=[KGt.d, Vbd.d], writes=[bS.d], inc=False)
    kb.op("pe", lambda e: e.matmul(bS[:, 0:128], lhsT=BGt[:], rhs=U[:], start=False, stop=True), reads=[BGt.d, U.d], writes=[bS.d])
    kb.op("dve", lambda e: e.scalar_tensor_tensor(out=M0f[:], in0=M0f[:], scalar=gCt[:, cg:cg + 1], in1=bS[:, 0:128], op0=ALU.mult, op1=ALU.add),
          reads=[M0f.d, gCt.d, bS.d], writes=[M0f.d])
    kb.op("act", lambda e: e.copy(out=M0b[:], in_=M0f[:]), reads=[M0f.d], writes=[M0b.d])


def bc_gate(g, seg, which, bc, bank_ring):
    kb, c = g.kb, g.c
    for half in range(2):
        bank = bank_ring.next()
        kb.op("pe", lambda e, bank=bank, half=half: e.matmul(
            bank[:, :], lhsT=c.sel[:, seg, :], rhs=c.gates[:, which * D + half * 512: which * D + (half + 1) * 512],
            start=True, stop=True), reads=[c.sel.d, c.gates.d], writes=[bank.d])
        kb.op("act", lambda e, bank=bank, half=half: e.copy(out=bc[:, half * 512:(half + 1) * 512], in_=bank[:, :]),
              reads=[bank.d], writes=[bc.d])


def phase3(g):
    nc, kb, sb, c = g.nc, g.kb, g.sb, g.c
    with ExitStack() as ls:
        wout = sb("wout", [128, 8, D], BF16, ls)
        wv = g.w_out.rearrange("(kc p) n -> p kc n", p=128)
        for kc in range(8):
            kb.dma("pool", wout.ch, wout[:, kc, :], wv[:, kc, :], writes=[wout.d])
        mixs = Ring([sb("mixb%d" % i, [128, 8, 512], BF16, ls) for i in range(2)])
        xring = Ring([sb("x3_%d" % i, [128, D], F32, ls) for i in range(3)])
        outs = Ring([sb("o3_%d" % i, [128, D], F32, ls) for i in range(2)])
        bcs = Ring([sb("bc3_%d" % i, [128, D], F32, ls) for i in range(2)])
        banks = Ring([g.banks[i] for i in range(6)])
        gbanks = Ring([g.banks[6], g.banks[7]])
        for seg in range(NSEG):
            bc = bcs.next()
            bc_gate(g, seg, 0, bc, gbanks)
            for b in range(4):
                t0 = seg * SEG + b * 512
                mb = mixs.next()
                kb.dma("sp", mb.ch, mb[:], g.mix_s[:, :, t0:t0 + 512].rearrange("c p t -> p c t"), writes=[mb.d])
                for t in range(4):
                    xt = xring.next()
                    kb.dma("sp", xt.ch, xt[:], g.xa[t0 + t * 128:t0 + (t + 1) * 128, :], writes=[xt.d])
                    o = outs.next()
                    tmp = o
                    for half in range(2):
                        bank = banks.next()
                        for kc in range(8):
                            kb.op("pe", lambda e, bank=bank, kc=kc, mb=mb, t=t, half=half: e.matmul(
                                bank[:, :], lhsT=mb[:, kc, t * 128:(t + 1) * 128], rhs=wout[:, kc, half * 512:(half + 1) * 512],
                                start=(kc == 0), stop=(kc == 7)), reads=[mb.d, wout.d], writes=[bank.d], inc=(kc == 7))
                        kb.op("dve", lambda e, bank=bank, tmp=tmp, bc=bc, half=half: e.tensor_tensor(
                            out=tmp[:, half * 512:(half + 1) * 512], in0=bank[:, :], in1=bc[:, half * 512:(half + 1) * 512],
                            op=ALU.mult), reads=[bank.d, bc.d], writes=[tmp.d])
                    kb.op("pool", lambda e, o=o, xt=xt: e.tensor_tensor(out=o[:], in0=o[:], in1=xt[:], op=ALU.add),
                          reads=[o.d, xt.d], writes=[o.d])
                    kb.dma("sp", o.sch, g.y[t0 + t * 128:t0 + (t + 1) * 128, :], o[:], reads=[o.d])
        kb.barrier()


def phase4(g):
    nc, kb, sb, c = g.nc, g.kb, g.sb, g.c
    NB = 256
    with ExitStack() as ls:
        w1 = sb("wff1", [128, 8, 4 * D], BF16, ls)
        w2 = sb("wff2", [128, 32, D], BF16, ls)
        w1v = g.w_ff1.rearrange("(kc p) n -> p kc n", p=128)
        w2v = g.w_ff2.rearrange("(kc p) n -> p kc n", p=128)
        for kc in range(8):
            kb.dma("pool", w1.ch, w1[:, kc, :], w1v[:, kc, :], writes=[w1.d])
        for kc in range(0, 32, 4):
            kb.dma("pool", w2.ch, w2[:, kc:kc + 4, :], w2v[:, kc:kc + 4, :], writes=[w2.d])
        xring = Ring([sb("x4_%d" % i, [128, D], F32, ls) for i in range(4)])
        xsbs = [[sb("xs4_%d_%d" % (r, t), [128, D], BF16, ls) for t in range(2)] for r in range(2)]
        stat = Ring([sb("st4_%d" % i, [128, 4], F32, ls) for i in range(4)])
        hTs = Ring([sb("h2T%d" % i, [128, 8, NB], BF16, ls) for i in range(2)])
        hid = sb("hid", [128, 32, NB], BF16, ls)
        rl = Ring([sb("rl%d" % i, [128, NB], F32, ls) for i in range(3)])
        outs = Ring([sb("o4_%d" % i, [128, D], F32, ls) for i in range(2)])
        bcs = Ring([sb("bc4_%d" % i, [128, D], F32, ls) for i in range(1)])
        tbanks = Ring([g.banks[0], g.banks[1]])
        fbanks = Ring([g.banks[2], g.banks[3], g.banks[4]])
        obanks = Ring([g.banks[5], g.banks[6], g.banks[7]])
        bi = 0
        blist = [(seg, b) for seg in range(NSEG) for b in range(SEG // NB)]
        pre = {}

        def rms4(i):
            seg_, b_ = blist[i]
            hT_ = hTs.next()
            xts_ = rms_to_hT(g, (xring, xsbs[i % 2], stat, tbanks, None), g.y, seg_ * SEG + b_ * NB, c.scale2, 24, seg_, hT_, nt=2)
            pre[i] = (hT_, xts_)
        rms4(0)
        for seg in range(NSEG):
            bc = bcs.next()
            bc_gate(g, seg, 1, bc, obanks)
            for b in range(SEG // NB):
                t0 = seg * SEG + b * NB
                if bi + 1 < len(blist):
                    rms4(bi + 1)
                hT, xts = pre.pop(bi)
                bi += 1
                for fc in range(32):
                    bank = fbanks.next()
                    for kc in range(8):
                        kb.op("pe", lambda e, bank=bank, kc=kc, fc=fc, hT=hT: e.matmul(
                            bank[:, 0:NB], lhsT=w1[:, kc, fc * 128:(fc + 1) * 128], rhs=hT[:, kc, :],
                            start=(kc == 0), stop=(kc == 7)), reads=[w1.d, hT.d], writes=[bank.d], inc=(kc == 7))
                    r_ = rl.next()
                    kb.op("act", lambda e, bank=bank, r_=r_: e.activation(out=r_[:], in_=bank[:, 0:NB], func=AF.Relu),
                          reads=[bank.d], writes=[r_.d])
                    kb.op("pool", lambda e, r_=r_, fc=fc: e.tensor_tensor(out=hid[:, fc, :], in0=r_[:], in1=r_[:], op=ALU.mult),
                          reads=[r_.d], writes=[hid.d])
                for t in range(2):
                    o = outs.next()
                    tmp = o
                    for half in range(2):
                        bank = obanks.next()
                        for fc in range(32):
                            kb.op("pe", lambda e, bank=bank, fc=fc, t=t, half=half: e.matmul(
                                bank[:, :], lhsT=hid[:, fc, t * 128:(t + 1) * 128], rhs=w2[:, fc, half * 512:(half + 1) * 512],
                                start=(fc == 0), stop=(fc == 31)), reads=[hid.d, w2.d], writes=[bank.d], inc=(fc == 31))
                        kb.op("dve", lambda e, bank=bank, tmp=tmp, bc=bc, half=half: e.tensor_tensor(
                            out=tmp[:, half * 512:(half + 1) * 512], in0=bank[:, :], in1=bc[:, half * 512:(half + 1) * 512],
                            op=ALU.mult), reads=[bank.d, bc.d], writes=[tmp.d])
                    xt = xts[t]
                    kb.op("pool", lambda e, o=o, xt=xt: e.tensor_tensor(out=o[:], in0=o[:], in1=xt[:], op=ALU.add),
                          reads=[o.d, xt.d], writes=[o.d])
                    kb.dma("sp", o.sch, g.y[t0 + t * 128:t0 + (t + 1) * 128, :], o[:], reads=[o.d])
        kb.barrier()


def host_inputs(inputs):
    f = np.float32
    xp = np.ascontiguousarray(inputs["x_prompt"], dtype=f)[0]
    xs = np.ascontiguousarray(inputs["x_sample"], dtype=f)
    cp = np.asarray(inputs["c_prompt"], dtype=f)
    cs = np.asarray(inputs["c_sample"], dtype=f)

    def col(v):
        v = np.asarray(v, dtype=f).reshape(-1, 128)
        return np.ascontiguousarray(v.T)
    pp = np.zeros((128, NP), f)
    pp[:, 0:8] = col(inputs["g_norm1"][0])
    pp[:, 8:16] = col(inputs["g_norm2"][0])
    pp[:, 16] = np.tile(np.asarray(inputs["q_norm_g"][0], f), 2)
    pp[:, 17] = np.tile(np.asarray(inputs["k_norm_g"][0], f), 2)
    pp[:, 18:22] = col(inputs["attn_beta"][0])
    pp[:, 22:37] = col(inputs["mu_prev"][0])
    pp[:, 37:52] = col(inputs["mu_next"][0])
    pp[:, 52:60] = col(np.asarray(inputs["w0"][0]).reshape(-1))
    pp[:, 60:68] = col(np.asarray(inputs["a0"][0]).reshape(-1))
    pp[:, 68:72] = col(inputs["k_k"][0])
    pp[:, 72:76] = col(inputs["k_a"][0])
    pp[:, 76:80] = col(np.asarray(inputs["r_k"][0]).reshape(-1))
    pp[:, 80:84] = col(inputs["ln_x_w"][0])
    pp[:, 84:88] = col(inputs["ln_x_b"][0])
    shared = dict(
        w_ada=np.ascontiguousarray(inputs["w_ada"][0], dtype=f), b_ada=np.ascontiguousarray(inputs["b_ada"], dtype=f).reshape(1, -1),
        pp=pp, w_in=np.ascontiguousarray(inputs["w_in"][0], dtype=f), w_out=np.ascontiguousarray(inputs["w_out"][0], dtype=f),
        w_ff1=np.ascontiguousarray(inputs["w_ff1"][0], dtype=f), w_ff2=np.ascontiguousarray(inputs["w_ff2"][0], dtype=f),
        w_up=np.ascontiguousarray(inputs["w_up"][0], dtype=f).reshape(128, 512),
        a_up=np.ascontiguousarray(inputs["a_up"][0], dtype=f).reshape(128, 512),
        g_up=np.ascontiguousarray(inputs["g_up"][0], dtype=f))
    maps = []
    SP = xp.shape[0]
    for i in range(NCORES):
        xa = np.concatenate([xp[i * SEG:(i + 1) * SEG]] + [xs[4 * i + j] for j in range(4)], axis=0)
        xh = np.zeros((2048, D), f)
        lo, hi = i * SEG - 1024, (i + 1) * SEG
        valid = np.zeros(4096, f)
        valid[1024:3072] = 1
        if lo >= 0:
            xh[0:1024] = xp[lo:lo + 1024]
            valid[0:1024] = 1
        if hi + 1024 <= SP:
            xh[1024:2048] = xp[hi:hi + 1024]
            valid[3072:4096] = 1
        c5 = np.concatenate([cp[0:1], cs[4 * i:4 * i + 4]], axis=0)
        c5T = np.ascontiguousarray(c5.reshape(NSEG, 8, 128).transpose(2, 1, 0))
        vt = np.ones((128, NSEG, 48, 2), f)
        for sg in range(NSEG):
            NKs, HKs = (4096, 1024) if sg == 0 else (2048, 0)
            vseg = valid if sg == 0 else np.ones(2048, f)
            ti = 0
            for dil in (1, 4, 16):
                Lq = SEG // dil
                for r in range(dil):
                    for j in range(Lq // 128):
                        base_sub = j * 128 + HKs // dil
                        p = np.arange(128)

                        def vv(sub):
                            tok = sub * dil + r
                            ok = (sub >= 0) & (tok < NKs)
                            return np.where(ok, vseg[np.clip(tok, 0, NKs - 1)], 0.0)
                        vt[:, sg, ti, 0] = vv(base_sub - 64 + p)
                        vt[:, sg, ti, 1] = vv(base_sub + 64 + p)
                        ti += 1
        nbv = np.zeros((128, 2), f)
        nbv[:, 0] = valid[1023]
        nbv[:, 1] = valid[3072]
        xslot = np.zeros((7, 2176, D), f)
        sfl = np.zeros((7, 6), f)
        for j in range(7):
            if j < i:
                sg_ = j
                xslot[j, 0:SEG] = xp[sg_ * SEG:(sg_ + 1) * SEG]
                if sg_ > 0:
                    xslot[j, SEG] = xp[sg_ * SEG - 1]
                    sfl[j, 2] = 1
                xslot[j, SEG + 1] = xp[(sg_ + 1) * SEG]
                sfl[j, 0] = 1
                sfl[j, 3] = 0 if j == 0 else 1
                sfl[j, 4] = 1 if j == i - 1 else 0
            else:
                sg_ = 7 - (j - i)
                xslot[j, 0:SEG] = xp[sg_ * SEG:(sg_ + 1) * SEG][::-1]
                if sg_ < 7:
                    xslot[j, SEG] = xp[(sg_ + 1) * SEG]
                    sfl[j, 2] = 1
                xslot[j, SEG + 1] = xp[sg_ * SEG - 1]
                sfl[j, 1] = 1
                sfl[j, 3] = 0 if j == i else 1
                sfl[j, 5] = 1 if j == 6 else 0
        sf = np.ascontiguousarray(np.broadcast_to(sfl.reshape(1, 42), (128, 42))).astype(f)
        m = dict(shared)
        m.update(xa=np.ascontiguousarray(xa), xh=xh, c5T=c5T, vt=np.ascontiguousarray(vt.reshape(128, -1)), nbv=nbv,
                 xslot=xslot, sf=sf)
        maps.append(m)
    return maps


_CACHE = {}


def kernel(**inputs):
    maps = host_inputs(inputs)
    if "nc" not in _CACHE:
        _CACHE["nc"] = build_program(phases=("p0", "p1", "p2a", "p2b", "p3", "p4"))
    res = run_bass_kernel_spmd(_CACHE["nc"], maps, core_ids=list(range(NCORES)))
    ys = [np.asarray(r["y"]) for r in res.results]
    yp = np.concatenate([y[0:SEG] for y in ys], axis=0)[None]
    ysm = np.stack([ys[i][SEG * (1 + j):SEG * (2 + j)] for i in range(NCORES) for j in range(4)], axis=0)
    return yp.astype(np.float32), ysm.astype(np.float32)
```

```python
import numpy as np
from contextlib import ExitStack
import concourse.bass as bass
import concourse.mybir as mybir
from concourse.bass_utils import run_bass_kernel_spmd

F32 = mybir.dt.float32
BF16 = mybir.dt.bfloat16
AF = mybir.ActivationFunctionType
ALU = mybir.AluOpType
AX = mybir.AxisListType

NCORES = 8
D = 1024
SEG = 2048
NSEG = 5
NTOK = NSEG * SEG
KS_TOT = 4096 + 4 * SEG
RS_TOT = 3072 + 4 * SEG
NP = 88


class T:
    __slots__ = ("name", "lw", "rd")

    def __init__(self, name):
        self.name = name
        self.lw = None
        self.rd = {}


class KB:
    ENGS = ("pe", "act", "dve", "pool", "sp")

    def __init__(self, nc):
        self.nc = nc
        self.q = {e: [] for e in self.ENGS}
        self.sems = {}
        self.cnt = {}
        self.seen = {e: {} for e in self.ENGS}
        self.pending = {e: False for e in self.ENGS}
        self._stack = []
        for e in self.ENGS:
            self.newsem("E_" + e)
        self.nchan = 0
        self.nops = 0

    def newsem(self, key):
        cm = self.nc.semaphore(key)
        s = cm.__enter__()
        self._stack.append(cm)
        self.sems[key] = s
        self.cnt[key] = 0
        return key

    def chan(self):
        self.nchan += 1
        return self.newsem("C%d" % self.nchan)

    def _waits(self, eng, reads, writes):
        need = {}
        seen = self.seen[eng]
        own = "E_" + eng

        def add(k, v):
            if k == own and eng == "pe":
                return
            if seen.get(k, 0) >= v:
                return
            if need.get(k, 0) < v:
                need[k] = v
        for t in reads:
            if t.lw is not None:
                add(*t.lw)
        for t in writes:
            if t.lw is not None:
                add(*t.lw)
            for k, v in t.rd.items():
                add(k, v)
        return need

    def _emit_waits(self, eng, need):
        for k, v in need.items():
            self.seen[eng][k] = v
            s = self.sems[k]
            self.q[eng].append(lambda e, s=s, v=v: e.wait_ge(s, v))

    def _mark(self, tok, reads, writes):
        for t in reads:
            if t.rd.get(tok[0], 0) < tok[1]:
                t.rd[tok[0]] = tok[1]
        for t in writes:
            t.lw = tok
            t.rd = {}

    def op(self, eng, fn, reads=(), writes=(), inc=True):
        self._emit_waits(eng, self._waits(eng, reads, writes))
        key = "E_" + eng
        self.nops += 1
        if inc:
            self.cnt[key] += 1
            tok = (key, self.cnt[key])
            s = self.sems[key]
            self.q[eng].append(lambda e, fn=fn, s=s: fn(e).then_inc(s, 1))
            self.pending[eng] = False
        else:
            tok = (key, self.cnt[key] + 1)
            self.q[eng].append(lambda e, fn=fn: fn(e))
            self.pending[eng] = True
        self._mark(tok, reads, writes)
        return tok

    def dma(self, eng, ch, out, in_, reads=(), writes=(), **kw):
        self._emit_waits(eng, self._waits(eng, reads, writes))
        self.cnt[ch] += 16
        tok = (ch, self.cnt[ch])
        s = self.sems[ch]
        self.nops += 1
        self.q[eng].append(lambda e, s=s, out=out, in_=in_, kw=kw:
                           e.dma_start(out=out, in_=in_, **kw).then_inc(s, 16))
        self._mark(tok, reads, writes)
        return tok

    def barrier(self):
        for e in self.ENGS:
            need = {}
            for k, v in self.cnt.items():
                if v > 0 and self.seen[e].get(k, 0) < v and not (k == "E_" + e):
                    need[k] = v
            self._emit_waits(e, need)

    def finish(self):
        for e in self.ENGS:
            if self.pending[e]:
                raise RuntimeError("engine %s has un-signalled trailing op" % e)
        need = {k: v for k, v in self.cnt.items() if v > 0 and k != "E_sp"}
        for k, v in need.items():
            s = self.sems[k]
            self.q["sp"].append(lambda e, s=s, v=v: e.wait_ge(s, v))
        with self.nc.Block() as block:
            def run(name):
                def f(e):
                    for c in self.q[name]:
                        c(e)
                return f
            block.tensor(run("pe"))
            block.scalar(run("act"))
            block.vector(run("dve"))
            block.gpsimd(run("pool"))
            block.sync(run("sp"))
        for cm in reversed(self._stack):
            cm.__exit__(None, None, None)
        self._stack = []


class Buf:
    kb = None

    def __init__(self, t, name):
        self.t = t
        self.d = T(name)
        self._ch = None
        self._sch = None

    @property
    def ch(self):
        if self._ch is None:
            self._ch = Buf.kb.chan()
        return self._ch

    @property
    def sch(self):
        if self._sch is None:
            self._sch = Buf.kb.chan()
        return self._sch

    def __getitem__(self, k):
        return self.t[k]


class Ring:
    def __init__(self, bufs):
        self.bufs = bufs
        self.i = 0

    def next(self):
        b = self.bufs[self.i % len(self.bufs)]
        self.i += 1
        return b


class Ctx:
    pass


def build_program(phases=("p0", "p1"), debug=False, p2a_args={}, p2b_args={}):
    nc = bass.Bass("TRN2", target_bir_lowering=False)
    g = Ctx()
    g.nc = nc
    g.debug = debug
    g.p2a_args = p2a_args
    g.p2b_args = p2b_args
    kb = g.kb = KB(nc)
    Buf.kb = kb

    def din(name, shape, dt=F32):
        return nc.dram_tensor(name, list(shape), dt, kind="ExternalInput").ap()

    def dscratch(name, shape, dt):
        return nc.dram_tensor(name, list(shape), dt, kind=("ExternalOutput" if debug else "Internal")).ap()

    g.xa = din("xa", [NTOK, D])
    g.xh = din("xh", [2048, D])
    g.c5T = din("c5T", [128, 8, NSEG])
    g.w_ada = din("w_ada", [D, 6 * D])
    g.b_ada = din("b_ada", [1, 6 * D])
    g.pp_d = din("pp", [128, NP])
    g.w_in = din("w_in", [D, 3456])
    g.w_out = din("w_out", [D, D])
    g.w_ff1 = din("w_ff1", [D, 4 * D])
    g.w_ff2 = din("w_ff2", [4 * D, D])
    g.w_up = din("w_up", [128, 512])
    g.a_up = din("a_up", [128, 512])
    g.g_up = din("g_up", [128, 512])
    g.vt_d = din("vt", [128, NSEG * 48 * 2])
    g.nbv_d = din("nbv", [128, 2])
    g.xslot = din("xslot", [7, 2176, D])
    g.sf_d = din("sf", [128, 7 * 6])
    g.y = nc.dram_tensor("y", [NTOK, D], F32, kind="ExternalOutput").ap()

    g.qT_s = dscratch("qT_s", [4, 128, NTOK], BF16)
    g.kT_s = dscratch("kT_s", [4, 128, KS_TOT], BF16)
    g.V_s = dscratch("V_s", [KS_TOT, 512], BF16)
    g.pr_s = dscratch("pr_s", [15, 128, RS_TOT], F32)
    g.mix_s = dscratch("mix_s", [8, 128, NTOK], BF16)
    g.prs_s = dscratch("prs_s", [10, 128, 7 * 2176], F32)
    if debug:
        g.dbg_mod = nc.dram_tensor("dbg_mod", [NSEG, 6 * D], F32, kind="ExternalOutput").ap()

    with ExitStack() as es:
        g.es = es

        def sb(name, shape, dt, scope=es):
            return Buf(scope.enter_context(nc.sbuf_tensor(name, list(shape), dt)), name)
        g.sb = sb
        g.banks = [Buf(es.enter_context(nc.psum_tensor("bank%d" % i, [128, 512], F32)), "bank%d" % i)
                   for i in range(8)]
        phase0(g)
        if "p1" in phases:
            kb.barrier()
            phase1(g)
        if "p2a" in phases:
            phase2a(g, **g.p2a_args)
        if "p2b" in phases:
            phase2b(g, **g.p2b_args)
        if "p3" in phases:
            phase3(g)
        if "p4" in phases:
            phase4(g)
        kb.barrier()
        kb.finish()
    return nc


def phase0(g):
    nc, kb, sb = g.nc, g.kb, g.sb
    c = g.c = Ctx()
    c.identf = sb("identf", [128, 128], F32)
    c.identb = sb("identb", [128, 128], BF16)
    c.pp = sb("pp_sb", [128, NP], F32)
    c.sel = sb("sel", [NSEG, NSEG, 128], F32)
    c.ones1 = sb("ones1", [1, 128], F32)
    c.gates = sb("gates", [NSEG, 2 * D], F32)
    c.modT = sb("modT", [128, 48, NSEG], F32)
    c.scale1 = sb("scale1", [128, 8, NSEG], F32)
    c.scale2 = sb("scale2", [128, 8, NSEG], F32)
    c.bdones = sb("bdones", [128, 128], BF16)
    c.bdonesf = sb("bdonesf", [128, 128], F32)
    c.gq8 = sb("gq8", [128, 1], F32)

    kb.dma("sp", c.pp.ch, c.pp[:], g.pp_d, writes=[c.pp.d])
    kb.op("pool", lambda e: e.memset(c.identf[:], 1.0), writes=[c.identf.d])
    kb.op("pool", lambda e: e.affine_select(out=c.identf[:], in_=c.identf[:], pattern=[[-1, 128]],
                                            compare_op=ALU.is_equal, fill=0.0, base=0, channel_multiplier=1),
          reads=[c.identf.d], writes=[c.identf.d])
    kb.op("dve", lambda e: e.tensor_copy(out=c.identb[:], in_=c.identf[:]), reads=[c.identf.d], writes=[c.identb.d])
    kb.op("pool", lambda e: e.memset(c.bdonesf[:], 1.0 / 64), writes=[c.bdonesf.d])
    kb.op("pool", lambda e: e.memset(c.bdonesf[0:64, 64:128], 0.0), reads=[c.bdonesf.d], writes=[c.bdonesf.d])
    kb.op("pool", lambda e: e.memset(c.bdonesf[64:128, 0:64], 0.0), reads=[c.bdonesf.d], writes=[c.bdonesf.d])
    kb.op("dve", lambda e: e.tensor_copy(out=c.bdones[:], in_=c.bdonesf[:]), reads=[c.bdonesf.d], writes=[c.bdones.d])
    kb.op("pool", lambda e: e.memset(c.sel[:], 1.0), writes=[c.sel.d])
    kb.op("pool", lambda e: e.affine_select(out=c.sel[:], in_=c.sel[:], pattern=[[-1, NSEG], [0, 128]],
                                            compare_op=ALU.is_equal, fill=0.0, base=0, channel_multiplier=1),
          reads=[c.sel.d], writes=[c.sel.d])
    kb.op("pool", lambda e: e.memset(c.ones1[:], 1.0), writes=[c.ones1.d])
    kb.op("act", lambda e: e.mul(out=c.gq8[:], in_=c.pp[:, 16:17], mul=0.125), reads=[c.pp.d], writes=[c.gq8.d])

    with ExitStack() as ls:
        c.mod5 = sb("mod5", [NSEG, 6 * D], F32, ls)
        silu = sb("siluT", [128, 8, NSEG], F32, ls)
        brow = sb("brow", [1, 6 * D], F32, ls)
        wring = Ring([sb("wada%d" % i, [128, 8, 512], F32, ls) for i in range(2)])
        kb.dma("sp", silu.ch, silu[:], g.c5T, writes=[silu.d])
        kb.dma("sp", brow.ch, brow[:], g.b_ada, writes=[brow.d])
        kb.op("act", lambda e: e.activation(out=silu[:], in_=silu[:], func=AF.Silu), reads=[silu.d], writes=[silu.d])
        wv = g.w_ada.rearrange("(kc p) n -> p kc n", p=128)
        for cb in range(12):
            w = wring.next()
            kb.dma("sp", w.ch, w[:], wv[:, :, cb * 512:(cb + 1) * 512], writes=[w.d])
            bank = g.banks[cb % 2]
            for kc in range(8):
                kb.op("pe", lambda e, w=w, kc=kc, bank=bank: e.matmul(
                    bank[0:NSEG, :], lhsT=silu[:, kc, :], rhs=w[:, kc, :], start=(kc == 0), stop=False),
                    reads=[silu.d, w.d], writes=[bank.d], inc=False)
            kb.op("pe", lambda e, cb=cb, bank=bank: e.matmul(
                bank[0:NSEG, :], lhsT=c.ones1[0:1, 0:NSEG], rhs=brow[0:1, cb * 512:(cb + 1) * 512],
                start=False, stop=True), reads=[c.ones1.d, brow.d], writes=[bank.d])
            kb.op("act", lambda e, cb=cb, bank=bank: e.copy(out=c.mod5[:, cb * 512:(cb + 1) * 512], in_=bank[0:NSEG, :]),
                  reads=[bank.d], writes=[c.mod5.d])
        if g.debug:
            kb.dma("sp", c.mod5.sch, g.dbg_mod, c.mod5[:], reads=[c.mod5.d])
        kb.op("act", lambda e: e.copy(out=c.gates[:, 0:D], in_=c.mod5[:, 2 * D:3 * D]), reads=[c.mod5.d], writes=[c.gates.d])
        kb.op("act", lambda e: e.copy(out=c.gates[:, D:2 * D], in_=c.mod5[:, 5 * D:6 * D]), reads=[c.mod5.d], writes=[c.gates.d])
        bank = g.banks[2]
        for cc in range(48):
            kb.op("pe", lambda e, cc=cc: e.transpose(out=bank[:, cc * NSEG:(cc + 1) * NSEG],
                                                     in_=c.mod5[:, cc * 128:(cc + 1) * 128],
                                                     identity=c.identf[0:NSEG, 0:NSEG]),
                  reads=[c.mod5.d, c.identf.d], writes=[bank.d], inc=(cc == 47))
        kb.op("dve", lambda e: e.tensor_copy(out=c.modT[:].rearrange("p a s -> p (a s)"), in_=bank[:, 0:48 * NSEG]),
              reads=[bank.d], writes=[c.modT.d])
        for (dst, goff, coff) in ((c.scale1, 0, 8), (c.scale2, 8, 32)):
            for kc in range(8):
                kb.op("dve", lambda e, dst=dst, goff=goff, coff=coff, kc=kc: e.tensor_scalar(
                    out=dst[:, kc, :], in0=c.modT[:, coff + kc, :], scalar1=1.0, scalar2=c.pp[:, goff + kc:goff + kc + 1],
                    op0=ALU.add, op1=ALU.mult), reads=[c.modT.d, c.pp.d], writes=[dst.d])
        kb.barrier()


def p1_blocks():
    bl = []
    bl.append(dict(src="xh", row=0, seg=0, ks=0, rs=None, qs=None))
    bl.append(dict(src="xh", row=512, seg=0, ks=512, rs=0, qs=None))
    for b in range(4):
        bl.append(dict(src="xa", row=512 * b, seg=0, ks=1024 + 512 * b, rs=512 + 512 * b, qs=512 * b))
    bl.append(dict(src="xh", row=1024, seg=0, ks=3072, rs=2560, qs=None))
    bl.append(dict(src="xh", row=1536, seg=0, ks=3584, rs=None, qs=None))
    for s in range(1, NSEG):
        for b in range(4):
            bl.append(dict(src="xa", row=SEG * s + 512 * b, seg=s, ks=4096 + (s - 1) * SEG + 512 * b,
                           rs=3072 + (s - 1) * SEG + 512 * b, qs=SEG * s + 512 * b))
    for j in range(7):
        for b in range(4):
            bl.append(dict(src="slot", slot=j, row=512 * b, seg=0, ks=None, rs=None, qs=None))
        bl.append(dict(src="slot", slot=j, row=2048, seg=0, ks=None, rs=None, qs=None, nt=1))
    return bl


def rms_to_hT(g, ls_bufs, src_ap, row0, scale, bias_col, seg, hT, nt=4):
    kb, c = g.kb, g.c
    xring, xsb, stat, tbanks, xch = ls_bufs
    xts = []
    for t in range(nt):
        xt = xring.next()
        xts.append(xt)
        kb.dma("sp", xt.ch, xt[:], src_ap[row0 + t * 128: row0 + (t + 1) * 128, :], writes=[xt.d])
        st = stat.next()
        junk = xsb[t]
        kb.op("act", lambda e, xt=xt, st=st, junk=junk: e.activation(out=junk[:], in_=xt[:], func=AF.Square,
                                                                      accum_out=st[:, 0:1]),
              reads=[xt.d], writes=[junk.d, st.d])
        kb.op("dve", lambda e, st=st: e.tensor_scalar(out=st[:, 1:2], in0=st[:, 0:1], scalar1=1.0 / D, scalar2=1e-6,
                                                      op0=ALU.mult, op1=ALU.add), reads=[st.d], writes=[st.d])
        kb.op("act", lambda e, st=st: e.activation(out=st[:, 1:2], in_=st[:, 1:2], func=AF.Sqrt), reads=[st.d], writes=[st.d])
        kb.op("dve", lambda e, st=st: e.reciprocal(out=st[:, 2:3], in_=st[:, 1:2]), reads=[st.d], writes=[st.d])
        kb.op("act", lambda e, xt=xt, st=st, junk=junk: e.activation(out=junk[:], in_=xt[:], func=AF.Copy, scale=st[:, 2:3]),
              reads=[xt.d, st.d], writes=[junk.d])
    for kp in range(4):
        bank = tbanks.next()
        bv = bank.t[:].bitcast(BF16)
        for j in range(2):
            kc = kp * 2 + j
            for t in range(nt):
                kb.op("pe", lambda e, bv=bv, j=j, t=t, kc=kc: e.transpose(
                    out=bv[:, j * 512 + t * 128: j * 512 + (t + 1) * 128], in_=xsb[t][:, kc * 128:(kc + 1) * 128],
                    identity=c.identb[:]), reads=[xsb[t].d, c.identb.d], writes=[bank.d],
                    inc=(j == 1 and t == nt - 1))
        for j in range(2):
            kc = kp * 2 + j
            eng = "act" if j == 0 else "dve"
            if eng == "act":
                kb.op("act", lambda e, bv=bv, j=j, kc=kc: e.activation(
                    out=hT[:, kc, 0:nt * 128], in_=bv[:, j * 512: j * 512 + nt * 128], func=AF.Identity,
                    scale=scale[:, kc, seg:seg + 1], bias=c.modT[:, bias_col + kc, seg:seg + 1]),
                    reads=[bank.d, scale.d, c.modT.d], writes=[hT.d])
            else:
                kb.op("dve", lambda e, bv=bv, j=j, kc=kc: e.tensor_scalar(
                    out=hT[:, kc, 0:nt * 128], in0=bv[:, j * 512: j * 512 + nt * 128],
                    scalar1=scale[:, kc, seg:seg + 1], scalar2=c.modT[:, bias_col + kc, seg:seg + 1],
                    op0=ALU.mult, op1=ALU.add), reads=[bank.d, scale.d, c.modT.d], writes=[hT.d])
    return xts


def phase1(g):
    nc, kb, sb, c = g.nc, g.kb, g.sb, g.c
    with ExitStack() as ls:
        win = sb("win", [128, 8, 3456], BF16, ls)
        wv = g.w_in.rearrange("(kc p) n -> p kc n", p=128)
        for kc in range(8):
            kb.dma("pool", win.ch, win[:, kc, :], wv[:, kc, :], writes=[win.d])
        xring = Ring([sb("xt%d" % i, [128, D], F32, ls) for i in range(3)])
        xch = [kb.chan() for _ in range(3)]
        xsbs = [[sb("xs%d_%d" % (r, t), [128, D], BF16, ls) for t in range(4)] for r in range(2)]
        stat = Ring([sb("st%d" % i, [128, 4], F32, ls) for i in range(4)])
        hTs = Ring([sb("hT%d" % i, [128, 8, 512], BF16, ls) for i in range(2)])
        tbanks = Ring([g.banks[0], g.banks[1]])
        pbanks = Ring([g.banks[2], g.banks[3], g.banks[4]])
        mbanks = Ring([g.banks[5], g.banks[6]])
        sq = Ring([sb("sq%d" % i, [128, 512], BF16, ls) for i in range(2)])
        rsd = Ring([sb("rsd%d" % i, [128, 512], F32, ls) for i in range(2)])
        obf = Ring([sb("obf%d" % i, [128, 512], BF16, ls) for i in range(3)])
        of32 = Ring([sb("of%d" % i, [128, 512], F32, ls) for i in range(3)])

        def store(dst, src_buf, src_ap):
            kb.dma("sp", src_buf.sch, dst, src_ap, reads=[src_buf.d])

        blocks = p1_blocks()
        hts = {}

        def rms_blk(bi):
            blk = blocks[bi]
            hT = hTs.next()
            nt = blk.get("nt", 4)
            if blk["src"] == "slot":
                src = g.xslot[blk["slot"]]
            else:
                src = g.xa if blk["src"] == "xa" else g.xh
            rms_to_hT(g, (xring, xsbs[bi % 2], stat, tbanks, xch), src, blk["row"], c.scale1, 0, blk["seg"], hT, nt=nt)
            hts[bi] = hT
        rms_blk(0)
        for bi, blk in enumerate(blocks):
            if bi + 1 < len(blocks):
                rms_blk(bi + 1)
            seg = blk["seg"]
            hT = hts.pop(bi)
            nt = blk.get("nt", 4)
            ncol = nt * 128
            chunks = []
            if blk["src"] == "slot":
                chunks += [("s", j, 12 + 4 + j) for j in range(8)] + [("s", 8, 24), ("s", 9, 25)]
            else:
                if blk["qs"] is not None:
                    chunks += [("q", hp, hp) for hp in range(4)]
                chunks += [("k", hp, 4 + hp) for hp in range(4)]
                if blk["rs"] is not None:
                    chunks += [("r", j, 12 + j) for j in range(15)]
            for (kind, idx, cc) in chunks:
                bank = pbanks.next()
                for kc in range(8):
                    kb.op("pe", lambda e, bank=bank, kc=kc, cc=cc, hT=hT, ncol=ncol: e.matmul(
                        bank[:, 0:ncol], lhsT=win[:, kc, cc * 128:(cc + 1) * 128], rhs=hT[:, kc, 0:ncol],
                        start=(kc == 0), stop=(kc == 7)), reads=[win.d, hT.d], writes=[bank.d], inc=(kc == 7))
                if kind in ("q", "k"):
                    s2, ms, rs_, ob = sq.next(), mbanks.next(), rsd.next(), obf.next()
                    kb.op("act", lambda e, s2=s2, bank=bank: e.activation(out=s2[:], in_=bank[:, :], func=AF.Square),
                          reads=[bank.d], writes=[s2.d])
                    kb.op("pe", lambda e, ms=ms, s2=s2: e.matmul(ms[:, :], lhsT=c.bdones[:], rhs=s2[:], start=True, stop=True),
                          reads=[c.bdones.d, s2.d], writes=[ms.d])
                    kb.op("dve", lambda e, rs_=rs_, ms=ms: e.tensor_scalar_add(out=rs_[:], in0=ms[:, :], scalar1=1e-6),
                          reads=[ms.d], writes=[rs_.d])
                    kb.op("act", lambda e, rs_=rs_: e.activation(out=rs_[:], in_=rs_[:], func=AF.Sqrt),
                          reads=[rs_.d], writes=[rs_.d])
                    kb.op("dve", lambda e, rs_=rs_: e.reciprocal(out=rs_[:], in_=rs_[:]), reads=[rs_.d], writes=[rs_.d])
                    gcol = c.gq8[:, 0:1] if kind == "q" else c.pp[:, 17:18]
                    gd = c.gq8.d if kind == "q" else c.pp.d
                    kb.op("dve", lambda e, ob=ob, bank=bank, rs_=rs_, gcol=gcol: e.scalar_tensor_tensor(
                        out=ob[:], in0=bank[:, :], scalar=gcol, in1=rs_[:], op0=ALU.mult, op1=ALU.mult),
                        reads=[bank.d, rs_.d, gd], writes=[ob.d])
                    if kind == "q":
                        store(g.qT_s[idx, :, blk["qs"]:blk["qs"] + 512], ob, ob[:])
                    else:
                        store(g.kT_s[idx, :, blk["ks"]:blk["ks"] + 512], ob, ob[:])
                elif kind == "s":
                    o = of32.next()
                    kb.op("act", lambda e, o=o, bank=bank, ncol=ncol: e.copy(out=o[:, 0:ncol], in_=bank[:, 0:ncol]), reads=[bank.d], writes=[o.d])
                    p0 = blk["slot"] * 2176 + blk["row"]
                    store(g.prs_s[idx, :, p0:p0 + ncol], o, o[:, 0:ncol])
                else:
                    o = of32.next()
                    kb.op("act", lambda e, o=o, bank=bank: e.copy(out=o[:], in_=bank[:, :]), reads=[bank.d], writes=[o.d])
                    store(g.pr_s[idx, :, blk["rs"]:blk["rs"] + 512], o, o[:])
            for t in range(4 if blk["src"] != "slot" else 0):
                bank = pbanks.next()
                ob = obf.next()
                for kc in range(8):
                    kb.op("pe", lambda e, bank=bank, kc=kc, hT=hT, t=t: e.matmul(
                        bank[:, :], lhsT=hT[:, kc, t * 128:(t + 1) * 128], rhs=win[:, kc, 1024:1536],
                        start=(kc == 0), stop=(kc == 7)), reads=[win.d, hT.d], writes=[bank.d], inc=(kc == 7))
                kb.op("dve", lambda e, ob=ob, bank=bank: e.tensor_copy(out=ob[:], in_=bank[:, :]), reads=[bank.d], writes=[ob.d])
                store(g.V_s[blk["ks"] + t * 128: blk["ks"] + (t + 1) * 128, :], ob, ob[:])
        kb.barrier()


def attn_geom(seg):
    if seg == 0:
        return 4096, 1024, 0
    return 2048, 0, 4096 + (seg - 1) * SEG


def phase2a(g, segs=range(NSEG), hps=range(4), level=9):
    nc, kb, sb, c = g.nc, g.kb, g.sb, g.c
    with ExitStack() as ls:
        onesb = sb("onesb", [128, 64], BF16, ls)
        vt = sb("vt_sb", [128, NSEG * 48 * 2], F32, ls)
        E = sb("Emask", [128, 12, 512], BF16, ls)
        ones128 = sb("ones128", [128, 128], BF16, ls)
        kb.op("pool", lambda e: e.memset(ones128[:], 1.0), writes=[ones128.d])
        R = sb("Rrel", [128, 2, 128], F32, ls)
        M = sb("Mband", [128, 2, 128], F32, ls)
        tmpE = Ring([sb("tmpE%d" % i, [128, 2, 128], F32, ls) for i in range(2)])
        kb.op("pool", lambda e: e.memset(onesb[:], 1.0), writes=[onesb.d])
        kb.dma("sp", vt.ch, vt[:], g.vt_d, writes=[vt.d])
        kb.op("pool", lambda e: e.iota(R[:, 0, :], pattern=[[-1, 128]], base=-64, channel_multiplier=1,
                                       allow_small_or_imprecise_dtypes=True), writes=[R.d])
        kb.op("pool", lambda e: e.iota(R[:, 1, :], pattern=[[-1, 128]], base=64, channel_multiplier=1,
                                       allow_small_or_imprecise_dtypes=True), reads=[R.d], writes=[R.d])
        kb.op("act", lambda e: e.activation(out=R[:], in_=R[:], func=AF.Abs),
              reads=[R.d], writes=[R.d])
        kb.op("dve", lambda e: e.tensor_single_scalar(out=M[:], in_=R[:], scalar=64.0, op=ALU.is_le),
              reads=[R.d], writes=[M.d])
        for h in range(8):
            slope = 2.0 ** (-(h + 1.0))
            for di, dil in enumerate((1, 4, 16)):
                tm = tmpE.next()
                kb.op("act", lambda e, tm=tm, sc=-slope * dil: e.activation(out=tm[:], in_=R[:], func=AF.Exp, scale=sc),
                      reads=[R.d], writes=[tm.d])
                ev = E[:, (h // 2) * 3 + di, :].rearrange("p (x s n) -> p x s n", x=2, s=2)[:, h % 2, :, :]
                kb.op("dve", lambda e, tm=tm, ev=ev: e.tensor_mul(out=ev, in0=tm[:], in1=M[:]),
                      reads=[tm.d, M.d], writes=[E.d])

        qAs = Ring([sb("qA%d" % i, [128, SEG], BF16, ls) for i in range(2)])
        qBs = Ring([sb("qB%d" % i, [128, SEG], BF16, ls) for i in range(2)])
        for qb_ in qAs.bufs:
            kb.op("pool", lambda e, qb_=qb_: e.memset(qb_[64:128, :], 0.0), writes=[qb_.d])
        for qb_ in qBs.bufs:
            kb.op("pool", lambda e, qb_=qb_: e.memset(qb_[0:64, :], 0.0), writes=[qb_.d])
        kTs = Ring([sb("kT%d" % i, [128, 6144], BF16, ls) for i in range(2)])
        for k_ in kTs.bufs:
            kb.op("pool", lambda e, k_=k_: e.memset(k_[:], 0.0), writes=[k_.d])
        Vds = [Ring([sb("Vd%d_%d" % (di, i), [128, 48, 128], BF16, ls) for i in range(2)]) for di in range(3)]
        for di in range(3):
            for v_ in Vds[di].bufs:
                kb.op("pool", lambda e, v_=v_: e.memset(v_[:], 0.0), writes=[v_.d])
        acc = sb("acc", [128, 2, SEG], F32, ls)
        exs = Ring([sb("ex%d" % i, [128, 512], F32, ls) for i in range(4)])
        pTs = Ring([sb("pT%d" % i, [128, 512], BF16, ls) for i in range(4)])
        osbs = Ring([sb("osb%d" % i, [128, SEG], BF16, ls) for i in range(2)])
        sbanks = Ring([g.banks[0], g.banks[1], g.banks[2]])
        obanks = Ring([g.banks[3], g.banks[4], g.banks[5]])

        def vgeom(seg, dil):
            NK, HK, ksb = attn_geom(seg)
            first = 0 if (HK // dil) % 128 == 64 else -64
            Lk = NK // dil
            nsets = -(-(Lk - first) // 128)
            return first, Lk, nsets

        def loads(seg, hp):
            NK, HK, ksb = attn_geom(seg)
            qA, qB, kT = qAs.next(), qBs.next(), kTs.next()
            kb.dma("sp", qA.ch, qA[0:64, :], g.qT_s[hp, 0:64, seg * SEG:(seg + 1) * SEG], writes=[qA.d])
            kb.dma("sp", qB.ch, qB[64:128, :], g.qT_s[hp, 64:128, seg * SEG:(seg + 1) * SEG], writes=[qB.d])
            kb.dma("sp", kT.ch, kT[:, 1024:1024 + NK], g.kT_s[hp, :, ksb:ksb + NK], writes=[kT.d])
            Vd = []
            for di, dil in enumerate((1, 4, 16)):
                V = Vds[di].next()
                Vd.append(V)
                first, Lk, nsets = vgeom(seg, dil)
                vsrc = g.V_s[ksb:ksb + NK, hp * 128:(hp + 1) * 128].rearrange("(s d) c -> d s c", d=dil)
                for r in range(dil):
                    m = 0
                    while m < nsets:
                        sub0 = first + 128 * m
                        if sub0 < 0:
                            kb.dma("sp", V.ch, V[64:128, r * nsets + m, :], vsrc[r, 0:64, :], writes=[V.d])
                            m += 1
                            continue
                        if sub0 + 128 > Lk:
                            kb.dma("sp", V.ch, V[0:64, r * nsets + m, :], vsrc[r, sub0:sub0 + 64, :], writes=[V.d])
                            m += 1
                            continue
                        cnt = min(8, (Lk - sub0) // 128)
                        kb.dma("sp", V.ch, V[:, r * nsets + m: r * nsets + m + cnt, :],
                               vsrc[r, sub0:sub0 + 128 * cnt, :].rearrange("(m p) c -> p m c", p=128), writes=[V.d])
                        m += cnt
            return qA, qB, kT, Vd

        def tile_gen(seg, hp, di, dil, r, j, ti, first, nsets, qvs, qds, kv, accv, V, Ev, kT, HK):
            sub0 = HK // dil + 128 * j - 64
            jb = (sub0 - first) // 128
            sbank = sbanks.next()
            Sv = sbank[:, :].rearrange("p (x s n) -> p x s n", x=2, s=2)
            for X in range(2):
                qx = qvs[X][:, 128 * j:128 * j + 128, r]
                for h2 in range(2):
                    kh = kv[:, sub0 + 64 + 128 * h2: sub0 + 64 + 128 * h2 + 128, r]
                    kb.op("pe", lambda e, Sv=Sv, X=X, kh=kh, qx=qx, h2=h2: e.matmul(
                        Sv[:, X, h2, :], lhsT=kh, rhs=qx, start=True, stop=True),
                        reads=[kT.d, qds[X]], writes=[sbank.d], inc=(X == 1 and h2 == 1))
            ex = exs.next()
            kb.op("act", lambda e, ex=ex, sbank=sbank: e.activation(out=ex[:], in_=sbank[:, :], func=AF.Exp),
                  reads=[sbank.d], writes=[ex.d])
            yield
            pT = pTs.next()
            exv = ex[:].rearrange("p (x s n) -> p x s n", x=2, s=2)
            pTv = pT[:].rearrange("p (x s n) -> p x s n", x=2, s=2)
            for slot in range(2):
                col = (seg * 48 + ti) * 2 + slot
                kb.op("dve", lambda e, pTv=pTv, exv=exv, slot=slot, col=col, Ev=Ev: e.scalar_tensor_tensor(
                    out=pTv[:, :, slot, :], in0=exv[:, :, slot, :], scalar=vt[:, col:col + 1],
                    in1=Ev[:, :, slot, :], op0=ALU.mult, op1=ALU.mult),
                    reads=[ex.d, vt.d, E.d], writes=[pT.d])
            obank = obanks.next()
            Ov = obank[:, :].rearrange("p (x a n) -> p x a n", x=2, a=2)
            for X in range(2):
                for half in range(2):
                    for h2 in range(2):
                        if half == 0:
                            lhsT = V[:, r * nsets + jb + h2, :]
                            rd = [V.d, pT.d]
                        else:
                            lhsT = ones128[:]
                            rd = [ones128.d, pT.d]
                        kb.op("pe", lambda e, Ov=Ov, X=X, half=half, lhsT=lhsT, pTv=pTv, h2=h2: e.matmul(
                            Ov[:, X, half, :], lhsT=lhsT, rhs=pTv[:, X, h2, :], start=(h2 == 0), stop=(h2 == 1)),
                            reads=rd, writes=[obank.d], inc=(X == 1 and half == 1 and h2 == 1))
            yield
            for X in range(2):
                dst = accv[64 * X:64 * X + 64, :, 128 * j:128 * j + 128, r]
                src = Ov[64 * X:64 * X + 64, X, :, :]
                if di == 0:
                    kb.op("act", lambda e, dst=dst, src=src: e.copy(out=dst, in_=src), reads=[obank.d], writes=[acc.d])
                else:
                    kb.op("dve", lambda e, dst=dst, src=src: e.tensor_tensor(out=dst, in0=dst, in1=src, op=ALU.add),
                          reads=[obank.d, acc.d], writes=[acc.d])

        pend = []
        work = [(s, h) for s in segs for h in hps]
        if level < 1:
            work = []
        else:
            nxt = loads(*work[0])
        for wi, (seg, hp) in enumerate(work):
            NK, HK, ksb = attn_geom(seg)
            qA, qB, kT, Vd = nxt
            if wi + 1 < len(work):
                nxt = loads(*work[wi + 1])
            ti = 0
            for di, dil in enumerate((1, 4, 16) if level >= 2 else ()):
                Lq = SEG // dil
                first, Lk, nsets = vgeom(seg, dil)
                qvs = [q_[:].rearrange("p (m d) -> p m d", d=dil) for q_ in (qA, qB)]
                qds = [qA.d, qB.d]
                kv = kT[:, 1024 - 64 * dil:1024 + NK + 64 * dil].rearrange("p (m d) -> p m d", d=dil)
                accv = acc[:].rearrange("p a (m d) -> p a m d", d=dil)
                V = Vd[di]
                Ev = E[:, hp * 3 + di, :].rearrange("p (x s n) -> p x s n", x=2, s=2)
                for r in range(dil):
                    for j in range(Lq // 128):
                        g_ = tile_gen(seg, hp, di, dil, r, j, ti, first, nsets, qvs, qds, kv, accv, V, Ev, kT, HK)
                        next(g_)
                        pend.append(g_)
                        if len(pend) >= 2:
                            next(pend[-2])
                        if len(pend) >= 3:
                            for _ in pend.pop(0):
                                pass
                        ti += 1
            while pend:
                for _ in pend.pop(0):
                    pass
            if level < 6:
                continue
            osb = osbs.next()
            kb.op("dve", lambda e: e.reciprocal(out=acc[:, 1, :], in_=acc[:, 1, :]), reads=[acc.d], writes=[acc.d])
            kb.op("dve", lambda e, osb=osb, hp=hp: e.scalar_tensor_tensor(
                out=osb[:], in0=acc[:, 0, :], scalar=c.pp[:, 18 + hp:19 + hp], in1=acc[:, 1, :], op0=ALU.mult, op1=ALU.mult),
                reads=[acc.d, c.pp.d], writes=[osb.d])
            kb.dma("sp", osb.sch, g.mix_s[hp, :, seg * SEG:(seg + 1) * SEG], osb[:], reads=[osb.d])
        kb.barrier()


KAPPA = 0.6065306597126334


def rs_base(seg):
    return 512 if seg == 0 else 3072 + (seg - 1) * SEG


def bcast_last(ap2, n):
    return bass.AP(ap2.tensor, ap2.offset, [list(ap2.ap[0]), list(ap2.ap[1]), [0, n]])


def phase2b(g, segs=range(NSEG), hps=range(4), dirs=(0, 1), do_slots=True):
    nc, kb, sb, c = g.nc, g.kb, g.sb, g.c
    NCH = SEG // 64
    with ExitStack() as ls:
        ptab = sb("ptab", [128, 20], F32, ls)
        kb.op("dve", lambda e: e.tensor_tensor(out=ptab[:, 0:15], in0=c.pp[:, 22:37], in1=c.pp[:, 37:52], op=ALU.add),
              reads=[c.pp.d], writes=[ptab.d])
        kb.op("dve", lambda e: e.tensor_scalar(out=ptab[:, 0:15], in0=ptab[:, 0:15], scalar1=-1.0, scalar2=1.0,
                                               op0=ALU.mult, op1=ALU.add), reads=[ptab.d], writes=[ptab.d])
        kb.op("dve", lambda e: e.tensor_scalar(out=ptab[:, 15:19], in0=c.pp[:, 72:76], scalar1=-1.0, scalar2=1.0,
                                               op0=ALU.mult, op1=ALU.add), reads=[c.pp.d, ptab.d], writes=[ptab.d])
        nbv = sb("nbv_sb", [128, 2], F32, ls)
        kb.dma("sp", nbv.ch, nbv[:], g.nbv_d, writes=[nbv.d])
        lw = {}
        for nm, src in (("wup", g.w_up), ("aup", g.a_up)):
            for d_, (r0, r1) in enumerate(((0, 64), (64, 128))):
                t_ = sb("%s%d" % (nm, d_), [128, 512], BF16, ls)
                kb.op("pool", lambda e, t_=t_: e.memset(t_[:], 0.0), writes=[t_.d])
                kb.dma("pool", t_.ch, t_[r0:r1, :], src[r0:r1, :], writes=[t_.d])
                lw[(nm, d_)] = t_
        gup = sb("gup", [128, 512], BF16, ls)
        kb.dma("pool", gup.ch, gup[:], g.g_up, writes=[gup.d])
        MA = [sb("MA%d" % d_, [128, 320], BF16, ls) for d_ in range(2)]
        MB = [sb("MB%d" % d_, [128, 192], BF16, ls) for d_ in range(2)]
        mscope = ExitStack()
        mf = sb("maskf", [128, 4, 128], F32, mscope)
        kb.op("pool", lambda e: e.memset(mf[:], 1.0), writes=[mf.d])
        for i, (st, cm, op) in enumerate(((1, -1, ALU.is_gt), (1, -1, ALU.is_ge), (-1, 1, ALU.is_gt), (-1, 1, ALU.is_ge))):
            kb.op("pool", lambda e, i=i, st=st, cm=cm, op=op: e.affine_select(
                out=mf[:, i, :], in_=mf[:, i, :], pattern=[[st, 128]], compare_op=op, fill=0.0, base=0,
                channel_multiplier=cm), reads=[mf.d], writes=[mf.d])
        for d_ in range(2):
            s_i, i_i, t_i = (0, 1, 2) if d_ == 0 else (2, 3, 0)
            kb.op("dve", lambda e, d_=d_, s_i=s_i: e.tensor_copy(out=MA[d_][:, 0:128], in_=mf[:, s_i, :]), reads=[mf.d], writes=[MA[d_].d])
            for X in range(2):
                kb.op("act", lambda e, d_=d_, i_i=i_i, X=X: e.mul(out=MA[d_][64 * X:64 * X + 64, 128:192], in_=mf[64 * X:64 * X + 64, i_i, 64 * X:64 * X + 64], mul=-1.0),
                      reads=[mf.d, MA[d_].d], writes=[MA[d_].d])
                kb.op("act", lambda e, d_=d_, i_i=i_i, X=X: e.copy(out=MB[d_][64 * X:64 * X + 64, 128:192], in_=mf[64 * X:64 * X + 64, i_i, 64 * X:64 * X + 64]),
                      reads=[mf.d, MB[d_].d], writes=[MB[d_].d])
            kb.op("dve", lambda e, d_=d_, t_i=t_i: e.tensor_copy(out=MA[d_][:, 192:320], in_=mf[:, t_i, :]), reads=[mf.d, MA[d_].d], writes=[MA[d_].d])
            kb.op("dve", lambda e, d_=d_, s_i=s_i: e.tensor_copy(out=MB[d_][:, 0:128], in_=mf[:, s_i, :]), reads=[mf.d, MB[d_].d], writes=[MB[d_].d])
        kb.barrier()
        mscope.close()
        B = Ctx()
        B.n = 0

        def open_prep():
            B.n += 1
            B.prep = ExitStack()
            B.P = sb("Pld_%d" % B.n, [128, SEG + 2], F32, B.prep)
            if not hasattr(B, "Pch"):
                B.Pch = kb.chan()
            B.P._ch = B.Pch
            B.Ss = Ring([sb("Sft%d_%d" % (i, B.n), [128, SEG], F32, B.prep) for i in range(2)])
            B.sgt = sb("sgt_%d" % B.n, [128, SEG], F32, B.prep)
            B.a_d = [sb("a_d%d_%d" % (i, B.n), [128, SEG], BF16, B.prep) for i in range(2)]

        def close_prep():
            kb.barrier()
            B.prep.close()
        twd = sb("twd", [128, SEG], BF16, ls)
        adb = sb("adb", [128, SEG], BF16, ls)
        sgd = sb("sgd", [128, SEG], BF16, ls)
        rb = sb("r_b", [128, SEG], BF16, ls)
        kkb = sb("kk_b", [128, SEG], BF16, ls)
        vb = sb("v_b", [128, SEG], BF16, ls)
        kd = [sb("kd%d" % i, [128, SEG], BF16, ls) for i in range(2)]
        bd = [sb("bd%d" % i, [128, SEG], BF16, ls) for i in range(2)]
        cumz = [sb("cumz%d" % i, [128, SEG + 1], F32, ls) for i in range(2)]
        gb = sb("g_b", [128, SEG], BF16, ls)
        bon = sb("bonus", [128, SEG], BF16, ls)
        yacc = sb("yacc", [128, SEG], F32, ls)
        vTbd = sb("vTbd", [128, NCH, 2, 64], BF16, ls)
        Vbd = sb("Vbd", [128, NCH, 128], BF16, ls)
        gC = [sb("gC%d" % i, [128, NCH], F32, ls) for i in range(2)]
        kb.op("pool", lambda e: e.memset(vTbd[:], 0.0), writes=[vTbd.d])
        for d_ in range(2):
            kb.op("pool", lambda e, d_=d_: e.memset(cumz[d_][:, 0:1], 0.0), writes=[cumz[d_].d])
        Dbuf = [Ring([sb("D%d_%d" % (j, i), [128, 512], F32, ls) for i in range(1)]) for j in range(3)]
        Ebuf = [Ring([sb("E%d_%d" % (j, i), [128, 512], BF16, ls) for i in range(1)]) for j in range(5)]
        WA, DEPTH = 5, 7

        class Chain:
            pass

        def open_scan(specs):
            B.n += 1
            B.scan = ExitStack()
            chains = []
            for spec in specs:
                d_, cdir, state_only = spec[0], spec[1], spec[2]
                ch = Chain()
                ch.d_, ch.cdir, ch.state_only = d_, cdir, state_only
                bufs = spec[3] if len(spec) > 3 else {}
                ch.kk = bufs.get("kk", kkb)
                ch.kd = bufs.get("kd", kd[d_])
                ch.bd = bufs.get("bd", bd[d_])
                ch.cumz = bufs.get("cumz", cumz[d_])
                ch.gC = bufs.get("gC", gC[d_])
                ch.Vbd = bufs.get("Vbd", Vbd)
                ch.r = rb
                tg = "%d_%d" % (d_, B.n)
                ch.RKs = Ring([sb("RK%d_%s" % (i, tg), [128, 8, 192], BF16, B.scan) for i in range(2)])
                ch.KH = sb("KH_%s" % tg, [128, 8, 128], BF16, B.scan)
                ch.BH = sb("BH_%s" % tg, [128, 8, 128], BF16, B.scan)
                ch.KG = sb("KG_%s" % tg, [128, 8, 128], BF16, B.scan)
                ch.BG = sb("BG_%s" % tg, [128, 8, 128], BF16, B.scan)
                for b_ in ch.RKs.bufs + [ch.KH, ch.BH, ch.KG, ch.BG]:
                    kb.op("pool", lambda e, b_=b_: e.memset(b_[:], 0.0), writes=[b_.d])
                ch.stash = []
                for i in range(DEPTH):
                    st = Ctx()
                    st.NA = sb("NA%d_%s" % (i, tg), [128, 320], BF16, B.scan)
                    st.BB = sb("BB%d_%s" % (i, tg), [128, 192], BF16, B.scan)
                    st.W = sb("W%d_%s" % (i, tg), [128, 128], BF16, B.scan)
                    st.KBt = sb("KBt%d_%s" % (i, tg), [128, 256], BF16, B.scan)
                    ch.stash.append(st)
                ch.tmps = []
                for i in range(WA):
                    tp_ = Ctx()
                    tp_.XWT = [sb("XWT%d%d_%s" % (i, k_, tg), [128, 384], BF16, B.scan) for k_ in range(2)]
                    ch.tmps.append(tp_)
                ch.RHSs = Ring([sb("RHS%d_%s" % (i, tg), [128, 128], BF16, B.scan) for i in range(2)])
                ch.Us = Ring([sb("U%d_%s" % (i, tg), [128, 128], BF16, B.scan) for i in range(2)])
                ch.M0f = sb("M0f_%s" % tg, [128, 128], F32, B.scan)
                ch.M0b = sb("M0b_%s" % tg, [128, 128], BF16, B.scan)
                chains.append(ch)
            return chains

        def close_scan():
            kb.barrier()
            B.scan.close()

        pst = Ring([sb("pst%d" % i, [128, 512], F32, ls) for i in range(2)])
        osbs = Ring([sb("orw%d" % i, [128, 512], BF16, ls) for i in range(1)])
        banks = Ring([g.banks[i] for i in range(8)])

        def load_shift(seg, chunk, out_S):
            base = rs_base(seg)
            kb.dma("sp", B.P.ch, B.P[:, 1:SEG + 1], g.pr_s[chunk, :, base:base + SEG], writes=[B.P.d])
            if seg == 0:
                kb.dma("sp", B.P.ch, B.P[:, 0:1], g.pr_s[chunk, :, base - 1:base], writes=[B.P.d], allow_slow_non_contiguous=True)
                kb.dma("sp", B.P.ch, B.P[:, SEG + 1:SEG + 2], g.pr_s[chunk, :, base + SEG:base + SEG + 1], writes=[B.P.d], allow_slow_non_contiguous=True)
                kb.op("dve", lambda e: e.tensor_scalar_mul(out=B.P[:, 0:1], in0=B.P[:, 0:1], scalar1=nbv[:, 0:1]),
                      reads=[B.P.d, nbv.d], writes=[B.P.d])
                kb.op("dve", lambda e: e.tensor_scalar_mul(out=B.P[:, SEG + 1:SEG + 2], in0=B.P[:, SEG + 1:SEG + 2], scalar1=nbv[:, 1:2]),
                      reads=[B.P.d, nbv.d], writes=[B.P.d])
            else:
                kb.op("pool", lambda e: e.memset(B.P[:, 0:1], 0.0), reads=[B.P.d], writes=[B.P.d])
                kb.op("pool", lambda e: e.memset(B.P[:, SEG + 1:SEG + 2], 0.0), reads=[B.P.d], writes=[B.P.d])
            kb.op("act", lambda e: e.activation(out=out_S[:], in_=B.P[:, 1:SEG + 1], func=AF.Copy, scale=ptab[:, chunk:chunk + 1]),
                  reads=[B.P.d, ptab.d], writes=[out_S.d])
            kb.op("dve", lambda e: e.scalar_tensor_tensor(out=out_S[:], in0=B.P[:, 0:SEG], scalar=c.pp[:, 22 + chunk:23 + chunk],
                                                          in1=out_S[:], op0=ALU.mult, op1=ALU.add),
                  reads=[B.P.d, c.pp.d, out_S.d], writes=[out_S.d])
            kb.op("dve", lambda e: e.scalar_tensor_tensor(out=out_S[:], in0=B.P[:, 2:SEG + 2], scalar=c.pp[:, 37 + chunk:38 + chunk],
                                                          in1=out_S[:], op0=ALU.mult, op1=ALU.add),
                  reads=[B.P.d, c.pp.d, out_S.d], writes=[out_S.d])

        def bdsum(dst_fn, src_buf, src_ap_fn, f32=False):
            for b4 in range(4):
                bank = banks.next()
                lhs = c.bdonesf if f32 else c.bdones
                kb.op("pe", lambda e, bank=bank, b4=b4, lhs=lhs: e.matmul(bank[:, :], lhsT=lhs[:], rhs=src_ap_fn(b4), start=True, stop=True),
                      reads=[lhs.d, src_buf.d], writes=[bank.d])
                dst_fn(b4, bank)

        def decay_and_a(d_, wup_t, aup_t, w0col, a0col, hp):
            for b4 in range(4):
                sl = slice(b4 * 512, (b4 + 1) * 512)
                bank = banks.next()
                kb.op("pe", lambda e, bank=bank, sl=sl: e.matmul(
                    bank[:, :], lhsT=wup_t[:, hp * 128:(hp + 1) * 128], rhs=twd[:, sl], start=True, stop=True),
                    reads=[wup_t.d, twd.d], writes=[bank.d])
                kb.op("act", lambda e, bank=bank, sl=sl: e.activation(out=B.sgt[:, sl], in_=bank[:, :], func=AF.Sigmoid, bias=w0col[0]),
                      reads=[bank.d, w0col[1]], writes=[B.sgt.d])
                bank = banks.next()
                kb.op("pe", lambda e, bank=bank, sl=sl: e.matmul(
                    bank[:, :], lhsT=aup_t[:, hp * 128:(hp + 1) * 128], rhs=adb[:, sl], start=True, stop=True),
                    reads=[aup_t.d, adb.d], writes=[bank.d])
                kb.op("act", lambda e, bank=bank, sl=sl: e.activation(out=B.a_d[d_][:, sl], in_=bank[:, :], func=AF.Sigmoid, bias=a0col[0]),
                      reads=[bank.d, a0col[1]], writes=[B.a_d[d_].d])
            kb.op("dve", lambda e: e.tensor_tensor_scan(out=cumz[d_][:, 1:SEG + 1], data0=B.sgt[:], data1=B.sgt[:],
                                                        initial=0.0, op0=ALU.add, op1=ALU.bypass),
                  reads=[B.sgt.d], writes=[cumz[d_].d])
            kb.op("dve", lambda e: e.tensor_tensor(out=gC[d_][:], in0=cumz[d_][:, 64:SEG + 1:64], in1=cumz[d_][:, 0:SEG:64],
                                                   op=ALU.subtract), reads=[cumz[d_].d], writes=[gC[d_].d])
            kb.op("act", lambda e: e.activation(out=gC[d_][:], in_=gC[d_][:], func=AF.Exp, scale=-KAPPA),
                  reads=[gC[d_].d], writes=[gC[d_].d])

        def prep_v(loader, vb=vb, Vbd=Vbd):
            S = B.Ss.next()
            loader(S)
            kb.op("act", lambda e, S=S: e.copy(out=vb[:], in_=S[:]), reads=[S.d], writes=[vb.d])
            for X in range(2):
                kb.op("pool", lambda e, X=X: e.tensor_copy(
                    out=vTbd[64 * X:64 * X + 64, :, X, :], in_=vb[64 * X:64 * X + 64, :].rearrange("p (c t) -> p c t", t=64)),
                    reads=[vb.d], writes=[vTbd.d])
            for c8 in range(NCH // 8):
                bank = banks.next()
                bv = bank.t[:].bitcast(BF16)
                for j in range(8):
                    ci = c8 * 8 + j
                    kb.op("pe", lambda e, bv=bv, j=j, ci=ci: e.transpose(
                        out=bv[:, j * 128:(j + 1) * 128], in_=vTbd[:, ci, :, :].rearrange("p x t -> p (x t)"), identity=c.identb[:]),
                        reads=[vTbd.d, c.identb.d], writes=[bank.d], inc=(j == 7))
                kb.op("act", lambda e, bv=bv, c8=c8: e.copy(out=Vbd[:, c8 * 8:(c8 + 1) * 8, :].rearrange("p c n -> p (c n)"), in_=bv[:, :]),
                      reads=[bank.d], writes=[Vbd.d])

        def prep_k(loader, hp, dl, kkb=kkb):
            Sk = B.Ss.next()
            loader(Sk)
            for d_ in dl:
                kb.op("dve", lambda e, d_=d_: e.tensor_scalar(
                    out=kd[d_][:], in0=B.a_d[d_][:], scalar1=c.pp[:, 72 + hp:73 + hp], scalar2=ptab[:, 15 + hp:16 + hp],
                    op0=ALU.mult, op1=ALU.add), reads=[B.a_d[d_].d, c.pp.d, ptab.d], writes=[kd[d_].d])
                kb.op("pool", lambda e, d_=d_: e.tensor_tensor(out=kd[d_][:], in0=kd[d_][:], in1=Sk[:], op=ALU.mult),
                      reads=[kd[d_].d, Sk.d], writes=[kd[d_].d])
            kb.op("act", lambda e: e.activation(out=Sk[:], in_=Sk[:], func=AF.Copy, scale=c.pp[:, 68 + hp:69 + hp]),
                  reads=[Sk.d, c.pp.d], writes=[Sk.d])
            S2 = B.Ss.next()
            kb.op("act", lambda e: e.activation(out=S2[:], in_=Sk[:], func=AF.Square), reads=[Sk.d], writes=[S2.d])
            kb.op("dve", lambda e: e.tensor_copy(out=kkb[:], in_=S2[:]), reads=[S2.d], writes=[kkb.d])

            def kk_dst(b4, bank):
                sl = slice(b4 * 512, (b4 + 1) * 512)
                kb.op("act", lambda e: e.activation(out=S2[:, sl], in_=bank[:, :], func=AF.Sqrt, scale=64.0),
                      reads=[bank.d], writes=[S2.d])
                kb.op("dve", lambda e: e.tensor_scalar_max(out=S2[:, sl], in0=S2[:, sl], scalar1=1e-12), reads=[S2.d], writes=[S2.d])
                kb.op("dve", lambda e: e.reciprocal(out=S2[:, sl], in_=S2[:, sl]), reads=[S2.d], writes=[S2.d])
            bdsum(kk_dst, kkb, lambda b4: kkb[:, b4 * 512:(b4 + 1) * 512])
            kb.op("dve", lambda e: e.tensor_tensor(out=kkb[:], in0=Sk[:], in1=S2[:], op=ALU.mult),
                  reads=[Sk.d, S2.d], writes=[kkb.d])
            for d_ in dl:
                kb.op("pool", lambda e, d_=d_: e.tensor_tensor(out=bd[d_][:], in0=kkb[:], in1=B.a_d[d_][:], op=ALU.mult),
                      reads=[kkb.d, B.a_d[d_].d], writes=[bd[d_].d])
            return S2

        yts = [T("yacc%d" % i) for i in range(NCH)]

        class View:
            def __init__(self, ap, name):
                self.t = ap
                self.d = T(name)

            def __getitem__(self, k):
                return self.t[k]
        Vbd2 = View(yacc.t[:].bitcast(BF16).rearrange("p (c n) -> p c n", n=128), "Vbd2")
        ywritten = [False] * NCH

        def block_prep(ch, b4):
            d_, cdir, state_only = ch.d_, ch.cdir, ch.state_only
            b0 = b4 * 512
            czv = ch.cumz
            cur = czv[:, 1 + b0:1 + b0 + 512].rearrange("p (c t) -> p c t", t=64)
            prv = czv[:, b0:b0 + 512].rearrange("p (c t) -> p c t", t=64)
            bs = bcast_last(czv[:, b0:b0 + 512:64], 64)
            be = bcast_last(czv[:, b0 + 64:b0 + 512 + 1:64], 64)
            Dl = [r_.next() for r_ in Dbuf]
            El = [r_.next() for r_ in Ebuf]
            if cdir == 0:
                dspec = ((cur, bs), (prv, bs), (cur, be))
                espec = ((0, -KAPPA), (1, -KAPPA), (0, KAPPA), (2, KAPPA))
            else:
                dspec = ((prv, be), (cur, be), (prv, bs))
                espec = ((0, KAPPA), (1, KAPPA), (0, -KAPPA), (2, -KAPPA))
            for j, (a_, b_) in enumerate(dspec):
                kb.op("dve", lambda e, j=j, a_=a_, b_=b_, Dl=Dl: e.tensor_tensor(
                    out=Dl[j][:].rearrange("p (c t) -> p c t", t=64), in0=a_, in1=b_, op=ALU.subtract),
                    reads=[czv.d], writes=[Dl[j].d])
            for j, (di_, sc) in enumerate(espec):
                if state_only and j == 0:
                    continue
                kb.op("act", lambda e, j=j, di_=di_, sc=sc, Dl=Dl, El=El: e.activation(out=El[j][:], in_=Dl[di_][:], func=AF.Exp, scale=sc),
                      reads=[Dl[di_].d], writes=[El[j].d])
            RK = ch.RKs.next()
            if not state_only:
                kb.op("pool", lambda e, RK=RK, El=El: e.tensor_tensor(
                    out=RK[:, :, 128:192], in0=ch.r[:, b0:b0 + 512].rearrange("p (c t) -> p c t", t=64),
                    in1=El[0][:].rearrange("p (c t) -> p c t", t=64), op=ALU.mult), reads=[ch.r.d, El[0].d], writes=[RK.d])
            kb.op("act", lambda e, El=El: e.mul(out=El[4][:], in_=El[3][:], mul=-1.0), reads=[El[3].d], writes=[El[4].d])
            prods = ((RK, ch.kk, 1), (ch.KH, ch.kd, 2), (ch.BH, ch.bd, 2), (ch.KG, ch.kd, 3), (ch.BG, ch.bd, 4))
            pi = 0
            for (dstb, srcb, ei) in prods:
                for X in range(2):
                    o_ap = dstb[64 * X:64 * X + 64, :, 64 * X:64 * X + 64]
                    i0 = srcb[64 * X:64 * X + 64, b0:b0 + 512].rearrange("p (c t) -> p c t", t=64)
                    i1 = El[ei][64 * X:64 * X + 64, :].rearrange("p (c t) -> p c t", t=64)
                    eng = "pool" if pi % 2 == 0 else "dve"
                    pi += 1
                    kb.op(eng, lambda e, o_ap=o_ap, i0=i0, i1=i1: e.tensor_tensor(out=o_ap, in0=i0, in1=i1, op=ALU.mult),
                          reads=[srcb.d, El[ei].d], writes=[dstb.d])
            return RK

        def blk2(ap_fn, w):
            base = ap_fn[:, 0:128]
            return bass.AP(base.tensor, base.offset, [list(base.ap[0]), [256, 2], [1, 128]])

        def gen_A(ch, ci):
            cg = ch.order[ci]
            b4, cl = divmod(cg, 8)
            RK = ch.blkRK[b4]
            KH, BH, KG, BG = ch.KH, ch.BH, ch.KG, ch.BG
            st, tp_ = ch.stash[ci % DEPTH], ch.tmps[ci % WA]
            MAm, MBm = MA[ch.cdir], MB[ch.cdir]
            rk = RK[:, cl, :]
            kkbd = RK[:, cl, 0:128]
            NA, BB = st.NA, st.BB
            b1 = banks.next()
            kb.op("pe", lambda e: e.matmul(b1[:, 0:192], lhsT=BH[:, cl, :], rhs=rk, start=True, stop=True), reads=[BH.d, RK.d], writes=[b1.d], inc=False)
            kb.op("pe", lambda e: e.matmul(b1[:, 192:320], lhsT=kkbd, rhs=BH[:, cl, :], start=True, stop=True), reads=[BH.d, RK.d], writes=[b1.d])
            kb.op("dve", lambda e: e.tensor_tensor(out=NA[:], in0=b1[:, 0:320], in1=MAm[:], op=ALU.mult), reads=[b1.d, MAm.d], writes=[NA.d])
            b2 = banks.next()
            kb.op("pe", lambda e: e.matmul(b2[:, 0:192], lhsT=KH[:, cl, :], rhs=rk, start=True, stop=True), reads=[KH.d, RK.d], writes=[b2.d])
            kb.op("dve", lambda e: e.tensor_tensor(out=BB[:], in0=b2[:, 0:192], in1=MBm[:], op=ALU.mult), reads=[b2.d, MBm.d], writes=[BB.d])
            yield
            bt = banks.next()
            btv = bt.t[:].bitcast(BF16)
            kb.op("pe", lambda e: e.transpose(out=btv[:, 0:128], in_=KG[:, cl, :], identity=c.identb[:]), reads=[KG.d, c.identb.d], writes=[bt.d], inc=False)
            kb.op("pe", lambda e: e.transpose(out=btv[:, 128:256], in_=BG[:, cl, :], identity=c.identb[:]), reads=[BG.d, c.identb.d], writes=[bt.d])
            kb.op("act", lambda e: e.copy(out=st.KBt[:], in_=btv[:, 0:256]), reads=[bt.d], writes=[st.KBt.d])
            Nap, NTT = NA[:, 0:128], NA[:, 192:320]
            XWT = tp_.XWT[0]
            kb.op("pool", lambda e: e.tensor_tensor(out=XWT[:, 128:256], in0=c.identb[:], in1=Nap, op=ALU.subtract),
                  reads=[c.identb.d, NA.d], writes=[XWT.d])
            bx = banks.next()
            kb.op("pe", lambda e: e.matmul(bx[:, 0:128], lhsT=NTT, rhs=Nap, start=True, stop=True), reads=[NA.d], writes=[bx.d], inc=False)
            kb.op("pe", lambda e: e.matmul(bx[:, 256:384], lhsT=Nap, rhs=NTT, start=True, stop=True), reads=[NA.d], writes=[bx.d])
            kb.op("act", lambda e: e.copy(out=blk2(XWT, 0), in_=blk2(bx, 0)), reads=[bx.d], writes=[XWT.d])
            yield
            for k in range(1, 6):
                XWc, XWn = tp_.XWT[(k - 1) % 2], tp_.XWT[k % 2]
                bz = banks.next()
                ev = "act" if k % 2 == 1 else "dve"
                if k < 5:
                    kb.op("pe", lambda e, bz=bz, XWc=XWc: e.matmul(bz[:, 0:256], lhsT=XWc[:, 256:384], rhs=XWc[:, 0:256], start=True, stop=False),
                          reads=[XWc.d], writes=[bz.d], inc=False)
                    kb.op("pe", lambda e, bz=bz, XWc=XWc: e.matmul(bz[:, 128:256], lhsT=c.identb[:], rhs=XWc[:, 128:256], start=False, stop=True),
                          reads=[XWc.d, c.identb.d], writes=[bz.d], inc=False)
                    kb.op("pe", lambda e, bz=bz, XWc=XWc: e.matmul(bz[:, 256:384], lhsT=XWc[:, 0:128], rhs=XWc[:, 256:384], start=True, stop=True),
                          reads=[XWc.d], writes=[bz.d])
                    if ev == "act":
                        kb.op("act", lambda e, bz=bz, XWn=XWn: e.copy(out=XWn[:, 0:384], in_=bz[:, 0:384]), reads=[bz.d], writes=[XWn.d])
                    else:
                        kb.op("dve", lambda e, bz=bz, XWn=XWn: e.tensor_copy(out=XWn[:, 0:384], in_=bz[:, 0:384]), reads=[bz.d], writes=[XWn.d])
                else:
                    kb.op("pe", lambda e, bz=bz, XWc=XWc: e.matmul(bz[:, 0:128], lhsT=XWc[:, 256:384], rhs=XWc[:, 128:256], start=True, stop=False),
                          reads=[XWc.d], writes=[bz.d], inc=False)
                    kb.op("pe", lambda e, bz=bz, XWc=XWc: e.matmul(bz[:, 0:128], lhsT=c.identb[:], rhs=XWc[:, 128:256], start=False, stop=True),
                          reads=[XWc.d, c.identb.d], writes=[bz.d])
                    kb.op("act", lambda e, bz=bz: e.copy(out=st.W[:], in_=bz[:, 0:128]), reads=[bz.d], writes=[st.W.d])
                yield

        def gen_B(ch):
            M0f, M0b = ch.M0f, ch.M0b
            for ci in range(len(ch.order)):
                while ch.a_done <= ci:
                    yield
                cg = ch.order[ci]
                b4, cl = divmod(cg, 8)
                RK = ch.blkRK[b4]
                kkbd, rpl = RK[:, cl, 0:128], RK[:, cl, 128:192]
                st = ch.stash[ci % DEPTH]
                RHS, U = ch.RHSs.next(), ch.Us.next()
                br = banks.next()
                kb.op("pe", lambda e, br=br, kkbd=kkbd: e.matmul(br[:, 0:128], lhsT=kkbd, rhs=M0b[:], start=True, stop=False), reads=[RK.d, M0b.d], writes=[br.d], inc=False)
                kb.op("pe", lambda e, br=br, st=st, cg=cg: e.matmul(br[:, 0:128], lhsT=st.BB[:, 0:128], rhs=ch.Vbd[:, cg, :], start=False, stop=True), reads=[st.BB.d, ch.Vbd.d], writes=[br.d])
                kb.op("act", lambda e, br=br, RHS=RHS: e.copy(out=RHS[:], in_=br[:, 0:128]), reads=[br.d], writes=[RHS.d])
                yield
                bu = banks.next()
                kb.op("pe", lambda e, bu=bu, st=st, RHS=RHS: e.matmul(bu[:, 0:128], lhsT=st.W[:], rhs=RHS[:], start=True, stop=True), reads=[st.W.d, RHS.d], writes=[bu.d])
                kb.op("act", lambda e, bu=bu, U=U: e.copy(out=U[:], in_=bu[:, 0:128]), reads=[bu.d], writes=[U.d])
                yield
                bS = banks.next()
                kb.op("pe", lambda e, bS=bS, st=st, cg=cg: e.matmul(bS[:, 0:128], lhsT=st.KBt[:, 0:128], rhs=ch.Vbd[:, cg, :], start=True, stop=False), reads=[st.KBt.d, ch.Vbd.d], writes=[bS.d], inc=False)
                kb.op("pe", lambda e, bS=bS, st=st, U=U: e.matmul(bS[:, 0:128], lhsT=st.KBt[:, 128:256], rhs=U[:], start=False, stop=True), reads=[st.KBt.d, U.d], writes=[bS.d])
                if not ch.state_only:
                    bY = banks.next()
                    kb.op("pe", lambda e, bY=bY, rpl=rpl: e.matmul(bY[:, 0:64], lhsT=M0b[:], rhs=rpl, start=True, stop=False), reads=[M0b.d, RK.d], writes=[bY.d], inc=False)
                    kb.op("pe", lambda e, bY=bY, st=st, cg=cg: e.matmul(bY[:, 0:64], lhsT=ch.Vbd[:, cg, :], rhs=st.BB[:, 128:192], start=False, stop=False), reads=[ch.Vbd.d, st.BB.d], writes=[bY.d], inc=False)
                    kb.op("pe", lambda e, bY=bY, st=st, U=U: e.matmul(bY[:, 0:64], lhsT=U[:], rhs=st.NA[:, 128:192], start=False, stop=True), reads=[U.d, st.NA.d], writes=[bY.d])
                kb.op("dve", lambda e, bS=bS, cg=cg: e.scalar_tensor_tensor(out=M0f[:], in0=M0f[:], scalar=ch.gC[:, cg:cg + 1], in1=bS[:, 0:128], op0=ALU.mult, op1=ALU.add),
                      reads=[M0f.d, ch.gC.d, bS.d], writes=[M0f.d])
                kb.op("act", lambda e: e.copy(out=M0b[:], in_=M0f[:]), reads=[M0f.d], writes=[M0b.d])
                if not ch.state_only:
                    dst = yacc[:, cg * 64:(cg + 1) * 64]
                    src = bY[:, 0:64]
                    if not ywritten[cg]:
                        kb.op("act", lambda e, dst=dst, src=src: e.copy(out=dst, in_=src), reads=[bY.d], writes=[yts[cg]])
                    else:
                        kb.op("dve", lambda e, dst=dst, src=src: e.tensor_tensor(out=dst, in0=dst, in1=src, op=ALU.add), reads=[bY.d, yts[cg]], writes=[yts[cg]])
                    ywritten[cg] = True
                ch.b_done = ci + 1
                yield

        def run_chains(chains):
            for ch in chains:
                ch.order = list(range(NCH)) if ch.cdir == 0 else list(range(NCH - 1, -1, -1))
                ch.blkRK = {}
                ch.active = []
                ch.next_a = 0
                ch.a_done = 0
                ch.b_done = 0
                ch.gb = gen_B(ch)
                ch.b_fin = False
            def adv_b():
                for ch in chains:
                    if not ch.b_fin:
                        try:
                            next(ch.gb)
                        except StopIteration:
                            ch.b_fin = True

            while not all(ch.b_fin for ch in chains):
                for ch in chains:
                    while ch.next_a < NCH and len(ch.active) < WA and ch.next_a - ch.b_done < DEPTH:
                        b4 = ch.order[ch.next_a] // 8
                        if b4 not in ch.blkRK:
                            if ch.active:
                                break
                            ch.blkRK[b4] = block_prep(ch, b4)
                        ch.active.append(gen_A(ch, ch.next_a))
                        ch.next_a += 1
                snaps = [list(ch.active) for ch in chains]
                nA = max([len(sn) for sn in snaps] + [1])
                for i in range(nA):
                    for ch, sn in zip(chains, snaps):
                        if i < len(sn):
                            try:
                                next(sn[i])
                            except StopIteration:
                                ch.active.remove(sn[i])
                                ch.a_done += 1
                    if i % 2 == 1 or i == nA - 1:
                        adv_b()

        Sf = [sb("Sf%d" % h_, [128, 128], F32, ls) for h_ in range(4)]
        Sbk = [sb("Sbk%d" % h_, [128, 128], F32, ls) for h_ in range(4)]
        Mst = [sb("Mst%d" % h_, [128, 128], F32, ls) for h_ in range(4)]
        for t_ in Sf + Sbk + Mst:
            kb.op("pool", lambda e, t_=t_: e.memset(t_[:], 0.0), writes=[t_.d])
        if do_slots:
            sft = sb("sft", [128, 42], F32, ls)
            kb.dma("sp", sft.ch, sft[:], g.sf_d, writes=[sft.d])
            mu_a = sb("mu_a", [128, 15], F32, ls)
            mu_b = sb("mu_b", [128, 15], F32, ls)
            w0e = sb("w0e", [128, 4], F32, ls)
            a0e = sb("a0e", [128, 4], F32, ls)
            wupe = sb("wupe", [128, 512], BF16, ls)
            aupe = sb("aupe", [128, 512], BF16, ls)

            def blend(dst, srcf, srcb_, rf, rb_, ff, fb):
                kb.op("dve", lambda e: e.tensor_scalar_mul(out=dst, in0=srcf, scalar1=ff), reads=rf + [sft.d], writes=[rb_.d])
                kb.op("dve", lambda e: e.scalar_tensor_tensor(out=dst, in0=srcb_, scalar=fb, in1=dst, op0=ALU.mult, op1=ALU.add),
                      reads=rf + [sft.d, rb_.d], writes=[rb_.d])

            for j in range(7):
                fc = [sft[:, j * 6 + k_:j * 6 + k_ + 1] for k_ in range(6)]
                blend(mu_a[:], c.pp[:, 22:37], c.pp[:, 37:52], [c.pp.d], mu_a, fc[0], fc[1])
                blend(mu_b[:], c.pp[:, 37:52], c.pp[:, 22:37], [c.pp.d], mu_b, fc[0], fc[1])
                blend(w0e[:], c.pp[:, 52:56], c.pp[:, 56:60], [c.pp.d], w0e, fc[0], fc[1])
                blend(a0e[:], c.pp[:, 60:64], c.pp[:, 64:68], [c.pp.d], a0e, fc[0], fc[1])
                blend(wupe[:], lw[("wup", 0)][:], lw[("wup", 1)][:], [lw[("wup", 0)].d, lw[("wup", 1)].d], wupe, fc[0], fc[1])
                blend(aupe[:], lw[("aup", 0)][:], lw[("aup", 1)][:], [lw[("aup", 0)].d, lw[("aup", 1)].d], aupe, fc[0], fc[1])

                def slot_loader(idx, chunk, j=j, fc=fc):
                    def f(out_S):
                        p0 = j * 2176
                        kb.dma("sp", B.P.ch, B.P[:, 1:SEG + 1], g.prs_s[idx, :, p0:p0 + SEG], writes=[B.P.d])
                        kb.dma("sp", B.P.ch, B.P[:, 0:1], g.prs_s[idx, :, p0 + SEG:p0 + SEG + 1], writes=[B.P.d], allow_slow_non_contiguous=True)
                        kb.dma("sp", B.P.ch, B.P[:, SEG + 1:SEG + 2], g.prs_s[idx, :, p0 + SEG + 1:p0 + SEG + 2], writes=[B.P.d],
                               allow_slow_non_contiguous=True)
                        kb.op("dve", lambda e: e.tensor_scalar_mul(out=B.P[:, 0:1], in0=B.P[:, 0:1], scalar1=fc[2]),
                              reads=[B.P.d, sft.d], writes=[B.P.d])
                        kb.op("act", lambda e: e.activation(out=out_S[:], in_=B.P[:, 1:SEG + 1], func=AF.Copy, scale=ptab[:, chunk:chunk + 1]),
                              reads=[B.P.d, ptab.d], writes=[out_S.d])
                        kb.op("dve", lambda e: e.scalar_tensor_tensor(out=out_S[:], in0=B.P[:, 0:SEG], scalar=mu_a[:, chunk:chunk + 1],
                                                                      in1=out_S[:], op0=ALU.mult, op1=ALU.add),
                              reads=[B.P.d, mu_a.d, out_S.d], writes=[out_S.d])
                        kb.op("dve", lambda e: e.scalar_tensor_tensor(out=out_S[:], in0=B.P[:, 2:SEG + 2], scalar=mu_b[:, chunk:chunk + 1],
                                                                      in1=out_S[:], op0=ALU.mult, op1=ALU.add),
                              reads=[B.P.d, mu_b.d, out_S.d], writes=[out_S.d])
                    return f
                open_prep()
                for idx, chunk, dstb, fn in ((8, 12, twd, AF.Tanh), (9, 13, adb, AF.Copy)):
                    S = B.Ss.next()
                    slot_loader(idx, chunk)(S)
                    kb.op("act", lambda e, dstb=dstb, fn=fn, S=S: e.activation(out=dstb[:], in_=S[:], func=fn), reads=[S.d], writes=[dstb.d])
                for pi_, (hpa, hpb) in enumerate(((0, 1), (2, 3))):
                    if pi_ > 0:
                        open_prep()
                    decay_and_a(0, wupe, aupe, (w0e[:, hpa:hpa + 1], w0e.d), (a0e[:, hpa:hpa + 1], a0e.d), hpa)
                    prep_v(slot_loader(4 + hpa, 8 + hpa))
                    prep_k(slot_loader(hpa, 4 + hpa), hpa, (0,))
                    decay_and_a(1, wupe, aupe, (w0e[:, hpb:hpb + 1], w0e.d), (a0e[:, hpb:hpb + 1], a0e.d), hpb)
                    prep_v(slot_loader(4 + hpb, 8 + hpb), vb=gb, Vbd=Vbd2)
                    prep_k(slot_loader(hpb, 4 + hpb), hpb, (1,), kkb=rb)
                    close_prep()
                    chs = open_scan([(0, 0, True), (1, 0, True, dict(kk=rb, Vbd=Vbd2))])
                    for ch, hp in zip(chs, (hpa, hpb)):
                        M0f, M0b = ch.M0f, ch.M0b
                        kb.op("dve", lambda e, hp=hp, fc=fc, M0f=M0f: e.tensor_scalar_mul(out=M0f[:], in0=Mst[hp][:], scalar1=fc[3]),
                              reads=[Mst[hp].d, sft.d], writes=[M0f.d])
                        kb.op("act", lambda e, M0f=M0f, M0b=M0b: e.copy(out=M0b[:], in_=M0f[:]), reads=[M0f.d], writes=[M0b.d])
                    run_chains(chs)
                    for ch, hp in zip(chs, (hpa, hpb)):
                        M0f = ch.M0f
                        kb.op("act", lambda e, hp=hp, M0f=M0f: e.copy(out=Mst[hp][:], in_=M0f[:]), reads=[M0f.d], writes=[Mst[hp].d])
                        kb.op("dve", lambda e, hp=hp, fc=fc, M0f=M0f: e.scalar_tensor_tensor(out=Sf[hp][:], in0=M0f[:], scalar=fc[4], in1=Sf[hp][:],
                                                                                  op0=ALU.mult, op1=ALU.add),
                              reads=[M0f.d, sft.d, Sf[hp].d], writes=[Sf[hp].d])
                        kb.op("dve", lambda e, hp=hp, fc=fc, M0f=M0f: e.scalar_tensor_tensor(out=Sbk[hp][:], in0=M0f[:], scalar=fc[5], in1=Sbk[hp][:],
                                                                                  op0=ALU.mult, op1=ALU.add),
                              reads=[M0f.d, sft.d, Sbk[hp].d], writes=[Sbk[hp].d])
                    close_scan()

        for seg in segs:
            open_prep()
            for chunk, dstb, fn in ((12, twd, AF.Tanh), (13, adb, AF.Copy), (14, sgd, AF.Sigmoid)):
                S = B.Ss.next()
                load_shift(seg, chunk, S)
                kb.op("act", lambda e, dstb=dstb, fn=fn, S=S: e.activation(out=dstb[:], in_=S[:], func=fn), reads=[S.d], writes=[dstb.d])
            for hi_, hp in enumerate(hps):
                if hi_ > 0:
                    open_prep()
                for d_ in range(2):
                    decay_and_a(d_, lw[("wup", d_)], lw[("aup", d_)],
                                (c.pp[:, 52 + d_ * 4 + hp:53 + d_ * 4 + hp], c.pp.d), (c.pp[:, 60 + d_ * 4 + hp:61 + d_ * 4 + hp], c.pp.d), hp)
                for b4 in range(4):
                    sl = slice(b4 * 512, (b4 + 1) * 512)
                    bank = banks.next()
                    kb.op("pe", lambda e, bank=bank, sl=sl, hp=hp: e.matmul(
                        bank[:, :], lhsT=gup[:, hp * 128:(hp + 1) * 128], rhs=sgd[:, sl], start=True, stop=True),
                        reads=[gup.d, sgd.d], writes=[bank.d])
                    kb.op("act", lambda e, bank=bank, sl=sl: e.copy(out=gb[:, sl], in_=bank[:, :]), reads=[bank.d], writes=[gb.d])
                S = B.Ss.next()
                load_shift(seg, hp, S)
                kb.op("act", lambda e, S=S: e.copy(out=rb[:], in_=S[:]), reads=[S.d], writes=[rb.d])
                prep_v(lambda S_, seg=seg, hp=hp: load_shift(seg, 8 + hp, S_))
                S2 = prep_k(lambda S_, seg=seg, hp=hp: load_shift(seg, 4 + hp, S_), hp, (0, 1))
                kb.op("pool", lambda e, S2=S2: e.tensor_tensor(out=S2[:], in0=kd[0][:], in1=kd[1][:], op=ALU.add),
                      reads=[kd[0].d, kd[1].d], writes=[S2.d])
                kb.op("dve", lambda e, S2=S2, hp=hp: e.scalar_tensor_tensor(out=bon[:], in0=rb[:], scalar=c.pp[:, 76 + hp:77 + hp], in1=S2[:],
                                                                            op0=ALU.mult, op1=ALU.mult),
                      reads=[rb.d, c.pp.d, S2.d], writes=[bon.d])

                def bon_dst(b4, bank):
                    sl = slice(b4 * 512, (b4 + 1) * 512)
                    kb.op("dve", lambda e: e.scalar_tensor_tensor(out=bon[:, sl], in0=bank[:, :], scalar=64.0, in1=vb[:, sl],
                                                                  op0=ALU.mult, op1=ALU.mult), reads=[bank.d, vb.d], writes=[bon.d])
                bdsum(bon_dst, bon, lambda b4: bon[:, b4 * 512:(b4 + 1) * 512])
                close_prep()
                chains = open_scan([(d_, d_, False) for d_ in dirs])
                for i_ in range(NCH):
                    ywritten[i_] = False
                for ch in chains:
                    M0f, M0b = ch.M0f, ch.M0b
                    if seg == 0:
                        src_state = Sf[hp] if ch.d_ == 0 else Sbk[hp]
                        kb.op("act", lambda e, src_state=src_state, M0f=M0f: e.copy(out=M0f[:], in_=src_state[:]), reads=[src_state.d], writes=[M0f.d])
                        kb.op("act", lambda e, M0f=M0f, M0b=M0b: e.copy(out=M0b[:], in_=M0f[:]), reads=[M0f.d], writes=[M0b.d])
                    else:
                        kb.op("pool", lambda e, M0f=M0f: e.memset(M0f[:], 0.0), writes=[M0f.d])
                        kb.op("pool", lambda e, M0b=M0b: e.memset(M0b[:], 0.0), writes=[M0b.d])
                run_chains(chains)
                for b4 in range(4):
                    sl = slice(b4 * 512, (b4 + 1) * 512)
                    bank = banks.next()
                    kb.op("pe", lambda e, bank=bank, sl=sl: e.matmul(bank[:, :], lhsT=c.bdonesf[:], rhs=yacc[:, sl], start=True, stop=True),
                          reads=[c.bdonesf.d] + yts[b4 * 8:(b4 + 1) * 8], writes=[bank.d])
                    t1 = pst.next()
                    kb.op("dve", lambda e, bank=bank, sl=sl, t1=t1: e.tensor_tensor(out=t1[:], in0=yacc[:, sl], in1=bank[:, :], op=ALU.subtract),
                          reads=[bank.d] + yts[b4 * 8:(b4 + 1) * 8], writes=[t1.d])
                    t2 = pst.next()
                    kb.op("act", lambda e, t1=t1, t2=t2: e.activation(out=t2[:], in_=t1[:], func=AF.Square), reads=[t1.d], writes=[t2.d])
                    bank2 = banks.next()
                    kb.op("pe", lambda e, bank2=bank2, t2=t2: e.matmul(bank2[:, :], lhsT=c.bdonesf[:], rhs=t2[:], start=True, stop=True),
                          reads=[c.bdonesf.d, t2.d], writes=[bank2.d])
                    kb.op("dve", lambda e, bank2=bank2, t2=t2: e.tensor_scalar_add(out=t2[:], in0=bank2[:, :], scalar1=64e-5),
                          reads=[bank2.d], writes=[t2.d])
                    kb.op("act", lambda e, t2=t2: e.activation(out=t2[:], in_=t2[:], func=AF.Sqrt), reads=[t2.d], writes=[t2.d])
                    kb.op("dve", lambda e, t2=t2: e.reciprocal(out=t2[:], in_=t2[:]), reads=[t2.d], writes=[t2.d])
                    kb.op("dve", lambda e, t1=t1, t2=t2: e.tensor_tensor(out=t1[:], in0=t1[:], in1=t2[:], op=ALU.mult),
                          reads=[t1.d, t2.d], writes=[t1.d])
                    kb.op("act", lambda e, t1=t1, hp=hp: e.activation(out=t1[:], in_=t1[:], func=AF.Identity,
                                                                       scale=c.pp[:, 80 + hp:81 + hp], bias=c.pp[:, 84 + hp:85 + hp]),
                          reads=[t1.d, c.pp.d], writes=[t1.d])
                    kb.op("pool", lambda e, t1=t1, sl=sl: e.tensor_tensor(out=t1[:], in0=t1[:], in1=bon[:, sl], op=ALU.add),
                          reads=[t1.d, bon.d], writes=[t1.d])
                    o = osbs.next()
                    kb.op("pool", lambda e, t1=t1, sl=sl, o=o: e.tensor_tensor(out=o[:], in0=t1[:], in1=gb[:, sl], op=ALU.mult),
                          reads=[t1.d, gb.d], writes=[o.d])
                    kb.dma("sp", o.sch, g.mix_s[4 + hp, :, seg * SEG + b4 * 512: seg * SEG + (b4 + 1) * 512], o[:], reads=[o.d])
                close_scan()
        kb.barrier()


def scan_step(g, d_, cl, cg, RK, KH, BH, KG, BG, Vbd, M0f, M0b, gCt, yacc, masks, banks, rings, first_dir, state_only=False):
    kb, c = g.kb, g.c
    MA, MB, MT = masks
    NAs, BBs, NTTs, XWs, XTs, RHSs, Us, KGts, BGts = rings
    rk = RK[:, cl, :, :].rearrange("p a n -> p (a n)")
    kkbd = RK[:, cl, 0, :]
    rbd = RK[:, cl, 1, :]
    NA, BB, NTT = NAs.next(), BBs.next(), NTTs.next()
    nsc = 128 if state_only else 256
    b1 = banks.next()
    kb.op("pe", lambda e: e.matmul(b1[:, 0:nsc], lhsT=BH[:, cl, :], rhs=rk[:, 0:nsc], start=True, stop=True), reads=[BH.d, RK.d], writes=[b1.d])
    kb.op("dve", lambda e: e.tensor_tensor(out=NA[:, 0:nsc], in0=b1[:, 0:nsc], in1=MA[:, 0:nsc], op=ALU.mult), reads=[b1.d, MA.d], writes=[NA.d])
    b2 = banks.next()
    kb.op("pe", lambda e: e.matmul(b2[:, 0:nsc], lhsT=KH[:, cl, :], rhs=rk[:, 0:nsc], start=True, stop=True), reads=[KH.d, RK.d], writes=[b2.d])
    kb.op("dve", lambda e: e.tensor_tensor(out=BB[:, 0:nsc], in0=b2[:, 0:nsc], in1=MB[:, 0:nsc], op=ALU.mult), reads=[b2.d, MB.d], writes=[BB.d])
    b3 = banks.next()
    kb.op("pe", lambda e: e.matmul(b3[:, 0:128], lhsT=kkbd, rhs=BH[:, cl, :], start=True, stop=True), reads=[BH.d, RK.d], writes=[b3.d])
    kb.op("dve", lambda e: e.tensor_tensor(out=NTT[:], in0=b3[:, 0:128], in1=MT[:], op=ALU.mult), reads=[b3.d, MT.d], writes=[NTT.d])
    KGt, BGt = KGts.next(), BGts.next()
    bt = banks.next()
    btv = bt.t[:].bitcast(BF16)
    kb.op("pe", lambda e: e.transpose(out=btv[:, 0:128], in_=KG[:, cl, :], identity=c.identb[:]), reads=[KG.d, c.identb.d], writes=[bt.d], inc=False)
    kb.op("pe", lambda e: e.transpose(out=btv[:, 128:256], in_=BG[:, cl, :], identity=c.identb[:]), reads=[BG.d, c.identb.d], writes=[bt.d])
    kb.op("act", lambda e: e.copy(out=KGt[:], in_=btv[:, 0:128]), reads=[bt.d], writes=[KGt.d])
    kb.op("act", lambda e: e.mul(out=BGt[:], in_=btv[:, 128:256], mul=-1.0), reads=[bt.d], writes=[BGt.d])
    Nap = NA[:, 0:128]
    XW = XWs.next()
    kb.op("pool", lambda e, XW=XW: e.tensor_tensor(out=XW[:, 128:256], in0=c.identb[:], in1=Nap, op=ALU.subtract),
          reads=[c.identb.d, NA.d], writes=[XW.d])
    bx = banks.next()
    kb.op("pe", lambda e: e.matmul(bx[:, 0:128], lhsT=NTT[:], rhs=Nap, start=True, stop=True), reads=[NTT.d, NA.d], writes=[bx.d])
    kb.op("act", lambda e, XW=XW: e.copy(out=XW[:, 0:128], in_=bx[:, 0:128]), reads=[bx.d], writes=[XW.d])
    XT = XTs.next()
    by = banks.next()
    kb.op("pe", lambda e: e.matmul(by[:, 0:128], lhsT=Nap, rhs=NTT[:], start=True, stop=True), reads=[NTT.d, NA.d], writes=[by.d])
    kb.op("act", lambda e, XT=XT: e.copy(out=XT[:], in_=by[:, 0:128]), reads=[by.d], writes=[XT.d])
    for k in range(1, 6):
        last = (k == 5)
        XWn = XWs.next()
        bz = banks.next()
        if not last:
            kb.op("pe", lambda e, bz=bz, XT=XT, XW=XW: e.matmul(bz[:, 0:256], lhsT=XT[:], rhs=XW[:, :], start=True, stop=True),
                  reads=[XT.d, XW.d], writes=[bz.d])
            kb.op("act", lambda e, bz=bz, XWn=XWn: e.copy(out=XWn[:, 0:128], in_=bz[:, 0:128]), reads=[bz.d], writes=[XWn.d])
            kb.op("dve", lambda e, bz=bz, XWn=XWn, XW=XW: e.tensor_tensor(out=XWn[:, 128:256], in0=XW[:, 128:256], in1=bz[:, 128:256], op=ALU.add),
                  reads=[bz.d, XW.d], writes=[XWn.d])
            XTn = XTs.next()
            bw = banks.next()
            kb.op("pe", lambda e, bw=bw, XT=XT, XW=XW: e.matmul(bw[:, 0:128], lhsT=XW[:, 0:128], rhs=XT[:], start=True, stop=True),
                  reads=[XT.d, XW.d], writes=[bw.d])
            kb.op("act", lambda e, bw=bw, XTn=XTn: e.copy(out=XTn[:], in_=bw[:, 0:128]), reads=[bw.d], writes=[XTn.d])
            XT = XTn
        else:
            kb.op("pe", lambda e, bz=bz, XT=XT, XW=XW: e.matmul(bz[:, 0:128], lhsT=XT[:], rhs=XW[:, 128:256], start=True, stop=True),
                  reads=[XT.d, XW.d], writes=[bz.d])
            kb.op("dve", lambda e, bz=bz, XWn=XWn, XW=XW: e.tensor_tensor(out=XWn[:, 128:256], in0=XW[:, 128:256], in1=bz[:, 0:128], op=ALU.add),
                  reads=[bz.d, XW.d], writes=[XWn.d])
        XW = XWn
    W = XW[:, 128:256]
    RHS, U = RHSs.next(), Us.next()
    br = banks.next()
    kb.op("pe", lambda e: e.matmul(br[:, 0:128], lhsT=kkbd, rhs=M0b[:], start=True, stop=False), reads=[RK.d, M0b.d], writes=[br.d], inc=False)
    kb.op("pe", lambda e: e.matmul(br[:, 0:128], lhsT=BB[:, 0:128], rhs=Vbd[:, cg, :], start=False, stop=True), reads=[BB.d, Vbd.d], writes=[br.d])
    kb.op("act", lambda e: e.copy(out=RHS[:], in_=br[:, 0:128]), reads=[br.d], writes=[RHS.d])
    bu = banks.next()
    kb.op("pe", lambda e: e.matmul(bu[:, 0:128], lhsT=W, rhs=RHS[:], start=True, stop=True), reads=[XW.d, RHS.d], writes=[bu.d])
    kb.op("act", lambda e: e.copy(out=U[:], in_=bu[:, 0:128]), reads=[bu.d], writes=[U.d])
    bY = banks.next() if not state_only else None
    if not state_only:
        kb.op("pe", lambda e: e.matmul(bY[:, 0:128], lhsT=M0b[:], rhs=rbd, start=True, stop=False), reads=[M0b.d, RK.d], writes=[bY.d], inc=False)
        kb.op("pe", lambda e: e.matmul(bY[:, 0:128], lhsT=Vbd[:, cg, :], rhs=BB[:, 128:256], start=False, stop=False), reads=[Vbd.d, BB.d], writes=[bY.d], inc=False)
        kb.op("pe", lambda e: e.matmul(bY[:, 0:128], lhsT=U[:], rhs=NA[:, 128:256], start=False, stop=True), reads=[U.d, NA.d], writes=[bY.d])
    for X in range(2 if not state_only else 0):
        dst = yacc[64 * X:64 * X + 64, cg * 64:(cg + 1) * 64]
        src = bY[64 * X:64 * X + 64, 64 * X:64 * X + 64]
        if first_dir:
            kb.op("act", lambda e, dst=dst, src=src: e.copy(out=dst, in_=src), reads=[bY.d], writes=[yacc.d])
        else:
            kb.op("dve", lambda e, dst=dst, src=src: e.tensor_tensor(out=dst, in0=dst, in1=src, op=ALU.add), reads=[bY.d, yacc.d], writes=[yacc.d])
    bS = banks.next()
    kb.op("pe", lambda e: e.matmul(bS[:, 0:128], lhsT=KGt[:], rhs=Vbd[:, cg, :], start=True, stop=False), reads=[KGt.d, Vbd.d], writes=[bS.d], inc=False)
    kb.op("pe", lambda e: e.matmul(bS[:, 0:128], lhsT=BGt[:], rhs=U[:], start=False, stop=True), reads=[BGt.d, U.d], writes=[bS.d])
    kb.op("dve", lambda e: e.scalar_tensor_tensor(out=M0f[:], in0=M0f[:], scalar=gCt[:, cg:cg + 1], in1=bS[:, 0:128], op0=ALU.mult, op1=ALU.add),
          reads=[M0f.d, gCt.d, bS.d], writes=[M0f.d])
    kb.op("act", lambda e: e.copy(out=M0b[:], in_=M0f[:]), reads=[M0f.d], writes=[M0b.d])


def bc_gate(g, seg, which, bc, bank_ring):
    kb, c = g.kb, g.c
    for half in range(2):
        bank = bank_ring.next()
        kb.op("pe", lambda e, bank=bank, half=half: e.matmul(
            bank[:, :], lhsT=c.sel[:, seg, :], rhs=c.gates[:, which * D + half * 512: which * D + (half + 1) * 512],
            start=True, stop=True), reads=[c.sel.d, c.gates.d], writes=[bank.d])
        kb.op("act", lambda e, bank=bank, half=half: e.copy(out=bc[:, half * 512:(half + 1) * 512], in_=bank[:, :]),
              reads=[bank.d], writes=[bc.d])


def phase3(g):
    nc, kb, sb, c = g.nc, g.kb, g.sb, g.c
    with ExitStack() as ls:
        wout = sb("wout", [128, 8, D], BF16, ls)
        wv = g.w_out.rearrange("(kc p) n -> p kc n", p=128)
        for kc in range(8):
            kb.dma("pool", wout.ch, wout[:, kc, :], wv[:, kc, :], writes=[wout.d])
        mixs = Ring([sb("mixb%d" % i, [128, 8, 512], BF16, ls) for i in range(2)])
        xring = Ring([sb("x3_%d" % i, [128, D], F32, ls) for i in range(3)])
        outs = Ring([sb("o3_%d" % i, [128, D], F32, ls) for i in range(2)])
        bcs = Ring([sb("bc3_%d" % i, [128, D], F32, ls) for i in range(2)])
        banks = Ring([g.banks[i] for i in range(6)])
        gbanks = Ring([g.banks[6], g.banks[7]])
        for seg in range(NSEG):
            bc = bcs.next()
            bc_gate(g, seg, 0, bc, gbanks)
            for b in range(4):
                t0 = seg * SEG + b * 512
                mb = mixs.next()
                kb.dma("sp", mb.ch, mb[:], g.mix_s[:, :, t0:t0 + 512].rearrange("c p t -> p c t"), writes=[mb.d])
                for t in range(4):
                    xt = xring.next()
                    kb.dma("sp", xt.ch, xt[:], g.xa[t0 + t * 128:t0 + (t + 1) * 128, :], writes=[xt.d])
                    o = outs.next()
                    tmp = o
                    for half in range(2):
                        bank = banks.next()
                        for kc in range(8):
                            kb.op("pe", lambda e, bank=bank, kc=kc, mb=mb, t=t, half=half: e.matmul(
                                bank[:, :], lhsT=mb[:, kc, t * 128:(t + 1) * 128], rhs=wout[:, kc, half * 512:(half + 1) * 512],
                                start=(kc == 0), stop=(kc == 7)), reads=[mb.d, wout.d], writes=[bank.d], inc=(kc == 7))
                        kb.op("dve", lambda e, bank=bank, tmp=tmp, bc=bc, half=half: e.tensor_tensor(
                            out=tmp[:, half * 512:(half + 1) * 512], in0=bank[:, :], in1=bc[:, half * 512:(half + 1) * 512],
                            op=ALU.mult), reads=[bank.d, bc.d], writes=[tmp.d])
                    kb.op("pool", lambda e, o=o, xt=xt: e.tensor_tensor(out=o[:], in0=o[:], in1=xt[:], op=ALU.add),
                          reads=[o.d, xt.d], writes=[o.d])
                    kb.dma("sp", o.sch, g.y[t0 + t * 128:t0 + (t + 1) * 128, :], o[:], reads=[o.d])
        kb.barrier()


def phase4(g):
    nc, kb, sb, c = g.nc, g.kb, g.sb, g.c
    NB = 256
    with ExitStack() as ls:
        w1 = sb("wff1", [128, 8, 4 * D], BF16, ls)
        w2 = sb("wff2", [128, 32, D], BF16, ls)
        w1v = g.w_ff1.rearrange("(kc p) n -> p kc n", p=128)
        w2v = g.w_ff2.rearrange("(kc p) n -> p kc n", p=128)
        for kc in range(8):
            kb.dma("pool", w1.ch, w1[:, kc, :], w1v[:, kc, :], writes=[w1.d])
        for kc in range(0, 32, 4):
            kb.dma("pool", w2.ch, w2[:, kc:kc + 4, :], w2v[:, kc:kc + 4, :], writes=[w2.d])
        xring = Ring([sb("x4_%d" % i, [128, D], F32, ls) for i in range(4)])
        xsbs = [[sb("xs4_%d_%d" % (r, t), [128, D], BF16, ls) for t in range(2)] for r in range(2)]
        stat = Ring([sb("st4_%d" % i, [128, 4], F32, ls) for i in range(4)])
        hTs = Ring([sb("h2T%d" % i, [128, 8, NB], BF16, ls) for i in range(2)])
        hid = sb("hid", [128, 32, NB], BF16, ls)
        rl = Ring([sb("rl%d" % i, [128, NB], F32, ls) for i in range(3)])
        outs = Ring([sb("o4_%d" % i, [128, D], F32, ls) for i in range(2)])
        bcs = Ring([sb("bc4_%d" % i, [128, D], F32, ls) for i in range(1)])
        tbanks = Ring([g.banks[0], g.banks[1]])
        fbanks = Ring([g.banks[2], g.banks[3], g.banks[4]])
        obanks = Ring([g.banks[5], g.banks[6], g.banks[7]])
        bi = 0
        blist = [(seg, b) for seg in range(NSEG) for b in range(SEG // NB)]
        pre = {}

        def rms4(i):
            seg_, b_ = blist[i]
            hT_ = hTs.next()
            xts_ = rms_to_hT(g, (xring, xsbs[i % 2], stat, tbanks, None), g.y, seg_ * SEG + b_ * NB, c.scale2, 24, seg_, hT_, nt=2)
            pre[i] = (hT_, xts_)
        rms4(0)
        for seg in range(NSEG):
            bc = bcs.next()
            bc_gate(g, seg, 1, bc, obanks)
            for b in range(SEG // NB):
                t0 = seg * SEG + b * NB
                if bi + 1 < len(blist):
                    rms4(bi + 1)
                hT, xts = pre.pop(bi)
                bi += 1
                for fc in range(32):
                    bank = fbanks.next()
                    for kc in range(8):
                        kb.op("pe", lambda e, bank=bank, kc=kc, fc=fc, hT=hT: e.matmul(
                            bank[:, 0:NB], lhsT=w1[:, kc, fc * 128:(fc + 1) * 128], rhs=hT[:, kc, :],
                            start=(kc == 0), stop=(kc == 7)), reads=[w1.d, hT.d], writes=[bank.d], inc=(kc == 7))
                    r_ = rl.next()
                    kb.op("act", lambda e, bank=bank, r_=r_: e.activation(out=r_[:], in_=bank[:, 0:NB], func=AF.Relu),
                          reads=[bank.d], writes=[r_.d])
                    kb.op("pool", lambda e, r_=r_, fc=fc: e.tensor_tensor(out=hid[:, fc, :], in0=r_[:], in1=r_[:], op=ALU.mult),
                          reads=[r_.d], writes=[hid.d])
                for t in range(2):
                    o = outs.next()
                    tmp = o
                    for half in range(2):
                        bank = obanks.next()
                        for fc in range(32):
                            kb.op("pe", lambda e, bank=bank, fc=fc, t=t, half=half: e.matmul(
                                bank[:, :], lhsT=hid[:, fc, t * 128:(t + 1) * 128], rhs=w2[:, fc, half * 512:(half + 1) * 512],
                                start=(fc == 0), stop=(fc == 31)), reads=[hid.d, w2.d], writes=[bank.d], inc=(fc == 31))
                        kb.op("dve", lambda e, bank=bank, tmp=tmp, bc=bc, half=half: e.tensor_tensor(
                            out=tmp[:, half * 512:(half + 1) * 512], in0=bank[:, :], in1=bc[:, half * 512:(half + 1) * 512],
                            op=ALU.mult), reads=[bank.d, bc.d], writes=[tmp.d])
                    xt = xts[t]
                    kb.op("pool", lambda e, o=o, xt=xt: e.tensor_tensor(out=o[:], in0=o[:], in1=xt[:], op=ALU.add),
                          reads=[o.d, xt.d], writes=[o.d])
                    kb.dma("sp", o.sch, g.y[t0 + t * 128:t0 + (t + 1) * 128, :], o[:], reads=[o.d])
        kb.barrier()


def host_inputs(inputs):
    f = np.float32
    xp = np.ascontiguousarray(inputs["x_prompt"], dtype=f)[0]
    xs = np.ascontiguousarray(inputs["x_sample"], dtype=f)
    cp = np.asarray(inputs["c_prompt"], dtype=f)
    cs = np.asarray(inputs["c_sample"], dtype=f)

    def col(v):
        v = np.asarray(v, dtype=f).reshape(-1, 128)
        return np.ascontiguousarray(v.T)
    pp = np.zeros((128, NP), f)
    pp[:, 0:8] = col(inputs["g_norm1"][0])
    pp[:, 8:16] = col(inputs["g_norm2"][0])
    pp[:, 16] = np.tile(np.asarray(inputs["q_norm_g"][0], f), 2)
    pp[:, 17] = np.tile(np.asarray(inputs["k_norm_g"][0], f), 2)
    pp[:, 18:22] = col(inputs["attn_beta"][0])
    pp[:, 22:37] = col(inputs["mu_prev"][0])
    pp[:, 37:52] = col(inputs["mu_next"][0])
    pp[:, 52:60] = col(np.asarray(inputs["w0"][0]).reshape(-1))
    pp[:, 60:68] = col(np.asarray(inputs["a0"][0]).reshape(-1))
    pp[:, 68:72] = col(inputs["k_k"][0])
    pp[:, 72:76] = col(inputs["k_a"][0])
    pp[:, 76:80] = col(np.asarray(inputs["r_k"][0]).reshape(-1))
    pp[:, 80:84] = col(inputs["ln_x_w"][0])
    pp[:, 84:88] = col(inputs["ln_x_b"][0])
    shared = dict(
        w_ada=np.ascontiguousarray(inputs["w_ada"][0], dtype=f), b_ada=np.ascontiguousarray(inputs["b_ada"], dtype=f).reshape(1, -1),
        pp=pp, w_in=np.ascontiguousarray(inputs["w_in"][0], dtype=f), w_out=np.ascontiguousarray(inputs["w_out"][0], dtype=f),
        w_ff1=np.ascontiguousarray(inputs["w_ff1"][0], dtype=f), w_ff2=np.ascontiguousarray(inputs["w_ff2"][0], dtype=f),
        w_up=np.ascontiguousarray(inputs["w_up"][0], dtype=f).reshape(128, 512),
        a_up=np.ascontiguousarray(inputs["a_up"][0], dtype=f).reshape(128, 512),
        g_up=np.ascontiguousarray(inputs["g_up"][0], dtype=f))
    maps = []
    SP = xp.shape[0]
    for i in range(NCORES):
        xa = np.concatenate([xp[i * SEG:(i + 1) * SEG]] + [xs[4 * i + j] for j in range(4)], axis=0)
        xh = np.zeros((2048, D), f)
        lo, hi = i * SEG - 1024, (i + 1) * SEG
        valid = np.zeros(4096, f)
        valid[1024:3072] = 1
        if lo >= 0:
            xh[0:1024] = xp[lo:lo + 1024]
            valid[0:1024] = 1
        if hi + 1024 <= SP:
            xh[1024:2048] = xp[hi:hi + 1024]
            valid[3072:4096] = 1
        c5 = np.concatenate([cp[0:1], cs[4 * i:4 * i + 4]], axis=0)
        c5T = np.ascontiguousarray(c5.reshape(NSEG, 8, 128).transpose(2, 1, 0))
        vt = np.ones((128, NSEG, 48, 2), f)
        for sg in range(NSEG):
            NKs, HKs = (4096, 1024) if sg == 0 else (2048, 0)
            vseg = valid if sg == 0 else np.ones(2048, f)
            ti = 0
            for dil in (1, 4, 16):
                Lq = SEG // dil
                for r in range(dil):
                    for j in range(Lq // 128):
                        base_sub = j * 128 + HKs // dil
                        p = np.arange(128)

                        def vv(sub):
                            tok = sub * dil + r
                            ok = (sub >= 0) & (tok < NKs)
                            return np.where(ok, vseg[np.clip(tok, 0, NKs - 1)], 0.0)
                        vt[:, sg, ti, 0] = vv(base_sub - 64 + p)
                        vt[:, sg, ti, 1] = vv(base_sub + 64 + p)
                        ti += 1
        nbv = np.zeros((128, 2), f)
        nbv[:, 0] = valid[1023]
        nbv[:, 1] = valid[3072]
        xslot = np.zeros((7, 2176, D), f)
        sfl = np.zeros((7, 6), f)
        for j in range(7):
            if j < i:
                sg_ = j
                xslot[j, 0:SEG] = xp[sg_ * SEG:(sg_ + 1) * SEG]
                if sg_ > 0:
                    xslot[j, SEG] = xp[sg_ * SEG - 1]
                    sfl[j, 2] = 1
                xslot[j, SEG + 1] = xp[(sg_ + 1) * SEG]
                sfl[j, 0] = 1
                sfl[j, 3] = 0 if j == 0 else 1
                sfl[j, 4] = 1 if j == i - 1 else 0
            else:
                sg_ = 7 - (j - i)
                xslot[j, 0:SEG] = xp[sg_ * SEG:(sg_ + 1) * SEG][::-1]
                if sg_ < 7:
                    xslot[j, SEG] = xp[(sg_ + 1) * SEG]
                    sfl[j, 2] = 1
                xslot[j, SEG + 1] = xp[sg_ * SEG - 1]
                sfl[j, 1] = 1
                sfl[j, 3] = 0 if j == i else 1
                sfl[j, 5] = 1 if j == 6 else 0
        sf = np.ascontiguousarray(np.broadcast_to(sfl.reshape(1, 42), (128, 42))).astype(f)
        m = dict(shared)
        m.update(xa=np.ascontiguousarray(xa), xh=xh, c5T=c5T, vt=np.ascontiguousarray(vt.reshape(128, -1)), nbv=nbv,
                 xslot=xslot, sf=sf)
        maps.append(m)
    return maps


_CACHE = {}


def kernel(**inputs):
    maps = host_inputs(inputs)
    if "nc" not in _CACHE:
        _CACHE["nc"] = build_program(phases=("p0", "p1", "p2a", "p2b", "p3", "p4"))
    res = run_bass_kernel_spmd(_CACHE["nc"], maps, core_ids=list(range(NCORES)))
    ys = [np.asarray(r["y"]) for r in res.results]
    yp = np.concatenate([y[0:SEG] for y in ys], axis=0)[None]
    ysm = np.stack([ys[i][SEG * (1 + j):SEG * (2 + j)] for i in range(NCORES) for j in range(4)], axis=0)
    return yp.astype(np.float32), ysm.astype(np.float32)
```
